# Optimizing a Trainium2 kernel written in Bass

```python
import jax, jax.numpy as jnp
from jax import lax
import numpy as np

D_MODEL = 1024
BATCH = 4
SEQ = 4096
DEPTH = 4

N_HEADS = D_MODEL // 128
QK_NOPE_DIM = 64
QK_ROPE_DIM = 32
QK_HEAD_DIM = QK_NOPE_DIM + QK_ROPE_DIM
V_HEAD_DIM = 64
ATTN_WIDTH = N_HEADS * V_HEAD_DIM
Q_LORA_RANK = D_MODEL // 4
KV_LORA_RANK = D_MODEL // 4
ROPE_THETA = 10000.0
Q_BLOCK = 128
FOURIER_WIDTH = D_MODEL // 2
FOURIER_GROUP_DIM = 128
N_FOURIER_GROUPS = FOURIER_WIDTH // FOURIER_GROUP_DIM
N_BRANCHES = 2
NORM_EPS = 1e-6
IN_WIDTH = (Q_LORA_RANK + KV_LORA_RANK + QK_ROPE_DIM + ATTN_WIDTH
            + 2 * FOURIER_WIDTH + N_BRANCHES * D_MODEL)

kernel_name = "hybrid_mla_fourier_gated_encoder"


def _rms_norm(x, g):
    xf = x.astype(jnp.float32)
    y = xf * lax.rsqrt(jnp.mean(xf * xf, axis=-1, keepdims=True) + NORM_EPS)
    return (y * g.astype(jnp.float32)).astype(x.dtype)


def _rope_tables(seq_len, dtype):
    half = QK_ROPE_DIM // 2
    inv_freq = ROPE_THETA ** (-jnp.arange(half, dtype=jnp.float32) / half)
    pos = jnp.arange(seq_len, dtype=jnp.float32)
    ang = pos[:, None] * inv_freq[None, :]
    return jnp.cos(ang).astype(dtype), jnp.sin(ang).astype(dtype)


def _apply_rope(x, cos, sin):
    half = QK_ROPE_DIM // 2
    x1, x2 = x[..., :half], x[..., half:]
    c, s = cos[:, None, :], sin[:, None, :]
    return jnp.concatenate([x1 * c - x2 * s, x2 * c + x1 * s], axis=-1)


def _split_in(p):
    sizes = [Q_LORA_RANK, KV_LORA_RANK, QK_ROPE_DIM, ATTN_WIDTH,
             FOURIER_WIDTH, FOURIER_WIDTH, D_MODEL]
    idx = []
    acc = 0
    for s in sizes:
        acc += s
        idx.append(acc)
    return jnp.split(p, idx, axis=-1)


def _bidirectional_attention(q, k, v):
    b, h, s, dk = q.shape
    nb = s // Q_BLOCK
    scale = QK_HEAD_DIM ** -0.5
    qb = q.reshape(b, h, nb, Q_BLOCK, dk).transpose(2, 0, 1, 3, 4)

    def one_block(q_blk):
        sc = jnp.einsum('bhqd,bhkd->bhqk', q_blk, k).astype(jnp.float32) * scale
        p = jax.nn.softmax(sc, axis=-1)
        return jnp.einsum('bhqk,bhkd->bhqd', p.astype(v.dtype), v)

    o = lax.map(one_block, qb)
    return o.transpose(1, 2, 0, 3, 4).reshape(b, h, s, V_HEAD_DIM)


def _mla_branch(c_q, c_kv, k_pe, q_latent_g, kv_latent_g, w_uq, w_ukv,
                q_head_g, k_head_g, cos, sin):
    b, s, _ = c_q.shape
    q = (_rms_norm(c_q, q_latent_g) @ w_uq).reshape(b, s, N_HEADS, QK_HEAD_DIM)
    kv = (_rms_norm(c_kv, kv_latent_g) @ w_ukv).reshape(b, s, N_HEADS, QK_NOPE_DIM + V_HEAD_DIM)
    k_nope, v = kv[..., :QK_NOPE_DIM], kv[..., QK_NOPE_DIM:]
    k_pe_h = jnp.broadcast_to(k_pe[:, :, None, :], (b, s, N_HEADS, QK_ROPE_DIM))
    k = jnp.concatenate([k_nope, k_pe_h], axis=-1)
    q = _rms_norm(q, q_head_g)
    k = _rms_norm(k, k_head_g)
    q = jnp.concatenate([q[..., :QK_NOPE_DIM], _apply_rope(q[..., QK_NOPE_DIM:], cos, sin)], axis=-1)
    k = jnp.concatenate([k[..., :QK_NOPE_DIM], _apply_rope(k[..., QK_NOPE_DIM:], cos, sin)], axis=-1)
    o = _bidirectional_attention(q.transpose(0, 2, 1, 3), k.transpose(0, 2, 1, 3),
                                 v.transpose(0, 2, 1, 3))
    return o.transpose(0, 2, 1, 3).reshape(b, s, ATTN_WIDTH)


def _fourier_branch(u):
    b, s, _ = u.shape
    ug = u.astype(jnp.float32).reshape(b, s, N_FOURIER_GROUPS, FOURIER_GROUP_DIM)
    f = jnp.real(jnp.fft.fft2(ug, axes=(1, 3), norm='ortho'))
    return f.reshape(b, s, FOURIER_WIDTH).astype(u.dtype)


def setup_inputs(seed: int = 0) -> dict:
    key = jax.random.key(seed)
    ks = jax.random.split(key, 14)
    L, D = DEPTH, D_MODEL

    def nrm(k, shape, fan_in):
        return jax.random.normal(k, shape, jnp.float32) * fan_in ** -0.5

    def gain(k, shape):
        return 1.0 + 0.05 * jax.random.normal(k, shape, jnp.float32)

    return {
        "x": jax.random.normal(ks[0], (BATCH, SEQ, D), jnp.float32),
        "norm_g": gain(ks[1], (L, D)),
        "w_in": nrm(ks[2], (L, D, IN_WIDTH), D),
        "q_latent_g": gain(ks[3], (L, Q_LORA_RANK)),
        "kv_latent_g": gain(ks[4], (L, KV_LORA_RANK)),
        "w_uq": nrm(ks[5], (L, Q_LORA_RANK, N_HEADS * QK_HEAD_DIM), Q_LORA_RANK),
        "w_ukv": nrm(ks[6], (L, KV_LORA_RANK, N_HEADS * (QK_NOPE_DIM + V_HEAD_DIM)), KV_LORA_RANK),
        "q_head_g": gain(ks[7], (L, QK_HEAD_DIM)),
        "k_head_g": gain(ks[8], (L, QK_HEAD_DIM)),
        "w_attn_proj": nrm(ks[9], (L, ATTN_WIDTH, D), ATTN_WIDTH),
        "w_fourier_proj": nrm(ks[10], (L, FOURIER_WIDTH, D), FOURIER_WIDTH),
        "b_merge": 0.1 * jax.random.normal(ks[11], (L, N_BRANCHES, D), jnp.float32),
        "w_out": nrm(ks[12], (L, D, D), D) * (2 * DEPTH) ** -0.5,
    }


def reference(x, norm_g, w_in, q_latent_g, kv_latent_g, w_uq, w_ukv, q_head_g,
              k_head_g, w_attn_proj, w_fourier_proj, b_merge, w_out):
    _, s, _ = x.shape
    cos, sin = _rope_tables(s, x.dtype)
    for l in range(DEPTH):
        h = _rms_norm(x, norm_g[l])
        p = h @ w_in[l]
        c_q, c_kv, k_pe, z_a, u_f, z_f, g_a, g_f = _split_in(p)
        o_a = _mla_branch(c_q, c_kv, k_pe, q_latent_g[l], kv_latent_g[l], w_uq[l], w_ukv[l],
                          q_head_g[l], k_head_g[l], cos, sin)
        y_a = (o_a * jax.nn.silu(z_a)) @ w_attn_proj[l]
        y_f = (_fourier_branch(u_f) * jax.nn.silu(z_f)) @ w_fourier_proj[l]
        m = (jax.nn.sigmoid(g_a + b_merge[l, 0]) * y_a
             + jax.nn.sigmoid(g_f + b_merge[l, 1]) * y_f)
        x = x + m @ w_out[l]
    return x
```

```python
import numpy as np
import ml_dtypes
import concourse.bass as bass
import concourse.mybir as mybir
from concourse.bass_utils import run_bass_kernel_spmd

F32 = mybir.dt.float32
BF16 = mybir.dt.bfloat16
AF = mybir.ActivationFunctionType
ALU = mybir.AluOpType
AX = mybir.AxisListType

D = 1024
S = 4096
NTOK = 2048
NT = NTOK // 128
H = 8
DK = 96
DV = 64
INW = 4128
EPS = 1e-6
C_CQ, C_KPE, C_ZA, C_UF, C_ZF, C_GA, C_GF = 0, 512, 544, 1056, 1568, 2080, 3104
ENGS = ("pe", "act", "dve", "pool", "sp")
NDMA = 8


class Sched:
    def __init__(self, nc):
        self.nc = nc
        self.streams = {e: [] for e in ENGS}
        self.sems = {}
        self.cur = {}
        self.cnt = {}
        self.waited = {e: {} for e in ENGS}
        self.lastw = {}
        self.readers = {}
        self.dma_gen = {q: [0] * NDMA for q in ("sp", "pool")}
        self.dma_rr = {q: 0 for q in ("sp", "pool")}
        self.epoch = -1
        self.ncc = 0
        for q in ("sp", "pool"):
            for i in range(NDMA):
                self._sem(f"dma_{q}_{i}")
        self.new_epoch()

    def _sem(self, name):
        self.sems[name] = self.nc.alloc_semaphore(name)
        return name

    def new_epoch(self):
        self.epoch += 1
        for e in ("pe", "act", "dve", "pool"):
            self.cur[e] = self._sem(f"c_{e}_{self.epoch}")
            self.cnt[e] = 0

    def _deps(self, reads, writes):
        deps = {}

        def add(tok):
            if tok is not None:
                n, v = tok
                if v > deps.get(n, 0):
                    deps[n] = v
        for k in reads:
            add(self.lastw.get(k))
        for k in writes:
            add(self.lastw.get(k))
            for n, v in self.readers.get(k, {}).items():
                add((n, v))
        return deps

    def _emit(self, eng, deps, fn, semname, inc):
        waits = []
        w = self.waited[eng]
        for n, v in deps.items():
            if v > w.get(n, 0):
                waits.append((n, v))
                w[n] = v
        self.streams[eng].append((waits, fn, semname, inc))

    def _mark(self, tok, reads, writes):
        n, v = tok
        for k in reads:
            self.readers.setdefault(k, {})[n] = v
        for k in writes:
            self.lastw[k] = tok
            self.readers[k] = {}

    def op(self, eng, fn, reads=(), writes=()):
        deps = self._deps(reads, writes)
        self.cnt[eng] += 1
        tok = (self.cur[eng], self.cnt[eng])
        self._emit(eng, deps, fn, tok[0], 1)
        self._mark(tok, reads, writes)

    def dma(self, q, fn, reads=(), writes=()):
        deps = self._deps(reads, writes)
        i = self.dma_rr[q]
        self.dma_rr[q] = (i + 1) % NDMA
        name = f"dma_{q}_{i}"
        gen = self.dma_gen[q][i]
        if gen > 0:
            deps[name] = max(deps.get(name, 0), 16 * gen)
        self.dma_gen[q][i] = gen + 1
        tok = (name, 16 * (gen + 1))
        self._emit(q, deps, fn, name, 16)
        self._mark(tok, reads, writes)

    def collective(self, slot, fn, reads=(), writes=()):
        deps = self._deps(reads, writes)
        name = f"cc_{slot}"
        if name not in self.sems:
            self._sem(name)
            self.cc_gen = getattr(self, "cc_gen", {})
            self.cc_gen[name] = 0
        self.cc_gen[name] += 1
        tok = (name, self.cc_gen[name])
        self._emit("pool", deps, fn, name, 1)
        self._mark(tok, reads, writes)

    def final_wait(self, eng, keys):
        deps = self._deps(keys, ())
        self._emit(eng, deps, None, None, 0)

    def replay(self, eng, e):
        for waits, fn, semname, inc in self.streams[eng]:
            for n, v in waits:
                e.wait_ge(self.sems[n], v)
            if fn is not None:
                ins = fn(e)
                ins.then_inc(self.sems[semname], inc)


class _Stop(Exception):
    pass


def build_program(L, stop=None):
    nc = bass.Bass("TRN2", target_bir_lowering=False)
    sc = Sched(nc)

    def din(name, shape, dt=F32):
        return nc.dram_tensor(name, shape, dt, kind="ExternalInput").ap()

    x_in = din("x", [NTOK, D])
    w_in = din("w_in", [L, D, INW])
    w_uq = din("w_uq", [L, 256, H * DK])
    w_ukv = din("w_ukv", [L, 256, H * 128])
    w_attn = din("w_attn", [L, 512, D])
    w_four = din("w_four", [L, 512, D])
    w_out = din("w_out", [L, D, D])
    normg_d = din("norm_gT", [L, 128, 8])
    glat_d = din("glatT", [L, 128, 4])
    gq_d = din("gq_rep", [L, 128, DK])
    gk_d = din("gk_rep", [L, 128, DK])
    bm_d = din("bmT", [L, 128, 16])
    rope_d = din("rope_cs", [128, NT, 32])
    dft_d = din("dft", [4, 8, 128, 4096], BF16)
    ccsc_d = din("ccsc", [128, 256], BF16)
    identb_d = din("ident_bf", [128, 128], BF16)
    identf_d = din("ident_f", [128, 128])
    y_out = nc.dram_tensor("y", [NTOK, D], F32, kind="ExternalOutput").ap()
    if DEBUG:
        dbg_og = nc.dram_tensor("dbg_og", [128, 4, NTOK], BF16, kind="ExternalOutput").ap()
        dbg_fg = nc.dram_tensor("dbg_fg", [128, 4, NTOK], BF16, kind="ExternalOutput").ap()
        dbg_m = nc.dram_tensor("dbg_m", [128, 8, NTOK], BF16, kind="ExternalOutput").ap()

    kloc = [[nc.dram_tensor(f"kloc{l}_{j}", [H * DK, 512], BF16) for j in range(4)] for l in range(L)]
    kful = [[nc.dram_tensor(f"kful{l}_{j}", [2 * H * DK, 512], BF16) for j in range(4)] for l in range(L)]
    vloc = [[nc.dram_tensor(f"vloc{l}_{j}", [512, H * 65], BF16) for j in range(4)] for l in range(L)]
    vful = [[nc.dram_tensor(f"vful{l}_{j}", [2 * 512, H * 65], BF16) for j in range(4)] for l in range(L)]
    aloc = [[nc.dram_tensor(f"aloc{l}_{j}", [512, 1024], BF16) for j in range(4)] for l in range(L)]
    aful = [[nc.dram_tensor(f"aful{l}_{j}", [2 * 512, 1024], BF16) for j in range(4)] for l in range(L)]
    xbuf = [nc.dram_tensor(f"xbuf{l}", [NTOK, D], F32) for l in range(max(L - 1, 1))]

    def sb(name, shape, dt):
        return nc.alloc_sbuf_tensor(name, shape, dt).ap()

    hT = sb("hT", [128, 8, NTOK], BF16)
    qT = sb("qT", [128, 8, NTOK], BF16)
    ogT = sb("ogT", [128, 4, NTOK], BF16)
    fgT = sb("fgT", [128, 4, NTOK], BF16)
    arena = sb("arena", [128, 16384], BF16)
    wst = sb("wst", [128, 2, 8, 512], BF16)
    stage = sb("stage", [128, 2, 1024], F32)
    xt = sb("xt", [128, 2, D], F32)
    tmp = sb("tmp", [128, 18432], BF16)
    identb = sb("identb", [128, 128], BF16)
    identf = sb("identf", [128, 128], F32)
    rope = sb("rope", [128, NT, 32], F32)
    ccsc = sb("ccsc_sb", [128, 256], BF16)
    normg = sb("normg", [128, 8], F32)
    glat = sb("glat", [128, 4], F32)
    gq = sb("gq", [128, DK], F32)
    gk = sb("gk", [128, DK], F32)
    bm = sb("bm", [128, 16], F32)
    stats = sb("stats", [128, 64], F32)
    eps_t = sb("eps_t", [128, 1], F32)

    psQ = nc.alloc_psum_tensor("psQ", [128, 1024], F32).ap()
    psK = nc.alloc_psum_tensor("psK", [128, 1024], F32).ap()
    psS = [nc.alloc_psum_tensor(f"psS{i}", [128, 1024], F32).ap() for i in range(2)]

    def bank(i):
        if i < 2:
            return psQ[:, i * 512:(i + 1) * 512]
        if i < 4:
            return psK[:, (i - 2) * 512:(i - 1) * 512]
        return psS[(i - 4) // 2][:, ((i - 4) % 2) * 512:((i - 4) % 2 + 1) * 512]

    def bk(i):
        return ("ps", i)

    class Tmp:
        def __init__(self, base_ap, prefix, nbytes):
            self.base, self.prefix, self.nbytes = base_ap, prefix, nbytes
            self.off = 0

        def reset(self):
            self.off = 0

        def get(self, shape, dt):
            es = 4 if dt == F32 else 2
            n = int(np.prod(shape[1:]))
            nb = (n * es + 63) // 64 * 64
            a, b = self.off, self.off + nb
            assert b <= self.nbytes, (self.prefix, b)
            self.off = b
            ap = self.base[:, a // 2:(a + n * es) // 2]
            if dt == F32:
                ap = ap.bitcast(F32)
            if len(shape) == 3:
                ap = ap.rearrange("p (a b) -> p a b", b=shape[2])
            ap = ap[0:shape[0]]
            keys = tuple((self.prefix, i) for i in range(a // 1024, (b - 1) // 1024 + 1))
            return ap, keys

    T = Tmp(tmp, "tmp", 36864)
    AR = Tmp(arena, "arena", 32768)

    stage_rr = [0]

    def load_w(dst, src, n, dst_keys, inner=None, eng="pool"):
        s = stage_rr[0]
        stage_rr[0] ^= 1
        st_ap = stage[:, s, 0:n]
        if inner is not None:
            st_ap = st_ap.rearrange("p (a b) -> p a b", b=inner)
        sc.dma("sp", lambda e, o=st_ap, i=src: e.dma_start(out=o, in_=i), writes=[("stage", s)])
        sc.op(eng, lambda e, o=dst, i=st_ap: e.tensor_copy(out=o, in_=i),
              reads=[("stage", s)], writes=dst_keys)

    def load_small(dst, src, key):
        sc.dma("sp", lambda e, o=dst, i=src: e.dma_start(out=o, in_=i), writes=[key])

    sc.op("dve", lambda e: e.memset(eps_t, EPS), writes=["eps_t"])
    load_small(identb, identb_d, "identb")
    load_small(identf, identf_d, "identf")
    load_small(rope, rope_d, "rope")
    load_small(ccsc, ccsc_d, "ccsc")

    def w_in_v(l):
        return w_in[l].rearrange("(k p) n -> p k n", p=128)

    def chk(name):
        if stop == name:
            sc.dma("sp", lambda e: e.dma_start(out=y_out, in_=x_in), writes=[("xdst", L - 1, t) for t in range(NT)])
            raise _Stop()

    try:
      chk("init")
      for l in range(L):
        if l > 0:
            sc.new_epoch()
        x_src = x_in if l == 0 else xbuf[l - 1].ap()
        x_dst = y_out if l == L - 1 else xbuf[l].ap()
        xs_v = x_src.rearrange("(t p) d -> p t d", p=128)
        xd_v = x_dst.rearrange("(t p) d -> p t d", p=128)

        load_small(normg, normg_d[l], "normg")
        load_small(glat, glat_d[l], "glat")
        load_small(gq, gq_d[l], "gq")
        load_small(gk, gk_d[l], "gk")
        load_small(bm, bm_d[l], "bm")
        wv = w_in_v(l)
        AR.reset()
        wA, k_wA = AR.get([128, 8, 544], BF16)
        wuq_sb, k_wuq = AR.get([128, 2, H * DK], BF16)
        wukv_sb, k_wukv = AR.get([128, 2, H * 128], BF16)
        uT, k_uT = AR.get([128, 4, 512], BF16)
        atile, k_atile = AR.get([128, 2, 1024], BF16)
        for k in range(8):
            load_w(wA[:, k, :], wv[:, k, 0:544], 544, [("wA", k)] + list(k_wA), eng="dve")
        wuq_v = w_uq[l].rearrange("(k p) n -> p k n", p=128)
        wukv_v = w_ukv[l].rearrange("(k p) n -> p k n", p=128)
        for k in range(2):
            load_w(wuq_sb[:, k, :], wuq_v[:, k, :], H * DK, [("wuq", k)] + list(k_wuq), eng="dve")
            load_w(wukv_sb[:, k, :], wukv_v[:, k, :], H * 128, [("wukv", k)] + list(k_wukv), eng="dve")
        wA_keys = [("wA", k) for k in range(8)] + list(k_wA)

        groups = [[0, 1], [2, 3], [4, 5], [6, 7]]

        def exchange1(nm, loc_l, ful_l, slot0, j):
            sc.collective(slot0 + j, lambda e, i=loc_l[j], o=ful_l[j]: e.collective_compute(
                "AllGather", ALU.bypass, replica_groups=groups, ins=[i.ap().opt()], outs=[o.ap().opt()]),
                reads=[(nm + "loc", l, j * 4 + i) for i in range(4)], writes=[(nm + "ful", l, j)])

        T.reset()
        xsb, k_xsb = T.get([128, D], BF16)
        junk, k_junk = T.get([128, D], BF16)
        junk2, k_junk2 = T.get([128, 32], BF16)
        cn, k_cn = T.get([128, 512], BF16)
        cnT, k_cnT = T.get([128, 4, 128], BF16)
        qsq, k_qsq = T.get([128, 768], F32)
        qn, k_qn = T.get([128, 768], F32)
        qg, k_qg = T.get([128, 768], F32)
        qb, k_qb = T.get([128, 768], BF16)
        ksq, k_ksq = T.get([128, 512], F32)
        knn, k_knn = T.get([128, 512], F32)
        kb, k_kb = T.get([128, 768], BF16)
        kpeg, k_kpeg = T.get([128, 32], F32)
        kpeh, k_kpeh = T.get([128, 256], F32)
        rtq, k_rtq = T.get([128, 4, 128], F32)
        rtk, k_rtk = T.get([128, 4, 128], F32)
        vb, k_vb = T.get([128, 2, H * 65], BF16)
        ktile, k_ktile = T.get([128, 2, 1024], BF16)

        vb4 = vb.rearrange("p s (h c) -> p s h c", c=65)
        sc.op("dve", lambda e, o=vb: e.memset(o, 1.0), writes=k_vb)

        def load_x(t):
            rd = [("xdst", l - 1, t)] if l > 0 else []
            sc.dma("sp", lambda e, o=xt[:, t % 2, :], i=xs_v[:, t, :]: e.dma_start(out=o, in_=i), reads=rd, writes=[("xt", t % 2)])

        qps = psQ[:, 0:768]
        q3 = qps.rearrange("p (h d) -> p h d", d=DK)
        kv3 = psK.rearrange("p (h d) -> p h d", d=128)
        qn3 = qn.rearrange("p (h d) -> p h d", d=DK)
        qg3 = qg.rearrange("p (h d) -> p h d", d=DK)
        qb3 = qb.rearrange("p (h d) -> p h d", d=DK)
        ksq3 = ksq.rearrange("p (h d) -> p h d", d=64)
        knn3 = knn.rearrange("p (h d) -> p h d", d=64)
        kb3 = kb.rearrange("p (h d) -> p h d", d=DK)
        kpeh3 = kpeh.rearrange("p (h d) -> p h d", d=32)
        ST = lambda a, b_=None: stats[:, a:(a + 1 if b_ is None else b_)]

        def tr_heads(e, dst, src):
            ins = None
            for h in range(H):
                ins = e.transpose(out=dst[0:DK, h, :], in_=src[:, h, :], identity=identb)
            return ins

        def chain_S1(t):
            ops = []
            xs_ = xt[:, t % 2, :]
            kx = ("xt", t % 2)
            ops.append(lambda: sc.op("act", lambda e: e.activation(out=junk, in_=xs_, func=AF.Square, accum_out=ST(0)),
                                     reads=[kx], writes=list(k_junk) + [("st", 0)]))
            ops.append(lambda: sc.op("act", lambda e: e.activation(out=ST(2), in_=ST(0), func=AF.Sqrt, scale=1.0 / D, bias=eps_t),
                                     reads=[("st", 0), "eps_t"], writes=[("st", 2)]))
            ops.append(lambda: sc.op("dve", lambda e: e.reciprocal(out=ST(3), in_=ST(2)), reads=[("st", 2)], writes=[("st", 3)]))
            ops.append(lambda: sc.op("act", lambda e: e.activation(out=xsb, in_=xs_, func=AF.Copy, scale=ST(3)),
                                     reads=[kx, ("st", 3)], writes=k_xsb))
            pt = bank(6).bitcast(BF16).rearrange("p (k j) -> p k j", j=128)

            def f_tr(e):
                ins = None
                for k in range(8):
                    ins = e.transpose(out=pt[:, k, :], in_=xsb[:, k * 128:(k + 1) * 128], identity=identb)
                return ins
            ops.append(lambda: sc.op("pe", f_tr, reads=list(k_xsb) + ["identb"], writes=[bk(6)]))
            hT_t = hT[:, :, t * 128:(t + 1) * 128]
            ops.append(lambda: sc.op("dve", lambda e: e.tensor_tensor(out=hT_t, in0=pt, in1=normg.unsqueeze(2).to_broadcast([128, 8, 128]), op=ALU.mult),
                                     reads=[bk(6), "normg"], writes=[("hT", t)]))

            def f_cq(e):
                ins = None
                for k in range(8):
                    ins = e.matmul(bank(4), lhsT=hT[:, k, t * 128:(t + 1) * 128], rhs=wA[:, k, 0:512], start=(k == 0), stop=(k == 7))
                for k in range(8):
                    ins = e.matmul(bank(5)[:, 0:32], lhsT=hT[:, k, t * 128:(t + 1) * 128], rhs=wA[:, k, 512:544], start=(k == 0), stop=(k == 7))
                return ins
            ops.append(lambda: sc.op("pe", f_cq, reads=[("hT", t)] + wA_keys, writes=[bk(4), bk(5)]))
            for j in range(2):
                ops.append(lambda j=j: sc.op("act", lambda e: e.activation(out=junk[:, 0:256], in_=bank(4)[:, j * 256:(j + 1) * 256], func=AF.Square, accum_out=ST(4 + j)),
                                             reads=[bk(4)], writes=list(k_junk) + [("st", 4 + j)]))
            ops.append(lambda: sc.op("act", lambda e: e.activation(out=ST(8, 10), in_=ST(4, 6), func=AF.Sqrt, scale=1.0 / 256, bias=eps_t),
                                     reads=[("st", 4), ("st", 5), "eps_t"], writes=[("st", 8)]))
            ops.append(lambda: sc.op("dve", lambda e: e.reciprocal(out=ST(10, 12), in_=ST(8, 10)), reads=[("st", 8)], writes=[("st", 10)]))
            for j in range(2):
                ops.append(lambda j=j: sc.op("dve", lambda e: e.tensor_scalar(out=cn[:, j * 256:(j + 1) * 256], in0=bank(4)[:, j * 256:(j + 1) * 256], scalar1=ST(10 + j), scalar2=None, op0=ALU.mult),
                                             reads=[bk(4), ("st", 10)], writes=k_cn))
            pc = bank(7).bitcast(BF16)[:, 0:512].rearrange("p (k j) -> p k j", j=128)

            def f_trc(e):
                ins = None
                for k in range(4):
                    ins = e.transpose(out=pc[:, k, :], in_=cn[:, k * 128:(k + 1) * 128], identity=identb)
                return ins
            ops.append(lambda: sc.op("pe", f_trc, reads=list(k_cn) + ["identb"], writes=[bk(7)]))
            ops.append(lambda: sc.op("dve", lambda e: e.tensor_tensor(out=cnT, in0=pc, in1=glat.unsqueeze(2).to_broadcast([128, 4, 128]), op=ALU.mult),
                                     reads=[bk(7), "glat"], writes=k_cnT))

            def f_q(e):
                ins = None
                for k in range(2):
                    ins = e.matmul(bank(0), lhsT=cnT[:, k, :], rhs=wuq_sb[:, k, 0:512], start=(k == 0), stop=(k == 1))
                for k in range(2):
                    ins = e.matmul(bank(1)[:, 0:256], lhsT=cnT[:, k, :], rhs=wuq_sb[:, k, 512:768], start=(k == 0), stop=(k == 1))
                for hf in range(2):
                    for k in range(2):
                        ins = e.matmul(bank(2 + hf), lhsT=cnT[:, 2 + k, :], rhs=wukv_sb[:, k, hf * 512:(hf + 1) * 512], start=(k == 0), stop=(k == 1))
                return ins
            last = lambda: sc.op("pe", f_q, reads=list(k_cnT) + [("wuq", 0), ("wuq", 1), ("wukv", 0), ("wukv", 1)] + list(k_wuq) + list(k_wukv),
                                 writes=[bk(0), bk(1), bk(2), bk(3)])
            return ops, last

        def rope_chain(ops, eng, x1, x2, o1, o2, rt, k_src, k_rt, k_dst, cos_b, sin_b):
            rt4 = rt.rearrange("p a (h d) -> p a h d", d=16)
            ops.append(lambda: sc.op(eng, lambda e: e.tensor_tensor(out=rt4[:, 0], in0=x1, in1=cos_b, op=ALU.mult), reads=list(k_src) + ["rope"], writes=k_rt))
            ops.append(lambda: sc.op(eng, lambda e: e.tensor_tensor(out=rt4[:, 1], in0=x2, in1=sin_b, op=ALU.mult), reads=list(k_src) + ["rope"], writes=k_rt))
            ops.append(lambda: sc.op(eng, lambda e: e.tensor_tensor(out=rt4[:, 2], in0=x2, in1=cos_b, op=ALU.mult), reads=list(k_src) + ["rope"], writes=k_rt))
            ops.append(lambda: sc.op(eng, lambda e: e.tensor_tensor(out=rt4[:, 3], in0=x1, in1=sin_b, op=ALU.mult), reads=list(k_src) + ["rope"], writes=k_rt))
            ops.append(lambda: sc.op(eng, lambda e: e.tensor_tensor(out=o1, in0=rt4[:, 0], in1=rt4[:, 1], op=ALU.subtract), reads=k_rt, writes=k_dst))
            ops.append(lambda: sc.op(eng, lambda e: e.tensor_tensor(out=o2, in0=rt4[:, 2], in1=rt4[:, 3], op=ALU.add), reads=k_rt, writes=k_dst))

        def chain_Sq(t):
            ops = []
            cos_b = rope[:, t, 0:16].unsqueeze(1).to_broadcast([128, H, 16])
            sin_b = rope[:, t, 16:32].unsqueeze(1).to_broadcast([128, H, 16])
            ops.append(lambda: sc.op("act", lambda e: e.activation(out=qsq, in_=qps, func=AF.Square), reads=[bk(0), bk(1)], writes=k_qsq))
            ops.append(lambda: sc.op("dve", lambda e: e.tensor_reduce(out=ST(16, 24), in_=qsq.rearrange("p (h d) -> p h d", d=DK), axis=AX.X, op=ALU.add),
                                     reads=k_qsq, writes=[("st", 16)]))
            ops.append(lambda: sc.op("act", lambda e: e.activation(out=ST(32, 40), in_=ST(16, 24), func=AF.Sqrt, scale=1.0 / DK, bias=eps_t),
                                     reads=[("st", 16), "eps_t"], writes=[("st", 32)]))
            ops.append(lambda: sc.op("dve", lambda e: e.reciprocal(out=ST(40, 48), in_=ST(32, 40)), reads=[("st", 32)], writes=[("st", 40)]))
            ops.append(lambda: sc.op("dve", lambda e: e.tensor_tensor(out=qn3, in0=q3, in1=ST(40, 48).unsqueeze(2).to_broadcast([128, H, DK]), op=ALU.mult),
                                     reads=[bk(0), bk(1), ("st", 40)], writes=k_qn))
            ops.append(lambda: sc.op("pool", lambda e: e.tensor_tensor(out=qg3, in0=qn3, in1=gq.unsqueeze(1).to_broadcast([128, H, DK]), op=ALU.mult),
                                     reads=list(k_qn) + ["gq"], writes=k_qg))
            ops.append(lambda: sc.op("pool", lambda e: e.tensor_copy(out=qb3[:, :, 0:64], in_=qg3[:, :, 0:64]), reads=k_qg, writes=k_qb))
            rope_chain(ops, "pool", qg3[:, :, 64:80], qg3[:, :, 80:96], qb3[:, :, 64:80], qb3[:, :, 80:96], rtq, k_qg, k_rtq, k_qb, cos_b, sin_b)
            pq = bank(6).bitcast(BF16).rearrange("p (h j) -> p h j", j=128)
            ops.append(lambda: sc.op("pe", lambda e: tr_heads(e, pq, qb3), reads=list(k_qb) + ["identb"], writes=[bk(6)]))
            ops.append(lambda: sc.op("act", lambda e: e.activation(out=qT[0:DK, :, t * 128:(t + 1) * 128], in_=pq[0:DK], func=AF.Copy),
                                     reads=[bk(6)], writes=[("qT", t)]))
            return ops

        def chain_Sk(t):
            ops = []
            cos_b = rope[:, t, 0:16].unsqueeze(1).to_broadcast([128, H, 16])
            sin_b = rope[:, t, 16:32].unsqueeze(1).to_broadcast([128, H, 16])
            vs = t % 2
            kvs = ("vb", vs)
            rk = ST(56, 64)
            ops.append(lambda: sc.op("act", lambda e: e.activation(out=ksq3, in_=kv3[:, :, 0:64], func=AF.Square), reads=[bk(2), bk(3)], writes=k_ksq))
            ops.append(lambda: sc.op("act", lambda e: e.activation(out=junk2, in_=bank(5)[:, 0:32], func=AF.Square, accum_out=ST(12)),
                                     reads=[bk(5)], writes=list(k_junk2) + [("st", 12)]))
            ops.append(lambda: sc.op("dve", lambda e: e.tensor_tensor(out=kpeg, in0=bank(5)[:, 0:32], in1=gk[:, 64:96], op=ALU.mult),
                                     reads=[bk(5), "gk"], writes=k_kpeg))
            ops.append(lambda: sc.op("act", lambda e: e.activation(out=vb4[:, vs, :, 0:64], in_=kv3[:, :, 64:128], func=AF.Copy),
                                     reads=[bk(2), bk(3)] + list(k_vb), writes=[kvs]))
            ops.append(lambda: sc.op("dve", lambda e: e.tensor_reduce(out=ST(48, 56), in_=ksq3, axis=AX.X, op=ALU.add), reads=k_ksq, writes=[("st", 48)]))
            ops.append(lambda: sc.op("dve", lambda e: e.tensor_scalar(out=ST(48, 56), in0=ST(48, 56), scalar1=ST(12), scalar2=None, op0=ALU.add),
                                     reads=[("st", 48), ("st", 12)], writes=[("st", 48)]))
            ops.append(lambda: sc.op("act", lambda e: e.activation(out=ST(24, 32), in_=ST(48, 56), func=AF.Sqrt, scale=1.0 / DK, bias=eps_t),
                                     reads=[("st", 48), "eps_t"], writes=[("st", 24)]))
            ops.append(lambda: sc.op("dve", lambda e: e.reciprocal(out=ST(56, 64), in_=ST(24, 32)), reads=[("st", 24)], writes=[("st", 56)]))
            ops.append(lambda: sc.op("dve", lambda e: e.tensor_tensor(out=knn3, in0=kv3[:, :, 0:64], in1=rk.unsqueeze(2).to_broadcast([128, H, 64]), op=ALU.mult),
                                     reads=[bk(2), bk(3), ("st", 56)], writes=k_knn))
            vdst = vloc[l][t // 4].ap()[(t % 4) * 128:(t % 4 + 1) * 128, :]
            ops.append(lambda: sc.dma("sp", lambda e: e.dma_start(out=vdst, in_=vb[:, vs, :]),
                                      reads=[kvs] + list(k_vb), writes=[("vloc", l, t)]))
            ops.append(lambda: sc.op("pool", lambda e: e.tensor_tensor(out=kb3[:, :, 0:64], in0=knn3, in1=gk[:, 0:64].unsqueeze(1).to_broadcast([128, H, 64]), op=ALU.mult),
                                     reads=list(k_knn) + ["gk"], writes=k_kb))
            ops.append(lambda: sc.op("dve", lambda e: e.tensor_tensor(out=kpeh3, in0=kpeg.unsqueeze(1).to_broadcast([128, H, 32]), in1=rk.unsqueeze(2).to_broadcast([128, H, 32]), op=ALU.mult),
                                     reads=list(k_kpeg) + [("st", 56)], writes=k_kpeh))
            rope_chain(ops, "dve", kpeh3[:, :, 0:16], kpeh3[:, :, 16:32], kb3[:, :, 64:80], kb3[:, :, 80:96], rtk, k_kpeh, k_rtk, k_kb, cos_b, sin_b)
            pk = bank(7).bitcast(BF16).rearrange("p (h j) -> p h j", j=128)
            ops.append(lambda: sc.op("pe", lambda e: tr_heads(e, pk, kb3), reads=list(k_kb) + ["identb"], writes=[bk(7)]))
            kt_ap = ktile[:, vs, :].rearrange("p (h j) -> p h j", j=128)
            kks = ("ktile", vs)
            ops.append(lambda: sc.op("act", lambda e: e.activation(out=kt_ap[0:DK], in_=pk[0:DK], func=AF.Copy),
                                     reads=[bk(7)] + list(k_ktile), writes=[kks]))
            kdst = kloc[l][t // 4].ap().rearrange("(h d) n -> d h n", d=DK)[:, :, (t % 4) * 128:(t % 4 + 1) * 128]
            ops.append(lambda: sc.dma("sp", lambda e: e.dma_start(out=kdst, in_=kt_ap[0:DK]),
                                      reads=[kks] + list(k_ktile), writes=[("kloc", l, t)]))
            return ops

        def interleave(chains):
            n = max(len(c) for c in chains)
            for r in range(n):
                for c in chains:
                    if r < len(c):
                        c[r]()

        for k in range(8):
            load_w(wst[:, 0, k, :], wv[:, k, C_UF:C_UF + 512], 512, [("wst", 0, k)])
        wst0_keys = [("wst", 0, k) for k in range(8)]

        def a3_u_chain(tt):
            ops = []
            for g in range(4):
                def f_u(e, g=g):
                    ins = None
                    for k in range(8):
                        ins = e.matmul(bank(7), lhsT=wst[:, 0, k, g * 128:(g + 1) * 128], rhs=hT[:, k, tt * 512:(tt + 1) * 512], start=(k == 0), stop=(k == 7))
                    return ins
                ops.append(lambda f_u=f_u: sc.op("pe", f_u, reads=[("hT", tt * 4 + j) for j in range(4)] + wst0_keys, writes=[bk(7)]))
                ops.append(lambda g=g: sc.op("act", lambda e: e.activation(out=uT[:, g, :], in_=bank(7), func=AF.Copy),
                                             reads=[bk(7)], writes=[("uT", g)] + list(k_uT)))
            return ops

        def a3_group(tt):
            for j in range(4):
                t = tt * 4 + j
                ps2 = psS[1]
                pkeys = [bk(6), bk(7)]

                def f_a(e, j=j, ps2=ps2):
                    ins = None
                    for g in range(4):
                        ins = e.matmul(ps2[:, g * 256:(g + 1) * 256], lhsT=uT[:, g, j * 128:(j + 1) * 128], rhs=ccsc, start=True, stop=True)
                    return ins
                sc.op("pe", f_a, reads=[("uT", g) for g in range(4)] + list(k_uT) + ["ccsc"], writes=pkeys)
                a_s = t % 2
                sc.op("act", lambda e, o=atile[:, a_s, :], i=ps2: e.activation(out=o, in_=i, func=AF.Copy),
                      reads=pkeys + list(k_atile), writes=[("atile", a_s)])
                adst = aloc[l][t // 4].ap()[(t % 4) * 128:(t % 4 + 1) * 128, :]
                sc.dma("sp", lambda e, o=adst, i=atile[:, a_s, :]: e.dma_start(out=o, in_=i),
                       reads=[("atile", a_s)] + list(k_atile), writes=[("aloc", l, t)])

        load_x(0)
        load_x(1)
        ops1, last1 = chain_S1(0)
        interleave([ops1])
        last1()
        for t in range(NT):
            chains = [chain_Sq(t), chain_Sk(t)]
            nxt = None
            if t + 1 < NT:
                if t + 2 < NT:
                    load_x(t + 2)
                ops1, nxt = chain_S1(t + 1)
                chains = [ops1] + chains
            if t % 4 == 3:
                chains.append(a3_u_chain(t // 4))
            interleave(chains)
            if nxt is not None:
                nxt()
            if t % 4 == 3:
                a3_group(t // 4)
            if (t >= 5 and t % 4 == 1) or t == NT - 1:
                for gidx in ([t // 4 - 1] if t < NT - 1 else [t // 4 - 1, t // 4] if t % 4 == 1 else [t // 4]):
                    exchange1("k", kloc[l], kful[l], 0, gidx)
                    exchange1("v", vloc[l], vful[l], 4, gidx)
                    exchange1("a", aloc[l], aful[l], 8, gidx)

        chk("A")

        chk("X")
        for k in range(8):
            load_w(wst[:, 1, k, :], wv[:, k, C_ZA:C_ZA + 512], 512, [("wst", 1, k)])
        wst1_keys = [("wst", 1, k) for k in range(8)]

        T.reset()
        AR.reset()
        NPT = 4
        pT_slots = [T.get([128, 1024], BF16) for _ in range(NPT)]
        oacc2 = [T.get([128, 512], F32) for _ in range(2)]
        za2, og2 = [], []
        for s_ in range(2):
            za2.append(T.get([128, NT, 128], BF16))
            og2.append(T.get([128, NT, 128], BF16))
        rinv, k_rinv = T.get([128, 8], F32)
        kv_bufs = []
        for s_ in range(2):
            kT_b, k_kT = AR.get([128, S], BF16)
            v_b, k_v = AR.get([128, 32, 65], BF16)
            kv_bufs.append((kT_b, k_kT, v_b, k_v))
        scale = float(DK) ** -0.5
        kfv = [kful[l][j].ap().rearrange("(r h d) n -> r h d n", r=2, h=H) for j in range(4)]
        vfv = [vful[l][j].ap().rearrange("(r i p) (h c) -> p r i h c", p=128, i=4, c=65) for j in range(4)]
        its = [(h, qt, kp) for h in range(H) for qt in range(4) for kp in range(16)]
        LOOK = 2
        SD = [(psK, 2), (psS[0], 4), (psS[1], 6)]

        def kvk(h):
            kT_b, k_kT, v_b, k_v = kv_bufs[h % 2]
            return [("kT", h % 2, i) for i in range(8)] + [("v", h % 2, j) for j in range(8)] + list(k_kT) + list(k_v)

        def kv_load(h):
            kT_b, k_kT, v_b, k_v = kv_bufs[h % 2]
            for r in range(2):
                for j in range(4):
                    c0 = r * NTOK + j * 512
                    first = (r == 0 and j == 0)
                    sc.dma("sp", lambda e, o=kT_b[0:DK, c0:c0 + 512], i=kfv[j][r, h]: e.dma_start(out=o, in_=i),
                           reads=[("kful", l, j)], writes=[("kT", h % 2, r * 4 + j)] + (list(k_kT) + list(k_v) if first else []))
            v5 = v_b.rearrange("p (r j i) c -> p r j i c", r=2, j=4)
            for j in range(4):
                for r in range(2):
                    sc.dma("sp", lambda e, o=v5[:, r, j], i=vfv[j][:, r, :, h, :]: e.dma_start(out=o, in_=i),
                           reads=[("vful", l, j)], writes=[("v", h % 2, j * 2 + r)])

        zt, k_zt = T.get([128, 512], F32)

        def za_part(hp, tq):
            za_sb, k_za = za2[hp % 2]

            def f_za(e):
                ins = None
                for j in range(4):
                    t = tq * 4 + j
                    for k in range(8):
                        ins = e.matmul(bank(1)[:, j * 128:(j + 1) * 128], lhsT=hT[:, k, t * 128:(t + 1) * 128], rhs=wst[:, 1, k, hp * 128:(hp + 1) * 128], start=(k == 0), stop=(k == 7))
                return ins
            sc.op("pe", f_za, reads=[("hT", tq * 4 + j) for j in range(4)] + wst1_keys, writes=[bk(1)])
            sc.op("act", lambda e: e.activation(out=zt, in_=bank(1), func=AF.Exp, scale=-1.0), reads=[bk(1)], writes=k_zt)
            sc.op("dve", lambda e: e.tensor_scalar(out=zt, in0=zt, scalar1=1.0, scalar2=None, op0=ALU.add), reads=k_zt, writes=k_zt)
            sc.op("dve", lambda e: e.reciprocal(out=zt, in_=zt), reads=k_zt, writes=k_zt)
            sc.op("dve", lambda e: e.tensor_tensor(out=za_sb[:, tq * 4:(tq + 1) * 4, :], in0=bank(1).rearrange("p (j c) -> p j c", c=128),
                                                   in1=zt.rearrange("p (j c) -> p j c", c=128), op=ALU.mult),
                  reads=[bk(1)] + list(k_zt), writes=[("za", hp % 2, tq)] + list(k_za))

        def trg_part(hp, tq):
            og, k_og = og2[hp % 2]
            pg = bank(1).bitcast(BF16)[:, 0:512].rearrange("p (j c) -> p j c", c=128)

            def f_trg(e):
                ins = None
                for j in range(4):
                    ins = e.transpose(out=pg[:, j, :], in_=og[:, tq * 4 + j, :], identity=identb)
                return ins
            sc.op("pe", f_trg, reads=[("og", hp % 2, tq * 4 + j) for j in range(4)] + list(k_og) + ["identb"], writes=[bk(1)])
            sc.op("dve", lambda e: e.tensor_copy(out=ogT[:, hp, tq * 512:(tq + 1) * 512], in_=bank(1).bitcast(BF16)[:, 0:512]),
                  reads=[bk(1)], writes=[("ogT", hp, tq)])

        def emit_qk(i):
            h, qt, kp = its[i]
            if h == 0 and qt == 0 and kp in (1, 3, 5, 7):
                za_part(0, (kp - 1) // 2)
            if h % 2 == 1 and kp == 4 and h + 1 < H:
                za_part((h + 1) // 2, qt)
            if h % 2 == 0 and h >= 2 and kp == 10:
                trg_part(h // 2 - 1, qt)
            kT_b = kv_bufs[h % 2][0]
            sd, b0 = SD[i % 3]
            pslot = i % NPT

            def f_qk(e, sd=sd, kT_b=kT_b, h=h, qt=qt, kp=kp):
                ins = None
                for u in range(2):
                    kt = 2 * kp + u
                    ins = e.matmul(sd[:, u * 512:(u + 1) * 512], lhsT=kT_b[0:DK, kt * 128:(kt + 1) * 128], rhs=qT[0:DK, h, qt * 512:(qt + 1) * 512], start=True, stop=True)
                return ins
            sc.op("pe", f_qk, reads=kvk(h) + [("qT", qt * 4 + j) for j in range(4)], writes=[bk(b0), bk(b0 + 1)])
            pT_s, k_pT_s = pT_slots[pslot]
            sc.op("act", lambda e, o=pT_s, i=sd: e.activation(out=o, in_=i, func=AF.Exp, scale=scale),
                  reads=[bk(b0), bk(b0 + 1)], writes=[("pT", pslot)] + list(k_pT_s))

        def emit_pv(i):
            h, qt, kp = its[i]
            hp, hh = h // 2, h % 2
            v_b = kv_bufs[h % 2][2]
            pslot = i % NPT
            ob = 0
            osl = qt % 2

            pT_s, k_pT_s = pT_slots[pslot]

            def f_pv(e, ob=ob, v_b=v_b, pT_s=pT_s, kp=kp):
                ins = None
                for u in range(2):
                    kt = 2 * kp + u
                    ins = e.matmul(bank(ob)[0:65, :], lhsT=v_b[:, kt, :], rhs=pT_s[:, u * 512:(u + 1) * 512], start=(kt == 0), stop=(kt == 31))
                return ins
            sc.op("pe", f_pv, reads=kvk(h) + [("pT", pslot)] + list(k_pT_s), writes=[bk(ob)])
            if kp != 15:
                return
            if qt == 3 and h + 2 < H:
                kv_load(h + 2)
            za_sb, k_za = za2[hp % 2]
            og, k_og = og2[hp % 2]
            oacc, k_oacc = oacc2[osl]
            sc.op("dve", lambda e, o=oacc[0:65, :], i=bank(ob)[0:65, :]: e.tensor_copy(out=o, in_=i),
                  reads=[bk(ob)], writes=[("oacc", osl)] + list(k_oacc))
            po = bank(1)[:, 0:260].rearrange("p (j c) -> p j c", c=65)

            def f_tro(e, oacc=oacc, po=po):
                ins = None
                for j in range(4):
                    ins = e.transpose(out=po[:, j, :], in_=oacc[0:65, j * 128:(j + 1) * 128], identity=identf[0:65, 0:65])
                return ins
            sc.op("pe", f_tro, reads=[("oacc", osl), "identf"] + list(k_oacc), writes=[bk(1)])
            sc.op("dve", lambda e, o=rinv[:, 0:4], i=po[:, :, 64]: e.reciprocal(out=o, in_=i),
                  reads=[bk(1)], writes=k_rinv)
            for j in range(4):
                t = qt * 4 + j
                sc.op("dve", lambda e, o=og[:, t, hh * 64:(hh + 1) * 64], i=po[:, j, 0:64], s_=rinv[:, j:j + 1], z=za_sb[:, t, hh * 64:(hh + 1) * 64]:
                      e.scalar_tensor_tensor(out=o, in0=i, scalar=s_, in1=z, op0=ALU.mult, op1=ALU.mult),
                      reads=[bk(1), ("za", hp % 2, qt)] + list(k_rinv) + list(k_za), writes=[("og", hp % 2, t)] + list(k_og))

        kv_load(0)
        kv_load(1)
        for i in range(LOOK):
            emit_qk(i)
        for i in range(len(its)):
            if i + LOOK < len(its):
                emit_qk(i + LOOK)
            emit_pv(i)
        for tq in range(4):
            trg_part(H // 2 - 1, tq)

        chk("B")
        for k in range(8):
            load_w(wst[:, 0, k, :], wv[:, k, C_ZF:C_ZF + 512], 512, [("wst", 0, k)])
        T.reset()
        AR.reset()
        szf, k_szf = T.get([128, 2, 512], BF16)
        NBUF = 3
        dbufs = [AR.get([128, 4096], BF16) for _ in range(NBUF)]
        abufs = [T.get([128, 4, 1024], BF16) for _ in range(NBUF)]
        fscale = float(S * 128) ** -0.5
        di = 0
        for kt in range(4):
            for scg in range(8):
                b = di % NBUF
                di += 1
                dbuf, k_dbuf = dbufs[b]
                abuf, k_abuf = abufs[b]
                sc.dma("sp", lambda e, o=dbuf, i=dft_d[kt, scg]: e.dma_start(out=o, in_=i),
                       writes=[("dbuf", b)] + list(k_dbuf))
                a_src = aful[l][scg % 4].ap()[(scg // 4) * 512:(scg // 4 + 1) * 512, :].rearrange("(sci p) n -> p sci n", p=128)
                sc.dma("sp", lambda e, o=abuf, i=a_src: e.dma_start(out=o, in_=i),
                       reads=[("aful", l, scg % 4)], writes=[("abuf", b)] + list(k_abuf))
                d4 = dbuf.rearrange("p (sci cs k) -> p sci cs k", cs=2, k=512)
                a5 = abuf.rearrange("p sci (g cs m) -> p sci g cs m", cs=2, m=128)

                def f_f(e, scg=scg, d4=d4, a5=a5):
                    ins = None
                    for sci in range(4):
                        for g in range(4):
                            for cs in range(2):
                                first = (scg == 0 and sci == 0 and cs == 0)
                                last = (scg == 7 and sci == 3 and cs == 1)
                                ins = e.matmul(bank(g), lhsT=a5[:, sci, g, cs, :], rhs=d4[:, sci, cs, :], start=first, stop=last)
                    return ins
                sc.op("pe", f_f, reads=[("dbuf", b), ("abuf", b)] + list(k_dbuf) + list(k_abuf), writes=[bk(0), bk(1), bk(2), bk(3)])
            for g in range(4):
                pb = 4 + (g % 2)
                zs = g % 2

                def f_zf(e, g=g, kt=kt, pb=pb):
                    ins = None
                    for k in range(8):
                        ins = e.matmul(bank(pb), lhsT=wst[:, 0, k, g * 128:(g + 1) * 128], rhs=hT[:, k, kt * 512:(kt + 1) * 512], start=(k == 0), stop=(k == 7))
                    return ins
                sc.op("pe", f_zf, reads=[("hT", kt * 4 + j) for j in range(4)] + wst0_keys, writes=[bk(pb)])
                sc.op("act", lambda e, o=szf[:, zs, :], i=bank(pb): e.activation(out=o, in_=i, func=AF.Silu),
                      reads=[bk(pb)], writes=[("szf", zs)] + list(k_szf))
                sc.op("dve", lambda e, o=fgT[:, g, kt * 512:(kt + 1) * 512], i=bank(g), z=szf[:, zs, :]:
                      e.scalar_tensor_tensor(out=o, in0=i, scalar=fscale, in1=z, op0=ALU.mult, op1=ALU.mult),
                      reads=[bk(g), ("szf", zs)] + list(k_szf), writes=[("fgT", g, kt)])

        chk("C")
        T.reset()
        AR.reset()
        wat_sb, k_wat = AR.get([128, 4, D], BF16)
        wfo_sb, k_wfo = AR.get([128, 4, D], BF16)
        wo_sb, k_wo = AR.get([128, 8, D], BF16)
        wat_v = w_attn[l].rearrange("(k p) n -> p k n", p=128)
        wfo_v = w_four[l].rearrange("(k p) n -> p k n", p=128)
        wo_v = w_out[l].rearrange("(k p) n -> p k n", p=128)
        for k in range(4):
            load_w(wat_sb[:, k, :], wat_v[:, k, :], D, [("wat", k)] + list(k_wat), eng="dve")
            load_w(wfo_sb[:, k, :], wfo_v[:, k, :], D, [("wfo", k)] + list(k_wfo), eng="dve")
        sg_slots = [T.get([128, 512], F32) for _ in range(4)]
        t1_slots = [T.get([128, 512], F32) for _ in range(2)]
        mT = qT
        for j in range(8):
            ws = j % 2
            wk_all = [("wst", ws, k) for k in range(8)]
            load_w(wst[:, ws, :, 0:128], wv[:, :, C_GA + j * 128:C_GA + (j + 1) * 128], 1024, wk_all, inner=128)
            load_w(wst[:, ws, :, 128:256], wv[:, :, C_GF + j * 128:C_GF + (j + 1) * 128], 1024, wk_all, inner=128)
            wkeys = [("wst", ws, k) for k in range(8)]
            load_w(wo_sb[:, j, :], wo_v[:, j, :], D, [("wo", j)] + list(k_wo), eng="dve")
            for tt in range(4):
                st_ = (j * 4 + tt) % 2
                b0 = 4 * st_

                def f_g(e, ws=ws, tt=tt, b0=b0, j=j):
                    ins = None
                    for a in range(2):
                        for k in range(8):
                            ins = e.matmul(bank(b0 + a), lhsT=wst[:, ws, k, a * 128:(a + 1) * 128], rhs=hT[:, k, tt * 512:(tt + 1) * 512], start=(k == 0), stop=(k == 7))
                    for k in range(4):
                        ins = e.matmul(bank(b0 + 2), lhsT=wat_sb[:, k, j * 128:(j + 1) * 128], rhs=ogT[:, k, tt * 512:(tt + 1) * 512], start=(k == 0), stop=(k == 3))
                    for k in range(4):
                        ins = e.matmul(bank(b0 + 3), lhsT=wfo_sb[:, k, j * 128:(j + 1) * 128], rhs=fgT[:, k, tt * 512:(tt + 1) * 512], start=(k == 0), stop=(k == 3))
                    return ins
                sc.op("pe", f_g,
                      reads=wkeys + [("hT", tt * 4 + i) for i in range(4)] + [("wat", k) for k in range(4)] + [("wfo", k) for k in range(4)]
                      + list(k_wat) + list(k_wfo) + [("ogT", k, tt) for k in range(4)] + [("fgT", k, tt) for k in range(4)],
                      writes=[bk(b0 + i) for i in range(4)] + [("qT", tt * 4 + i) for i in range(0)])
                sga, k_sga = sg_slots[st_ * 2]
                sgf, k_sgf = sg_slots[st_ * 2 + 1]
                t1s, k_t1s = t1_slots[st_]
                for a, (sgx, k_sgx) in enumerate(((sga, k_sga), (sgf, k_sgf))):
                    sc.op("act", lambda e, o=sgx, i=bank(b0 + a), b_=bm[:, a * 8 + j:a * 8 + j + 1]: e.activation(out=o, in_=i, func=AF.Sigmoid, bias=b_),
                          reads=[bk(b0 + a), "bm"], writes=list(k_sgx))
                sc.op("dve", lambda e, o=t1s, i=bank(b0 + 2), z=sga: e.tensor_tensor(out=o, in0=i, in1=z, op=ALU.mult),
                      reads=[bk(b0 + 2)] + list(k_sga), writes=list(k_t1s))
                sc.op("dve", lambda e, o=sgf, i=bank(b0 + 3), z=sgf: e.tensor_tensor(out=o, in0=i, in1=z, op=ALU.mult),
                      reads=[bk(b0 + 3)] + list(k_sgf), writes=list(k_sgf))
                sc.op("dve", lambda e, o=mT[:, j, tt * 512:(tt + 1) * 512], i=t1s, z=sgf: e.tensor_tensor(out=o, in0=i, in1=z, op=ALU.add),
                      reads=list(k_sgf) + list(k_t1s) + [("qT", tt * 4 + i) for i in range(4)],
                      writes=[("mT", j, tt)] + [("qT", tt * 4 + i) for i in range(4)])
        xo, k_xo = T.get([128, 2, D], F32)
        load_x(0)
        for t in range(NT):
            xs_ = xt[:, t % 2, :]
            kx = ("xt", t % 2)
            if t + 1 < NT:
                load_x(t + 1)
            ps2 = psQ if t % 2 == 0 else psK
            pkeys = [bk(0), bk(1)] if t % 2 == 0 else [bk(2), bk(3)]

            def f_o(e, t=t, ps2=ps2):
                ins = None
                for hf in range(2):
                    for k in range(8):
                        ins = e.matmul(ps2[:, hf * 512:(hf + 1) * 512], lhsT=mT[:, k, t * 128:(t + 1) * 128], rhs=wo_sb[:, k, hf * 512:(hf + 1) * 512], start=(k == 0), stop=(k == 7))
                return ins
            sc.op("pe", f_o, reads=[("mT", k, t // 4) for k in range(8)] + [("qT", t)] + [("wo", k) for k in range(8)] + list(k_wo), writes=pkeys)
            xs2 = t % 2
            sc.op("dve", lambda e, o=xo[:, xs2, :], i=ps2, z=xs_: e.tensor_tensor(out=o, in0=i, in1=z, op=ALU.add),
                  reads=pkeys + [kx] + list(k_xo), writes=[("xo", xs2)])
            sc.dma("sp", lambda e, o=xd_v[:, t, :], i=xo[:, xs2, :]: e.dma_start(out=o, in_=i),
                   reads=[("xo", xs2)] + list(k_xo), writes=[("xdst", l, t)])
        if DEBUG and l == 0:
            sc.dma("sp", lambda e: e.dma_start(out=dbg_og, in_=ogT), reads=[("ogT", a, b) for a in range(4) for b in range(4)], writes=["dbg_og"])
            sc.dma("sp", lambda e: e.dma_start(out=dbg_fg, in_=fgT), reads=[("fgT", a, b) for a in range(4) for b in range(4)], writes=["dbg_fg"])
            sc.dma("sp", lambda e: e.dma_start(out=dbg_m, in_=qT), reads=[("mT", a, b) for a in range(8) for b in range(4)] + [("qT", t) for t in range(NT)], writes=["dbg_m"])
    except _Stop:
        pass
    if DEBUG:
        sc.final_wait("sp", ["dbg_og", "dbg_fg", "dbg_m"])
    sc.final_wait("sp", [("xdst", L - 1, t) for t in range(NT)])

    with nc.Block() as block:
        @block.tensor
        def _(e):
            sc.replay("pe", e)

        @block.scalar
        def _(e):
            sc.replay("act", e)

        @block.vector
        def _(e):
            sc.replay("dve", e)

        @block.gpsimd
        def _(e):
            sc.replay("pool", e)

        @block.sync
        def _(e):
            sc.replay("sp", e)
    return nc


_CACHE = {}


def _tables(half):
    key = ("tab", half)
    if key in _CACHE:
        return _CACHE[key]
    hd = 16
    inv_freq = (10000.0 ** (-np.arange(hd, dtype=np.float32) / hd)).astype(np.float32)
    pos = (half * NTOK + np.arange(NTOK, dtype=np.float32)).astype(np.float32)
    ang = (pos[:, None] * inv_freq[None, :]).astype(np.float32)
    cs = np.concatenate([np.cos(ang), np.sin(ang)], axis=1).astype(np.float32)
    rope = np.ascontiguousarray(cs.reshape(NT, 128, 32).transpose(1, 0, 2))
    s_idx = np.arange(S, dtype=np.int64).reshape(8, 4, 128)
    k_idx = (half * NTOK + np.arange(NTOK, dtype=np.int64)).reshape(4, 512)
    prod = (s_idx[None, :, :, :, None] * k_idx[:, None, None, None, :]) % S
    th = (2.0 * np.pi / S) * prod.astype(np.float64)
    c = np.cos(th)
    sn = -np.sin(th)
    tab = np.stack([c, sn], axis=4)
    tab = tab.transpose(0, 1, 3, 2, 4, 5)
    dft = np.ascontiguousarray(tab.reshape(4, 8, 128, 4096)).astype(ml_dtypes.bfloat16)
    _CACHE[key] = (rope, dft)
    return rope, dft


def _consts():
    if "c" in _CACHE:
        return _CACHE["c"]
    c_i = np.arange(128, dtype=np.int64)
    th = (2.0 * np.pi / 128) * ((c_i[:, None] * c_i[None, :]) % 128).astype(np.float64)
    ccsc = np.concatenate([np.cos(th), np.sin(th)], axis=1).astype(ml_dtypes.bfloat16)
    ib = np.eye(128, dtype=np.float32).astype(ml_dtypes.bfloat16)
    i_f = np.eye(128, dtype=np.float32)
    _CACHE["c"] = (ccsc, ib, i_f)
    return _CACHE["c"]


STOP = None
DEBUG = False


def _get_nc(L):
    key = ("nc", L, STOP)
    if key not in _CACHE:
        _CACHE[key] = build_program(L, STOP)
    return _CACHE[key]


def _layer_params(norm_g, q_latent_g, kv_latent_g, q_head_g, k_head_g, b_merge, ls):
    L = len(ls)
    ng = np.stack([np.ascontiguousarray(norm_g[l].reshape(8, 128).T) for l in ls])
    gl = np.stack([np.ascontiguousarray(np.concatenate([q_latent_g[l].reshape(2, 128), kv_latent_g[l].reshape(2, 128)], 0).T) for l in ls])
    gqr = np.stack([np.ascontiguousarray(np.broadcast_to(q_head_g[l][None, :], (128, DK))) for l in ls])
    gkr = np.stack([np.ascontiguousarray(np.broadcast_to(k_head_g[l][None, :], (128, DK))) for l in ls])
    bmt = np.stack([np.ascontiguousarray(b_merge[l].reshape(2, 8, 128).transpose(2, 0, 1).reshape(128, 16)) for l in ls])
    return ng.astype(np.float32), gl.astype(np.float32), gqr.astype(np.float32), gkr.astype(np.float32), bmt.astype(np.float32)


FUSED_LAYERS = 4


def kernel(x, norm_g, w_in, q_latent_g, kv_latent_g, w_uq, w_ukv, q_head_g, k_head_g,
           w_attn_proj, w_fourier_proj, b_merge, w_out):
    f = lambda a: np.ascontiguousarray(np.asarray(a, dtype=np.float32))
    x, norm_g, w_in, q_latent_g, kv_latent_g = f(x), f(norm_g), f(w_in), f(q_latent_g), f(kv_latent_g)
    w_uq, w_ukv, q_head_g, k_head_g = f(w_uq), f(w_ukv), f(q_head_g), f(k_head_g)
    w_attn_proj, w_fourier_proj, b_merge, w_out = f(w_attn_proj), f(w_fourier_proj), f(b_merge), f(w_out)
    depth = w_in.shape[0]
    ccsc, ib, i_f = _consts()
    cur = [np.ascontiguousarray(x[c // 2, (c % 2) * NTOK:(c % 2 + 1) * NTOK, :]) for c in range(8)]
    step = FUSED_LAYERS
    for l0 in range(0, depth, step):
        ls = list(range(l0, min(l0 + step, depth)))
        nc = _get_nc(len(ls))
        ng, gl, gqr, gkr, bmt = _layer_params(norm_g, q_latent_g, kv_latent_g, q_head_g, k_head_g, b_merge, ls)
        sl = slice(ls[0], ls[-1] + 1)
        in_maps = []
        for c in range(8):
            rope, dft = _tables(c % 2)
            in_maps.append({
                "x": cur[c], "w_in": w_in[sl], "w_uq": w_uq[sl], "w_ukv": w_ukv[sl],
                "w_attn": w_attn_proj[sl], "w_four": w_fourier_proj[sl], "w_out": w_out[sl],
                "norm_gT": ng, "glatT": gl, "gq_rep": gqr, "gk_rep": gkr, "bmT": bmt,
                "rope_cs": rope, "dft": dft, "ccsc": ccsc, "ident_bf": ib, "ident_f": i_f,
            })
        res = run_bass_kernel_spmd(nc, in_maps, core_ids=list(range(8)))
        cur = [np.asarray(res.results[c]["y"], dtype=np.float32) for c in range(8)]
        if DEBUG:
            _CACHE["dbg"] = [{k: np.asarray(res.results[c][k]) for k in ("dbg_og", "dbg_fg", "dbg_m")} for c in range(8)]
    out = np.empty((4, S, D), dtype=np.float32)
    for c in range(8):
        out[c // 2, (c % 2) * NTOK:(c % 2 + 1) * NTOK, :] = cur[c]
    return out
```

```python
import numpy as np
import ml_dtypes
import concourse.bass as bass
import concourse.mybir as mybir
from concourse.bass_utils import run_bass_kernel_spmd

F32 = mybir.dt.float32
BF16 = mybir.dt.bfloat16
AF = mybir.ActivationFunctionType
ALU = mybir.AluOpType
AX = mybir.AxisListType

D = 1024
S = 4096
NTOK = 2048
NT = NTOK // 128
H = 8
DK = 96
DV = 64
INW = 4128
EPS = 1e-6
C_CQ, C_KPE, C_ZA, C_UF, C_ZF, C_GA, C_GF = 0, 512, 544, 1056, 1568, 2080, 3104
ENGS = ("pe", "act", "dve", "pool", "sp")
NDMA = 8


class Sched:
    def __init__(self, nc):
        self.nc = nc
        self.streams = {e: [] for e in ENGS}
        self.sems = {}
        self.cur = {}
        self.cnt = {}
        self.waited = {e: {} for e in ENGS}
        self.lastw = {}
        self.readers = {}
        self.dma_gen = {q: [0] * NDMA for q in ("sp", "pool")}
        self.dma_rr = {q: 0 for q in ("sp", "pool")}
        self.epoch = -1
        self.ncc = 0
        for q in ("sp", "pool"):
            for i in range(NDMA):
                self._sem(f"dma_{q}_{i}")
        self.new_epoch()

    def _sem(self, name):
        self.sems[name] = self.nc.alloc_semaphore(name)
        return name

    def new_epoch(self):
        self.epoch += 1
        for e in ("pe", "act", "dve", "pool"):
            self.cur[e] = self._sem(f"c_{e}_{self.epoch}")
            self.cnt[e] = 0

    def _deps(self, reads, writes):
        deps = {}

        def add(tok):
            if tok is not None:
                n, v = tok
                if v > deps.get(n, 0):
                    deps[n] = v
        for k in reads:
            add(self.lastw.get(k))
        for k in writes:
            add(self.lastw.get(k))
            for n, v in self.readers.get(k, {}).items():
                add((n, v))
        return deps

    def _emit(self, eng, deps, fn, semname, inc):
        waits = []
        w = self.waited[eng]
        for n, v in deps.items():
            if v > w.get(n, 0):
                waits.append((n, v))
                w[n] = v
        self.streams[eng].append((waits, fn, semname, inc))

    def _mark(self, tok, reads, writes):
        n, v = tok
        for k in reads:
            self.readers.setdefault(k, {})[n] = v
        for k in writes:
            self.lastw[k] = tok
            self.readers[k] = {}

    def op(self, eng, fn, reads=(), writes=()):
        deps = self._deps(reads, writes)
        self.cnt[eng] += 1
        tok = (self.cur[eng], self.cnt[eng])
        self._emit(eng, deps, fn, tok[0], 1)
        self._mark(tok, reads, writes)

    def dma(self, q, fn, reads=(), writes=()):
        deps = self._deps(reads, writes)
        i = self.dma_rr[q]
        self.dma_rr[q] = (i + 1) % NDMA
        name = f"dma_{q}_{i}"
        gen = self.dma_gen[q][i]
        if gen > 0:
            deps[name] = max(deps.get(name, 0), 16 * gen)
        self.dma_gen[q][i] = gen + 1
        tok = (name, 16 * (gen + 1))
        self._emit(q, deps, fn, name, 16)
        self._mark(tok, reads, writes)

    def collective(self, slot, fn, reads=(), writes=()):
        deps = self._deps(reads, writes)
        name = f"cc_{slot}"
        if name not in self.sems:
            self._sem(name)
            self.cc_gen = getattr(self, "cc_gen", {})
            self.cc_gen[name] = 0
        self.cc_gen[name] += 1
        tok = (name, self.cc_gen[name])
        self._emit("pool", deps, fn, name, 1)
        self._mark(tok, reads, writes)

    def final_wait(self, eng, keys):
        deps = self._deps(keys, ())
        self._emit(eng, deps, None, None, 0)

    def replay(self, eng, e):
        for waits, fn, semname, inc in self.streams[eng]:
            for n, v in waits:
                e.wait_ge(self.sems[n], v)
            if fn is not None:
                ins = fn(e)
                ins.then_inc(self.sems[semname], inc)


class _Stop(Exception):
    pass


def build_program(L, stop=None):
    nc = bass.Bass("TRN2", target_bir_lowering=False)
    sc = Sched(nc)

    def din(name, shape, dt=F32):
        return nc.dram_tensor(name, shape, dt, kind="ExternalInput").ap()

    x_in = din("x", [NTOK, D])
    w_in = din("w_in", [L, D, INW])
    w_uq = din("w_uq", [L, 256, H * DK])
    w_ukv = din("w_ukv", [L, 256, H * 128])
    w_attn = din("w_attn", [L, 512, D])
    w_four = din("w_four", [L, 512, D])
    w_out = din("w_out", [L, D, D])
    normg_d = din("norm_gT", [L, 128, 8])
    glat_d = din("glatT", [L, 128, 4])
    gq_d = din("gq_rep", [L, 128, DK])
    gk_d = din("gk_rep", [L, 128, DK])
    bm_d = din("bmT", [L, 128, 16])
    rope_d = din("rope_cs", [128, NT, 32])
    dft_d = din("dft", [4, 8, 128, 4096], BF16)
    ccsc_d = din("ccsc", [128, 256], BF16)
    identb_d = din("ident_bf", [128, 128], BF16)
    identf_d = din("ident_f", [128, 128])
    y_out = nc.dram_tensor("y", [NTOK, D], F32, kind="ExternalOutput").ap()
    if DEBUG:
        dbg_og = nc.dram_tensor("dbg_og", [128, 4, NTOK], BF16, kind="ExternalOutput").ap()
        dbg_fg = nc.dram_tensor("dbg_fg", [128, 4, NTOK], BF16, kind="ExternalOutput").ap()
        dbg_m = nc.dram_tensor("dbg_m", [128, 8, NTOK], BF16, kind="ExternalOutput").ap()

    kloc = [[nc.dram_tensor(f"kloc{l}_{j}", [H * DK, 512], BF16) for j in range(4)] for l in range(L)]
    kful = [[nc.dram_tensor(f"kful{l}_{j}", [2 * H * DK, 512], BF16) for j in range(4)] for l in range(L)]
    vloc = [[nc.dram_tensor(f"vloc{l}_{j}", [512, H * 65], BF16) for j in range(4)] for l in range(L)]
    vful = [[nc.dram_tensor(f"vful{l}_{j}", [2 * 512, H * 65], BF16) for j in range(4)] for l in range(L)]
    aloc = [[nc.dram_tensor(f"aloc{l}_{j}", [512, 1024], BF16) for j in range(4)] for l in range(L)]
    aful = [[nc.dram_tensor(f"aful{l}_{j}", [2 * 512, 1024], BF16) for j in range(4)] for l in range(L)]
    xbuf = [nc.dram_tensor(f"xbuf{l}", [NTOK, D], F32) for l in range(max(L - 1, 1))]

    def sb(name, shape, dt):
        return nc.alloc_sbuf_tensor(name, shape, dt).ap()

    hT = sb("hT", [128, 8, NTOK], BF16)
    qT = sb("qT", [128, 8, NTOK], BF16)
    ogT = sb("ogT", [128, 4, NTOK], BF16)
    fgT = sb("fgT", [128, 4, NTOK], BF16)
    arena = sb("arena", [128, 16384], BF16)
    wst = sb("wst", [128, 2, 8, 512], BF16)
    stage = sb("stage", [128, 2, 1024], F32)
    xt = sb("xt", [128, 2, D], F32)
    tmp = sb("tmp", [128, 18432], BF16)
    identb = sb("identb", [128, 128], BF16)
    identf = sb("identf", [128, 128], F32)
    rope = sb("rope", [128, NT, 32], F32)
    ccsc = sb("ccsc_sb", [128, 256], BF16)
    normg = sb("normg", [128, 8], F32)
    glat = sb("glat", [128, 4], F32)
    gq = sb("gq", [128, DK], F32)
    gk = sb("gk", [128, DK], F32)
    bm = sb("bm", [128, 16], F32)
    stats = sb("stats", [128, 64], F32)
    eps_t = sb("eps_t", [128, 1], F32)

    psQ = nc.alloc_psum_tensor("psQ", [128, 1024], F32).ap()
    psK = nc.alloc_psum_tensor("psK", [128, 1024], F32).ap()
    psS = [nc.alloc_psum_tensor(f"psS{i}", [128, 1024], F32).ap() for i in range(2)]

    def bank(i):
        if i < 2:
            return psQ[:, i * 512:(i + 1) * 512]
        if i < 4:
            return psK[:, (i - 2) * 512:(i - 1) * 512]
        return psS[(i - 4) // 2][:, ((i - 4) % 2) * 512:((i - 4) % 2 + 1) * 512]

    def bk(i):
        return ("ps", i)

    class Tmp:
        def __init__(self, base_ap, prefix, nbytes):
            self.base, self.prefix, self.nbytes = base_ap, prefix, nbytes
            self.off = 0

        def reset(self):
            self.off = 0

        def get(self, shape, dt):
            es = 4 if dt == F32 else 2
            n = int(np.prod(shape[1:]))
            nb = (n * es + 63) // 64 * 64
            a, b = self.off, self.off + nb
            assert b <= self.nbytes, (self.prefix, b)
            self.off = b
            ap = self.base[:, a // 2:(a + n * es) // 2]
            if dt == F32:
                ap = ap.bitcast(F32)
            if len(shape) == 3:
                ap = ap.rearrange("p (a b) -> p a b", b=shape[2])
            ap = ap[0:shape[0]]
            keys = tuple((self.prefix, i) for i in range(a // 1024, (b - 1) // 1024 + 1))
            return ap, keys

    T = Tmp(tmp, "tmp", 36864)
    AR = Tmp(arena, "arena", 32768)

    stage_rr = [0]

    def load_w(dst, src, n, dst_keys, inner=None, eng="pool"):
        s = stage_rr[0]
        stage_rr[0] ^= 1
        st_ap = stage[:, s, 0:n]
        if inner is not None:
            st_ap = st_ap.rearrange("p (a b) -> p a b", b=inner)
        sc.dma("sp", lambda e, o=st_ap, i=src: e.dma_start(out=o, in_=i), writes=[("stage", s)])
        sc.op(eng, lambda e, o=dst, i=st_ap: e.tensor_copy(out=o, in_=i),
              reads=[("stage", s)], writes=dst_keys)

    def load_small(dst, src, key):
        sc.dma("sp", lambda e, o=dst, i=src: e.dma_start(out=o, in_=i), writes=[key])

    sc.op("dve", lambda e: e.memset(eps_t, EPS), writes=["eps_t"])
    load_small(identb, identb_d, "identb")
    load_small(identf, identf_d, "identf")
    load_small(rope, rope_d, "rope")
    load_small(ccsc, ccsc_d, "ccsc")

    def w_in_v(l):
        return w_in[l].rearrange("(k p) n -> p k n", p=128)

    def chk(name):
        if stop == name:
            sc.dma("sp", lambda e: e.dma_start(out=y_out, in_=x_in), writes=[("xdst", L - 1, t) for t in range(NT)])
            raise _Stop()

    try:
      chk("init")
      for l in range(L):
        if l > 0:
            sc.new_epoch()
        x_src = x_in if l == 0 else xbuf[l - 1].ap()
        x_dst = y_out if l == L - 1 else xbuf[l].ap()
        xs_v = x_src.rearrange("(t p) d -> p t d", p=128)
        xd_v = x_dst.rearrange("(t p) d -> p t d", p=128)

        load_small(normg, normg_d[l], "normg")
        load_small(glat, glat_d[l], "glat")
        load_small(gq, gq_d[l], "gq")
        load_small(gk, gk_d[l], "gk")
        load_small(bm, bm_d[l], "bm")
        wv = w_in_v(l)
        AR.reset()
        wA, k_wA = AR.get([128, 8, 544], BF16)
        wuq_sb, k_wuq = AR.get([128, 2, H * DK], BF16)
        wukv_sb, k_wukv = AR.get([128, 2, H * 128], BF16)
        uT, k_uT = AR.get([128, 4, 512], BF16)
        atile, k_atile = AR.get([128, 2, 1024], BF16)
        for k in range(8):
            load_w(wA[:, k, :], wv[:, k, 0:544], 544, [("wA", k)] + list(k_wA), eng="dve")
        wuq_v = w_uq[l].rearrange("(k p) n -> p k n", p=128)
        wukv_v = w_ukv[l].rearrange("(k p) n -> p k n", p=128)
        for k in range(2):
            load_w(wuq_sb[:, k, :], wuq_v[:, k, :], H * DK, [("wuq", k)] + list(k_wuq), eng="dve")
            load_w(wukv_sb[:, k, :], wukv_v[:, k, :], H * 128, [("wukv", k)] + list(k_wukv), eng="dve")
        wA_keys = [("wA", k) for k in range(8)] + list(k_wA)

        groups = [[0, 1], [2, 3], [4, 5], [6, 7]]

        def exchange1(nm, loc_l, ful_l, slot0, j):
            sc.collective(slot0 + j, lambda e, i=loc_l[j], o=ful_l[j]: e.collective_compute(
                "AllGather", ALU.bypass, replica_groups=groups, ins=[i.ap().opt()], outs=[o.ap().opt()]),
                reads=[(nm + "loc", l, j * 4 + i) for i in range(4)], writes=[(nm + "ful", l, j)])

        T.reset()
        xsb, k_xsb = T.get([128, D], BF16)
        junk, k_junk = T.get([128, D], BF16)
        junk2, k_junk2 = T.get([128, 32], BF16)
        cn, k_cn = T.get([128, 512], BF16)
        cnT, k_cnT = T.get([128, 4, 128], BF16)
        qsq, k_qsq = T.get([128, 768], F32)
        qn, k_qn = T.get([128, 768], F32)
        qg, k_qg = T.get([128, 768], F32)
        qb, k_qb = T.get([128, 768], BF16)
        ksq, k_ksq = T.get([128, 512], F32)
        knn, k_knn = T.get([128, 512], F32)
        kb, k_kb = T.get([128, 768], BF16)
        kpeg, k_kpeg = T.get([128, 32], F32)
        kpeh, k_kpeh = T.get([128, 256], F32)
        rtq, k_rtq = T.get([128, 4, 128], F32)
        rtk, k_rtk = T.get([128, 4, 128], F32)
        vb, k_vb = T.get([128, 2, H * 65], BF16)
        ktile, k_ktile = T.get([128, 2, 1024], BF16)

        vb4 = vb.rearrange("p s (h c) -> p s h c", c=65)
        sc.op("dve", lambda e, o=vb: e.memset(o, 1.0), writes=k_vb)

        def load_x(t):
            rd = [("xdst", l - 1, t)] if l > 0 else []
            sc.dma("sp", lambda e, o=xt[:, t % 2, :], i=xs_v[:, t, :]: e.dma_start(out=o, in_=i), reads=rd, writes=[("xt", t % 2)])

        qps = psQ[:, 0:768]
        q3 = qps.rearrange("p (h d) -> p h d", d=DK)
        kv3 = psK.rearrange("p (h d) -> p h d", d=128)
        qn3 = qn.rearrange("p (h d) -> p h d", d=DK)
        qg3 = qg.rearrange("p (h d) -> p h d", d=DK)
        qb3 = qb.rearrange("p (h d) -> p h d", d=DK)
        ksq3 = ksq.rearrange("p (h d) -> p h d", d=64)
        knn3 = knn.rearrange("p (h d) -> p h d", d=64)
        kb3 = kb.rearrange("p (h d) -> p h d", d=DK)
        kpeh3 = kpeh.rearrange("p (h d) -> p h d", d=32)
        ST = lambda a, b_=None: stats[:, a:(a + 1 if b_ is None else b_)]

        def tr_heads(e, dst, src):
            ins = None
            for h in range(H):
                ins = e.transpose(out=dst[0:DK, h, :], in_=src[:, h, :], identity=identb)
            return ins

        def chain_S1(t):
            ops = []
            xs_ = xt[:, t % 2, :]
            kx = ("xt", t % 2)
            ops.append(lambda: sc.op("act", lambda e: e.activation(out=junk, in_=xs_, func=AF.Square, accum_out=ST(0)),
                                     reads=[kx], writes=list(k_junk) + [("st", 0)]))
            ops.append(lambda: sc.op("act", lambda e: e.activation(out=ST(2), in_=ST(0), func=AF.Sqrt, scale=1.0 / D, bias=eps_t),
                                     reads=[("st", 0), "eps_t"], writes=[("st", 2)]))
            ops.append(lambda: sc.op("dve", lambda e: e.reciprocal(out=ST(3), in_=ST(2)), reads=[("st", 2)], writes=[("st", 3)]))
            ops.append(lambda: sc.op("act", lambda e: e.activation(out=xsb, in_=xs_, func=AF.Copy, scale=ST(3)),
                                     reads=[kx, ("st", 3)], writes=k_xsb))
            pt = bank(6).bitcast(BF16).rearrange("p (k j) -> p k j", j=128)

            def f_tr(e):
                ins = None
                for k in range(8):
                    ins = e.transpose(out=pt[:, k, :], in_=xsb[:, k * 128:(k + 1) * 128], identity=identb)
                return ins
            ops.append(lambda: sc.op("pe", f_tr, reads=list(k_xsb) + ["identb"], writes=[bk(6)]))
            hT_t = hT[:, :, t * 128:(t + 1) * 128]
            ops.append(lambda: sc.op("dve", lambda e: e.tensor_tensor(out=hT_t, in0=pt, in1=normg.unsqueeze(2).to_broadcast([128, 8, 128]), op=ALU.mult),
                                     reads=[bk(6), "normg"], writes=[("hT", t)]))

            def f_cq(e):
                ins = None
                for k in range(8):
                    ins = e.matmul(bank(4), lhsT=hT[:, k, t * 128:(t + 1) * 128], rhs=wA[:, k, 0:512], start=(k == 0), stop=(k == 7))
                for k in range(8):
                    ins = e.matmul(bank(5)[:, 0:32], lhsT=hT[:, k, t * 128:(t + 1) * 128], rhs=wA[:, k, 512:544], start=(k == 0), stop=(k == 7))
                return ins
            ops.append(lambda: sc.op("pe", f_cq, reads=[("hT", t)] + wA_keys, writes=[bk(4), bk(5)]))
            for j in range(2):
                ops.append(lambda j=j: sc.op("act", lambda e: e.activation(out=junk[:, 0:256], in_=bank(4)[:, j * 256:(j + 1) * 256], func=AF.Square, accum_out=ST(4 + j)),
                                             reads=[bk(4)], writes=list(k_junk) + [("st", 4 + j)]))
            ops.append(lambda: sc.op("act", lambda e: e.activation(out=ST(8, 10), in_=ST(4, 6), func=AF.Sqrt, scale=1.0 / 256, bias=eps_t),
                                     reads=[("st", 4), ("st", 5), "eps_t"], writes=[("st", 8)]))
            ops.append(lambda: sc.op("dve", lambda e: e.reciprocal(out=ST(10, 12), in_=ST(8, 10)), reads=[("st", 8)], writes=[("st", 10)]))
            for j in range(2):
                ops.append(lambda j=j: sc.op("dve", lambda e: e.tensor_scalar(out=cn[:, j * 256:(j + 1) * 256], in0=bank(4)[:, j * 256:(j + 1) * 256], scalar1=ST(10 + j), scalar2=None, op0=ALU.mult),
                                             reads=[bk(4), ("st", 10)], writes=k_cn))
            pc = bank(7).bitcast(BF16)[:, 0:512].rearrange("p (k j) -> p k j", j=128)

            def f_trc(e):
                ins = None
                for k in range(4):
                    ins = e.transpose(out=pc[:, k, :], in_=cn[:, k * 128:(k + 1) * 128], identity=identb)
                return ins
            ops.append(lambda: sc.op("pe", f_trc, reads=list(k_cn) + ["identb"], writes=[bk(7)]))
            ops.append(lambda: sc.op("dve", lambda e: e.tensor_tensor(out=cnT, in0=pc, in1=glat.unsqueeze(2).to_broadcast([128, 4, 128]), op=ALU.mult),
                                     reads=[bk(7), "glat"], writes=k_cnT))

            def f_q(e):
                ins = None
                for k in range(2):
                    ins = e.matmul(bank(0), lhsT=cnT[:, k, :], rhs=wuq_sb[:, k, 0:512], start=(k == 0), stop=(k == 1))
                for k in range(2):
                    ins = e.matmul(bank(1)[:, 0:256], lhsT=cnT[:, k, :], rhs=wuq_sb[:, k, 512:768], start=(k == 0), stop=(k == 1))
                for hf in range(2):
                    for k in range(2):
                        ins = e.matmul(bank(2 + hf), lhsT=cnT[:, 2 + k, :], rhs=wukv_sb[:, k, hf * 512:(hf + 1) * 512], start=(k == 0), stop=(k == 1))
                return ins
            last = lambda: sc.op("pe", f_q, reads=list(k_cnT) + [("wuq", 0), ("wuq", 1), ("wukv", 0), ("wukv", 1)] + list(k_wuq) + list(k_wukv),
                                 writes=[bk(0), bk(1), bk(2), bk(3)])
            return ops, last

        def rope_chain(ops, eng, x1, x2, o1, o2, rt, k_src, k_rt, k_dst, cos_b, sin_b):
            rt4 = rt.rearrange("p a (h d) -> p a h d", d=16)
            ops.append(lambda: sc.op(eng, lambda e: e.tensor_tensor(out=rt4[:, 0], in0=x1, in1=cos_b, op=ALU.mult), reads=list(k_src) + ["rope"], writes=k_rt))
            ops.append(lambda: sc.op(eng, lambda e: e.tensor_tensor(out=rt4[:, 1], in0=x2, in1=sin_b, op=ALU.mult), reads=list(k_src) + ["rope"], writes=k_rt))
            ops.append(lambda: sc.op(eng, lambda e: e.tensor_tensor(out=rt4[:, 2], in0=x2, in1=cos_b, op=ALU.mult), reads=list(k_src) + ["rope"], writes=k_rt))
            ops.append(lambda: sc.op(eng, lambda e: e.tensor_tensor(out=rt4[:, 3], in0=x1, in1=sin_b, op=ALU.mult), reads=list(k_src) + ["rope"], writes=k_rt))
            ops.append(lambda: sc.op(eng, lambda e: e.tensor_tensor(out=o1, in0=rt4[:, 0], in1=rt4[:, 1], op=ALU.subtract), reads=k_rt, writes=k_dst))
            ops.append(lambda: sc.op(eng, lambda e: e.tensor_tensor(out=o2, in0=rt4[:, 2], in1=rt4[:, 3], op=ALU.add), reads=k_rt, writes=k_dst))

        def chain_Sq(t):
            ops = []
            cos_b = rope[:, t, 0:16].unsqueeze(1).to_broadcast([128, H, 16])
            sin_b = rope[:, t, 16:32].unsqueeze(1).to_broadcast([128, H, 16])
            ops.append(lambda: sc.op("act", lambda e: e.activation(out=qsq, in_=qps, func=AF.Square), reads=[bk(0), bk(1)], writes=k_qsq))
            ops.append(lambda: sc.op("dve", lambda e: e.tensor_reduce(out=ST(16, 24), in_=qsq.rearrange("p (h d) -> p h d", d=DK), axis=AX.X, op=ALU.add),
                                     reads=k_qsq, writes=[("st", 16)]))
            ops.append(lambda: sc.op("act", lambda e: e.activation(out=ST(32, 40), in_=ST(16, 24), func=AF.Sqrt, scale=1.0 / DK, bias=eps_t),
                                     reads=[("st", 16), "eps_t"], writes=[("st", 32)]))
            ops.append(lambda: sc.op("dve", lambda e: e.reciprocal(out=ST(40, 48), in_=ST(32, 40)), reads=[("st", 32)], writes=[("st", 40)]))
            ops.append(lambda: sc.op("dve", lambda e: e.tensor_tensor(out=qn3, in0=q3, in1=ST(40, 48).unsqueeze(2).to_broadcast([128, H, DK]), op=ALU.mult),
                                     reads=[bk(0), bk(1), ("st", 40)], writes=k_qn))
            ops.append(lambda: sc.op("pool", lambda e: e.tensor_tensor(out=qg3, in0=qn3, in1=gq.unsqueeze(1).to_broadcast([128, H, DK]), op=ALU.mult),
                                     reads=list(k_qn) + ["gq"], writes=k_qg))
            ops.append(lambda: sc.op("pool", lambda e: e.tensor_copy(out=qb3[:, :, 0:64], in_=qg3[:, :, 0:64]), reads=k_qg, writes=k_qb))
            rope_chain(ops, "pool", qg3[:, :, 64:80], qg3[:, :, 80:96], qb3[:, :, 64:80], qb3[:, :, 80:96], rtq, k_qg, k_rtq, k_qb, cos_b, sin_b)
            pq = bank(6).bitcast(BF16).rearrange("p (h j) -> p h j", j=128)
            ops.append(lambda: sc.op("pe", lambda e: tr_heads(e, pq, qb3), reads=list(k_qb) + ["identb"], writes=[bk(6)]))
            ops.append(lambda: sc.op("act", lambda e: e.activation(out=qT[0:DK, :, t * 128:(t + 1) * 128], in_=pq[0:DK], func=AF.Copy),
                                     reads=[bk(6)], writes=[("qT", t)]))
            return ops

        def chain_Sk(t):
            ops = []
            cos_b = rope[:, t, 0:16].unsqueeze(1).to_broadcast([128, H, 16])
            sin_b = rope[:, t, 16:32].unsqueeze(1).to_broadcast([128, H, 16])
            vs = t % 2
            kvs = ("vb", vs)
            rk = ST(56, 64)
            ops.append(lambda: sc.op("act", lambda e: e.activation(out=ksq3, in_=kv3[:, :, 0:64], func=AF.Square), reads=[bk(2), bk(3)], writes=k_ksq))
            ops.append(lambda: sc.op("act", lambda e: e.activation(out=junk2, in_=bank(5)[:, 0:32], func=AF.Square, accum_out=ST(12)),
                                     reads=[bk(5)], writes=list(k_junk2) + [("st", 12)]))
            ops.append(lambda: sc.op("dve", lambda e: e.tensor_tensor(out=kpeg, in0=bank(5)[:, 0:32], in1=gk[:, 64:96], op=ALU.mult),
                                     reads=[bk(5), "gk"], writes=k_kpeg))
            ops.append(lambda: sc.op("act", lambda e: e.activation(out=vb4[:, vs, :, 0:64], in_=kv3[:, :, 64:128], func=AF.Copy),
                                     reads=[bk(2), bk(3)] + list(k_vb), writes=[kvs]))
            ops.append(lambda: sc.op("dve", lambda e: e.tensor_reduce(out=ST(48, 56), in_=ksq3, axis=AX.X, op=ALU.add), reads=k_ksq, writes=[("st", 48)]))
            ops.append(lambda: sc.op("dve", lambda e: e.tensor_scalar(out=ST(48, 56), in0=ST(48, 56), scalar1=ST(12), scalar2=None, op0=ALU.add),
                                     reads=[("st", 48), ("st", 12)], writes=[("st", 48)]))
            ops.append(lambda: sc.op("act", lambda e: e.activation(out=ST(24, 32), in_=ST(48, 56), func=AF.Sqrt, scale=1.0 / DK, bias=eps_t),
                                     reads=[("st", 48), "eps_t"], writes=[("st", 24)]))
            ops.append(lambda: sc.op("dve", lambda e: e.reciprocal(out=ST(56, 64), in_=ST(24, 32)), reads=[("st", 24)], writes=[("st", 56)]))
            ops.append(lambda: sc.op("dve", lambda e: e.tensor_tensor(out=knn3, in0=kv3[:, :, 0:64], in1=rk.unsqueeze(2).to_broadcast([128, H, 64]), op=ALU.mult),
                                     reads=[bk(2), bk(3), ("st", 56)], writes=k_knn))
            vdst = vloc[l][t // 4].ap()[(t % 4) * 128:(t % 4 + 1) * 128, :]
            ops.append(lambda: sc.dma("sp", lambda e: e.dma_start(out=vdst, in_=vb[:, vs, :]),
                                      reads=[kvs] + list(k_vb), writes=[("vloc", l, t)]))
            ops.append(lambda: sc.op("pool", lambda e: e.tensor_tensor(out=kb3[:, :, 0:64], in0=knn3, in1=gk[:, 0:64].unsqueeze(1).to_broadcast([128, H, 64]), op=ALU.mult),
                                     reads=list(k_knn) + ["gk"], writes=k_kb))
            ops.append(lambda: sc.op("dve", lambda e: e.tensor_tensor(out=kpeh3, in0=kpeg.unsqueeze(1).to_broadcast([128, H, 32]), in1=rk.unsqueeze(2).to_broadcast([128, H, 32]), op=ALU.mult),
                                     reads=list(k_kpeg) + [("st", 56)], writes=k_kpeh))
            rope_chain(ops, "dve", kpeh3[:, :, 0:16], kpeh3[:, :, 16:32], kb3[:, :, 64:80], kb3[:, :, 80:96], rtk, k_kpeh, k_rtk, k_kb, cos_b, sin_b)
            pk = bank(7).bitcast(BF16).rearrange("p (h j) -> p h j", j=128)
            ops.append(lambda: sc.op("pe", lambda e: tr_heads(e, pk, kb3), reads=list(k_kb) + ["identb"], writes=[bk(7)]))
            kt_ap = ktile[:, vs, :].rearrange("p (h j) -> p h j", j=128)
            kks = ("ktile", vs)
            ops.append(lambda: sc.op("act", lambda e: e.activation(out=kt_ap[0:DK], in_=pk[0:DK], func=AF.Copy),
                                     reads=[bk(7)] + list(k_ktile), writes=[kks]))
            kdst = kloc[l][t // 4].ap().rearrange("(h d) n -> d h n", d=DK)[:, :, (t % 4) * 128:(t % 4 + 1) * 128]
            ops.append(lambda: sc.dma("sp", lambda e: e.dma_start(out=kdst, in_=kt_ap[0:DK]),
                                      reads=[kks] + list(k_ktile), writes=[("kloc", l, t)]))
            return ops

        def interleave(chains):
            n = max(len(c) for c in chains)
            for r in range(n):
                for c in chains:
                    if r < len(c):
                        c[r]()

        for k in range(8):
            load_w(wst[:, 0, k, :], wv[:, k, C_UF:C_UF + 512], 512, [("wst", 0, k)])
        wst0_keys = [("wst", 0, k) for k in range(8)]

        def a3_u_chain(tt):
            ops = []
            for g in range(4):
                def f_u(e, g=g):
                    ins = None
                    for k in range(8):
                        ins = e.matmul(bank(7), lhsT=wst[:, 0, k, g * 128:(g + 1) * 128], rhs=hT[:, k, tt * 512:(tt + 1) * 512], start=(k == 0), stop=(k == 7))
                    return ins
                ops.append(lambda f_u=f_u: sc.op("pe", f_u, reads=[("hT", tt * 4 + j) for j in range(4)] + wst0_keys, writes=[bk(7)]))
                ops.append(lambda g=g: sc.op("act", lambda e: e.activation(out=uT[:, g, :], in_=bank(7), func=AF.Copy),
                                             reads=[bk(7)], writes=[("uT", g)] + list(k_uT)))
            noop_ = lambda: None

            def fa_pair(j):
                t = tt * 4 + j
                ps2 = psS[1]
                pkeys = [bk(6), bk(7)]
                a_s = t % 2
                adst = aloc[l][t // 4].ap()[(t % 4) * 128:(t % 4 + 1) * 128, :]

                def f_a(e):
                    ins = None
                    for g in range(4):
                        ins = e.matmul(ps2[:, g * 256:(g + 1) * 256], lhsT=uT[:, g, j * 128:(j + 1) * 128], rhs=ccsc, start=True, stop=True)
                    return ins

                def mm():
                    sc.op("pe", f_a, reads=[("uT", g) for g in range(4)] + list(k_uT) + ["ccsc"], writes=pkeys)

                def ev():
                    sc.op("act", lambda e: e.activation(out=atile[:, a_s, :], in_=ps2, func=AF.Copy),
                          reads=pkeys + list(k_atile), writes=[("atile", a_s)])
                    sc.dma("sp", lambda e: e.dma_start(out=adst, in_=atile[:, a_s, :]),
                           reads=[("atile", a_s)] + list(k_atile), writes=[("aloc", l, t)])
                return [mm, ev]

            ops += fa_pair(0) + fa_pair(1) + [noop_, noop_, noop_] + fa_pair(2) + [noop_, noop_, noop_] + fa_pair(3)
            return ops

        def a3_group(tt):
            for j in range(4):
                t = tt * 4 + j
                ps2 = psS[1]
                pkeys = [bk(6), bk(7)]

                def f_a(e, j=j, ps2=ps2):
                    ins = None
                    for g in range(4):
                        ins = e.matmul(ps2[:, g * 256:(g + 1) * 256], lhsT=uT[:, g, j * 128:(j + 1) * 128], rhs=ccsc, start=True, stop=True)
                    return ins
                sc.op("pe", f_a, reads=[("uT", g) for g in range(4)] + list(k_uT) + ["ccsc"], writes=pkeys)
                a_s = t % 2
                sc.op("act", lambda e, o=atile[:, a_s, :], i=ps2: e.activation(out=o, in_=i, func=AF.Copy),
                      reads=pkeys + list(k_atile), writes=[("atile", a_s)])
                adst = aloc[l][t // 4].ap()[(t % 4) * 128:(t % 4 + 1) * 128, :]
                sc.dma("sp", lambda e, o=adst, i=atile[:, a_s, :]: e.dma_start(out=o, in_=i),
                       reads=[("atile", a_s)] + list(k_atile), writes=[("aloc", l, t)])

        load_x(0)
        load_x(1)
        ops1, last1 = chain_S1(0)
        interleave([ops1])
        last1()
        for t in range(NT):
            chains = [chain_Sq(t), chain_Sk(t)]
            nxt = None
            if t + 1 < NT:
                if t + 2 < NT:
                    load_x(t + 2)
                ops1, nxt = chain_S1(t + 1)
                chains = [ops1] + chains
            if t % 4 == 3:
                chains.append(a3_u_chain(t // 4))
            interleave(chains)
            if nxt is not None:
                nxt()
            if (t >= 5 and t % 4 == 1) or t == NT - 1:
                for gidx in ([t // 4 - 1] if t < NT - 1 else [t // 4 - 1, t // 4] if t % 4 == 1 else [t // 4]):
                    exchange1("k", kloc[l], kful[l], 0, gidx)
                    exchange1("v", vloc[l], vful[l], 4, gidx)
                    exchange1("a", aloc[l], aful[l], 8, gidx)

        chk("A")

        chk("X")
        for k in range(8):
            load_w(wst[:, 1, k, :], wv[:, k, C_ZA:C_ZA + 512], 512, [("wst", 1, k)])
        wst1_keys = [("wst", 1, k) for k in range(8)]

        T.reset()
        AR.reset()
        NPT = 4
        pT_slots = [T.get([128, 1024], BF16) for _ in range(NPT)]
        oacc2 = [T.get([128, 512], F32) for _ in range(2)]
        za2, og2 = [], []
        for s_ in range(2):
            za2.append(T.get([128, NT, 128], BF16))
            og2.append(T.get([128, NT, 128], BF16))
        rinv, k_rinv = T.get([128, 8], F32)
        kv_bufs = []
        for s_ in range(2):
            kT_b, k_kT = AR.get([128, S], BF16)
            v_b, k_v = AR.get([128, 32, 65], BF16)
            kv_bufs.append((kT_b, k_kT, v_b, k_v))
        scale = float(DK) ** -0.5
        kfv = [kful[l][j].ap().rearrange("(r h d) n -> r h d n", r=2, h=H) for j in range(4)]
        vfv = [vful[l][j].ap().rearrange("(r i p) (h c) -> p r i h c", p=128, i=4, c=65) for j in range(4)]
        its = [(h, qt, kp) for h in range(H) for qt in range(4) for kp in range(16)]
        LOOK = 2
        SD = [(psK, 2), (psS[0], 4), (psS[1], 6)]

        def kvk(h):
            kT_b, k_kT, v_b, k_v = kv_bufs[h % 2]
            return [("kT", h % 2, i) for i in range(8)] + [("v", h % 2, j) for j in range(8)] + list(k_kT) + list(k_v)

        def kv_load(h):
            kT_b, k_kT, v_b, k_v = kv_bufs[h % 2]
            for r in range(2):
                for j in range(4):
                    c0 = r * NTOK + j * 512
                    first = (r == 0 and j == 0)
                    sc.dma("sp", lambda e, o=kT_b[0:DK, c0:c0 + 512], i=kfv[j][r, h]: e.dma_start(out=o, in_=i),
                           reads=[("kful", l, j)], writes=[("kT", h % 2, r * 4 + j)] + (list(k_kT) + list(k_v) if first else []))
            v5 = v_b.rearrange("p (r j i) c -> p r j i c", r=2, j=4)
            for j in range(4):
                for r in range(2):
                    sc.dma("sp", lambda e, o=v5[:, r, j], i=vfv[j][:, r, :, h, :]: e.dma_start(out=o, in_=i),
                           reads=[("vful", l, j)], writes=[("v", h % 2, j * 2 + r)])

        zt, k_zt = T.get([128, 512], F32)

        def za_part(hp, tq):
            za_sb, k_za = za2[hp % 2]

            def f_za(e):
                ins = None
                for j in range(4):
                    t = tq * 4 + j
                    for k in range(8):
                        ins = e.matmul(bank(1)[:, j * 128:(j + 1) * 128], lhsT=hT[:, k, t * 128:(t + 1) * 128], rhs=wst[:, 1, k, hp * 128:(hp + 1) * 128], start=(k == 0), stop=(k == 7))
                return ins
            sc.op("pe", f_za, reads=[("hT", tq * 4 + j) for j in range(4)] + wst1_keys, writes=[bk(1)])
            sc.op("act", lambda e: e.activation(out=zt, in_=bank(1), func=AF.Exp, scale=-1.0), reads=[bk(1)], writes=k_zt)
            sc.op("dve", lambda e: e.tensor_scalar(out=zt, in0=zt, scalar1=1.0, scalar2=None, op0=ALU.add), reads=k_zt, writes=k_zt)
            sc.op("dve", lambda e: e.reciprocal(out=zt, in_=zt), reads=k_zt, writes=k_zt)
            sc.op("dve", lambda e: e.tensor_tensor(out=za_sb[:, tq * 4:(tq + 1) * 4, :], in0=bank(1).rearrange("p (j c) -> p j c", c=128),
                                                   in1=zt.rearrange("p (j c) -> p j c", c=128), op=ALU.mult),
                  reads=[bk(1)] + list(k_zt), writes=[("za", hp % 2, tq)] + list(k_za))

        def trg_part(hp, tq):
            og, k_og = og2[hp % 2]
            pg = bank(1).bitcast(BF16)[:, 0:512].rearrange("p (j c) -> p j c", c=128)

            def f_trg(e):
                ins = None
                for j in range(4):
                    ins = e.transpose(out=pg[:, j, :], in_=og[:, tq * 4 + j, :], identity=identb)
                return ins
            sc.op("pe", f_trg, reads=[("og", hp % 2, tq * 4 + j) for j in range(4)] + list(k_og) + ["identb"], writes=[bk(1)])
            sc.op("dve", lambda e: e.tensor_copy(out=ogT[:, hp, tq * 512:(tq + 1) * 512], in_=bank(1).bitcast(BF16)[:, 0:512]),
                  reads=[bk(1)], writes=[("ogT", hp, tq)])

        def emit_qk(i):
            h, qt, kp = its[i]
            if h == 0 and qt == 0 and kp in (1, 3, 5, 7):
                za_part(0, (kp - 1) // 2)
            if h % 2 == 1 and kp == 4 and h + 1 < H:
                za_part((h + 1) // 2, qt)
            if h % 2 == 0 and h >= 2 and kp == 10:
                trg_part(h // 2 - 1, qt)
            kT_b = kv_bufs[h % 2][0]
            sd, b0 = SD[i % 3]
            pslot = i % NPT

            def f_qk(e, sd=sd, kT_b=kT_b, h=h, qt=qt, kp=kp):
                ins = None
                for u in range(2):
                    kt = 2 * kp + u
                    ins = e.matmul(sd[:, u * 512:(u + 1) * 512], lhsT=kT_b[0:DK, kt * 128:(kt + 1) * 128], rhs=qT[0:DK, h, qt * 512:(qt + 1) * 512], start=True, stop=True)
                return ins
            sc.op("pe", f_qk, reads=kvk(h) + [("qT", qt * 4 + j) for j in range(4)], writes=[bk(b0), bk(b0 + 1)])
            pT_s, k_pT_s = pT_slots[pslot]
            sc.op("act", lambda e, o=pT_s, i=sd: e.activation(out=o, in_=i, func=AF.Exp, scale=scale),
                  reads=[bk(b0), bk(b0 + 1)], writes=[("pT", pslot)] + list(k_pT_s))

        def emit_pv(i):
            h, qt, kp = its[i]
            hp, hh = h // 2, h % 2
            v_b = kv_bufs[h % 2][2]
            pslot = i % NPT
            ob = 0
            osl = qt % 2

            pT_s, k_pT_s = pT_slots[pslot]

            def f_pv(e, ob=ob, v_b=v_b, pT_s=pT_s, kp=kp):
                ins = None
                for u in range(2):
                    kt = 2 * kp + u
                    ins = e.matmul(bank(ob)[0:65, :], lhsT=v_b[:, kt, :], rhs=pT_s[:, u * 512:(u + 1) * 512], start=(kt == 0), stop=(kt == 31))
                return ins
            sc.op("pe", f_pv, reads=kvk(h) + [("pT", pslot)] + list(k_pT_s), writes=[bk(ob)])
            if kp != 15:
                return
            if qt == 3 and h + 2 < H:
                kv_load(h + 2)
            za_sb, k_za = za2[hp % 2]
            og, k_og = og2[hp % 2]
            oacc, k_oacc = oacc2[osl]
            sc.op("dve", lambda e, o=oacc[0:65, :], i=bank(ob)[0:65, :]: e.tensor_copy(out=o, in_=i),
                  reads=[bk(ob)], writes=[("oacc", osl)] + list(k_oacc))
            po = bank(1)[:, 0:260].rearrange("p (j c) -> p j c", c=65)

            def f_tro(e, oacc=oacc, po=po):
                ins = None
                for j in range(4):
                    ins = e.transpose(out=po[:, j, :], in_=oacc[0:65, j * 128:(j + 1) * 128], identity=identf[0:65, 0:65])
                return ins
            sc.op("pe", f_tro, reads=[("oacc", osl), "identf"] + list(k_oacc), writes=[bk(1)])
            sc.op("dve", lambda e, o=rinv[:, 0:4], i=po[:, :, 64]: e.reciprocal(out=o, in_=i),
                  reads=[bk(1)], writes=k_rinv)
            for j in range(4):
                t = qt * 4 + j
                sc.op("dve", lambda e, o=og[:, t, hh * 64:(hh + 1) * 64], i=po[:, j, 0:64], s_=rinv[:, j:j + 1], z=za_sb[:, t, hh * 64:(hh + 1) * 64]:
                      e.scalar_tensor_tensor(out=o, in0=i, scalar=s_, in1=z, op0=ALU.mult, op1=ALU.mult),
                      reads=[bk(1), ("za", hp % 2, qt)] + list(k_rinv) + list(k_za), writes=[("og", hp % 2, t)] + list(k_og))

        kv_load(0)
        kv_load(1)
        for i in range(LOOK):
            emit_qk(i)
        for i in range(len(its)):
            if i + LOOK < len(its):
                emit_qk(i + LOOK)
            emit_pv(i)
        for tq in range(4):
            trg_part(H // 2 - 1, tq)

        chk("B")
        for k in range(8):
            load_w(wst[:, 0, k, :], wv[:, k, C_ZF:C_ZF + 512], 512, [("wst", 0, k)])
        T.reset()
        AR.reset()
        szf, k_szf = T.get([128, 2, 512], BF16)
        NBUF = 3
        dbufs = [AR.get([128, 4096], BF16) for _ in range(NBUF)]
        abufs = [T.get([128, 4, 1024], BF16) for _ in range(NBUF)]
        fscale = float(S * 128) ** -0.5
        di = 0
        for kt in range(4):
            for scg in range(8):
                b = di % NBUF
                di += 1
                dbuf, k_dbuf = dbufs[b]
                abuf, k_abuf = abufs[b]
                sc.dma("sp", lambda e, o=dbuf, i=dft_d[kt, scg]: e.dma_start(out=o, in_=i),
                       writes=[("dbuf", b)] + list(k_dbuf))
                a_src = aful[l][scg % 4].ap()[(scg // 4) * 512:(scg // 4 + 1) * 512, :].rearrange("(sci p) n -> p sci n", p=128)
                sc.dma("sp", lambda e, o=abuf, i=a_src: e.dma_start(out=o, in_=i),
                       reads=[("aful", l, scg % 4)], writes=[("abuf", b)] + list(k_abuf))
                d4 = dbuf.rearrange("p (sci cs k) -> p sci cs k", cs=2, k=512)
                a5 = abuf.rearrange("p sci (g cs m) -> p sci g cs m", cs=2, m=128)

                def f_f(e, scg=scg, d4=d4, a5=a5):
                    ins = None
                    for sci in range(4):
                        for g in range(4):
                            for cs in range(2):
                                first = (scg == 0 and sci == 0 and cs == 0)
                                last = (scg == 7 and sci == 3 and cs == 1)
                                ins = e.matmul(bank(g), lhsT=a5[:, sci, g, cs, :], rhs=d4[:, sci, cs, :], start=first, stop=last)
                    return ins
                sc.op("pe", f_f, reads=[("dbuf", b), ("abuf", b)] + list(k_dbuf) + list(k_abuf), writes=[bk(0), bk(1), bk(2), bk(3)])
            for g in range(4):
                pb = 4 + (g % 2)
                zs = g % 2

                def f_zf(e, g=g, kt=kt, pb=pb):
                    ins = None
                    for k in range(8):
                        ins = e.matmul(bank(pb), lhsT=wst[:, 0, k, g * 128:(g + 1) * 128], rhs=hT[:, k, kt * 512:(kt + 1) * 512], start=(k == 0), stop=(k == 7))
                    return ins
                sc.op("pe", f_zf, reads=[("hT", kt * 4 + j) for j in range(4)] + wst0_keys, writes=[bk(pb)])
                sc.op("act", lambda e, o=szf[:, zs, :], i=bank(pb): e.activation(out=o, in_=i, func=AF.Silu),
                      reads=[bk(pb)], writes=[("szf", zs)] + list(k_szf))
                sc.op("dve", lambda e, o=fgT[:, g, kt * 512:(kt + 1) * 512], i=bank(g), z=szf[:, zs, :]:
                      e.scalar_tensor_tensor(out=o, in0=i, scalar=fscale, in1=z, op0=ALU.mult, op1=ALU.mult),
                      reads=[bk(g), ("szf", zs)] + list(k_szf), writes=[("fgT", g, kt)])

        chk("C")
        T.reset()
        AR.reset()
        wat_sb, k_wat = AR.get([128, 4, D], BF16)
        wfo_sb, k_wfo = AR.get([128, 4, D], BF16)
        wo_sb, k_wo = AR.get([128, 8, D], BF16)
        wat_v = w_attn[l].rearrange("(k p) n -> p k n", p=128)
        wfo_v = w_four[l].rearrange("(k p) n -> p k n", p=128)
        wo_v = w_out[l].rearrange("(k p) n -> p k n", p=128)
        for k in range(4):
            load_w(wat_sb[:, k, :], wat_v[:, k, :], D, [("wat", k)] + list(k_wat), eng="dve")
            load_w(wfo_sb[:, k, :], wfo_v[:, k, :], D, [("wfo", k)] + list(k_wfo), eng="dve")
        sg_slots = [T.get([128, 512], F32) for _ in range(4)]
        t1_slots = [T.get([128, 512], F32) for _ in range(2)]
        mT = qT
        for j in range(8):
            ws = j % 2
            wk_all = [("wst", ws, k) for k in range(8)]
            load_w(wst[:, ws, :, 0:128], wv[:, :, C_GA + j * 128:C_GA + (j + 1) * 128], 1024, wk_all, inner=128)
            load_w(wst[:, ws, :, 128:256], wv[:, :, C_GF + j * 128:C_GF + (j + 1) * 128], 1024, wk_all, inner=128)
            wkeys = [("wst", ws, k) for k in range(8)]
            load_w(wo_sb[:, j, :], wo_v[:, j, :], D, [("wo", j)] + list(k_wo), eng="dve")
            for tt in range(4):
                st_ = (j * 4 + tt) % 2
                b0 = 4 * st_

                def f_g(e, ws=ws, tt=tt, b0=b0, j=j):
                    ins = None
                    for a in range(2):
                        for k in range(8):
                            ins = e.matmul(bank(b0 + a), lhsT=wst[:, ws, k, a * 128:(a + 1) * 128], rhs=hT[:, k, tt * 512:(tt + 1) * 512], start=(k == 0), stop=(k == 7))
                    for k in range(4):
                        ins = e.matmul(bank(b0 + 2), lhsT=wat_sb[:, k, j * 128:(j + 1) * 128], rhs=ogT[:, k, tt * 512:(tt + 1) * 512], start=(k == 0), stop=(k == 3))
                    for k in range(4):
                        ins = e.matmul(bank(b0 + 3), lhsT=wfo_sb[:, k, j * 128:(j + 1) * 128], rhs=fgT[:, k, tt * 512:(tt + 1) * 512], start=(k == 0), stop=(k == 3))
                    return ins
                sc.op("pe", f_g,
                      reads=wkeys + [("hT", tt * 4 + i) for i in range(4)] + [("wat", k) for k in range(4)] + [("wfo", k) for k in range(4)]
                      + list(k_wat) + list(k_wfo) + [("ogT", k, tt) for k in range(4)] + [("fgT", k, tt) for k in range(4)],
                      writes=[bk(b0 + i) for i in range(4)] + [("qT", tt * 4 + i) for i in range(0)])
                sga, k_sga = sg_slots[st_ * 2]
                sgf, k_sgf = sg_slots[st_ * 2 + 1]
                t1s, k_t1s = t1_slots[st_]
                for a, (sgx, k_sgx) in enumerate(((sga, k_sga), (sgf, k_sgf))):
                    sc.op("act", lambda e, o=sgx, i=bank(b0 + a), b_=bm[:, a * 8 + j:a * 8 + j + 1]: e.activation(out=o, in_=i, func=AF.Sigmoid, bias=b_),
                          reads=[bk(b0 + a), "bm"], writes=list(k_sgx))
                sc.op("dve", lambda e, o=t1s, i=bank(b0 + 2), z=sga: e.tensor_tensor(out=o, in0=i, in1=z, op=ALU.mult),
                      reads=[bk(b0 + 2)] + list(k_sga), writes=list(k_t1s))
                sc.op("dve", lambda e, o=sgf, i=bank(b0 + 3), z=sgf: e.tensor_tensor(out=o, in0=i, in1=z, op=ALU.mult),
                      reads=[bk(b0 + 3)] + list(k_sgf), writes=list(k_sgf))
                sc.op("dve", lambda e, o=mT[:, j, tt * 512:(tt + 1) * 512], i=t1s, z=sgf: e.tensor_tensor(out=o, in0=i, in1=z, op=ALU.add),
                      reads=list(k_sgf) + list(k_t1s) + [("qT", tt * 4 + i) for i in range(4)],
                      writes=[("mT", j, tt)] + [("qT", tt * 4 + i) for i in range(4)])
        xo, k_xo = T.get([128, 2, D], F32)
        load_x(0)
        for t in range(NT):
            xs_ = xt[:, t % 2, :]
            kx = ("xt", t % 2)
            if t + 1 < NT:
                load_x(t + 1)
            ps2 = psQ if t % 2 == 0 else psK
            pkeys = [bk(0), bk(1)] if t % 2 == 0 else [bk(2), bk(3)]

            def f_o(e, t=t, ps2=ps2):
                ins = None
                for hf in range(2):
                    for k in range(8):
                        ins = e.matmul(ps2[:, hf * 512:(hf + 1) * 512], lhsT=mT[:, k, t * 128:(t + 1) * 128], rhs=wo_sb[:, k, hf * 512:(hf + 1) * 512], start=(k == 0), stop=(k == 7))
                return ins
            sc.op("pe", f_o, reads=[("mT", k, t // 4) for k in range(8)] + [("qT", t)] + [("wo", k) for k in range(8)] + list(k_wo), writes=pkeys)
            xs2 = t % 2
            sc.op("dve", lambda e, o=xo[:, xs2, :], i=ps2, z=xs_: e.tensor_tensor(out=o, in0=i, in1=z, op=ALU.add),
                  reads=pkeys + [kx] + list(k_xo), writes=[("xo", xs2)])
            sc.dma("sp", lambda e, o=xd_v[:, t, :], i=xo[:, xs2, :]: e.dma_start(out=o, in_=i),
                   reads=[("xo", xs2)] + list(k_xo), writes=[("xdst", l, t)])
        if DEBUG and l == 0:
            sc.dma("sp", lambda e: e.dma_start(out=dbg_og, in_=ogT), reads=[("ogT", a, b) for a in range(4) for b in range(4)], writes=["dbg_og"])
            sc.dma("sp", lambda e: e.dma_start(out=dbg_fg, in_=fgT), reads=[("fgT", a, b) for a in range(4) for b in range(4)], writes=["dbg_fg"])
            sc.dma("sp", lambda e: e.dma_start(out=dbg_m, in_=qT), reads=[("mT", a, b) for a in range(8) for b in range(4)] + [("qT", t) for t in range(NT)], writes=["dbg_m"])
    except _Stop:
        pass
    if DEBUG:
        sc.final_wait("sp", ["dbg_og", "dbg_fg", "dbg_m"])
    sc.final_wait("sp", [("xdst", L - 1, t) for t in range(NT)])

    with nc.Block() as block:
        @block.tensor
        def _(e):
            sc.replay("pe", e)

        @block.scalar
        def _(e):
            sc.replay("act", e)

        @block.vector
        def _(e):
            sc.replay("dve", e)

        @block.gpsimd
        def _(e):
            sc.replay("pool", e)

        @block.sync
        def _(e):
            sc.replay("sp", e)
    return nc


_CACHE = {}


def _tables(half):
    key = ("tab", half)
    if key in _CACHE:
        return _CACHE[key]
    hd = 16
    inv_freq = (10000.0 ** (-np.arange(hd, dtype=np.float32) / hd)).astype(np.float32)
    pos = (half * NTOK + np.arange(NTOK, dtype=np.float32)).astype(np.float32)
    ang = (pos[:, None] * inv_freq[None, :]).astype(np.float32)
    cs = np.concatenate([np.cos(ang), np.sin(ang)], axis=1).astype(np.float32)
    rope = np.ascontiguousarray(cs.reshape(NT, 128, 32).transpose(1, 0, 2))
    s_idx = np.arange(S, dtype=np.int64).reshape(8, 4, 128)
    k_idx = (half * NTOK + np.arange(NTOK, dtype=np.int64)).reshape(4, 512)
    prod = (s_idx[None, :, :, :, None] * k_idx[:, None, None, None, :]) % S
    th = (2.0 * np.pi / S) * prod.astype(np.float64)
    c = np.cos(th)
    sn = -np.sin(th)
    tab = np.stack([c, sn], axis=4)
    tab = tab.transpose(0, 1, 3, 2, 4, 5)
    dft = np.ascontiguousarray(tab.reshape(4, 8, 128, 4096)).astype(ml_dtypes.bfloat16)
    _CACHE[key] = (rope, dft)
    return rope, dft


def _consts():
    if "c" in _CACHE:
        return _CACHE["c"]
    c_i = np.arange(128, dtype=np.int64)
    th = (2.0 * np.pi / 128) * ((c_i[:, None] * c_i[None, :]) % 128).astype(np.float64)
    ccsc = np.concatenate([np.cos(th), np.sin(th)], axis=1).astype(ml_dtypes.bfloat16)
    ib = np.eye(128, dtype=np.float32).astype(ml_dtypes.bfloat16)
    i_f = np.eye(128, dtype=np.float32)
    _CACHE["c"] = (ccsc, ib, i_f)
    return _CACHE["c"]


STOP = None
DEBUG = False


def _get_nc(L):
    key = ("nc", L, STOP)
    if key not in _CACHE:
        _CACHE[key] = build_program(L, STOP)
    return _CACHE[key]


def _layer_params(norm_g, q_latent_g, kv_latent_g, q_head_g, k_head_g, b_merge, ls):
    L = len(ls)
    ng = np.stack([np.ascontiguousarray(norm_g[l].reshape(8, 128).T) for l in ls])
    gl = np.stack([np.ascontiguousarray(np.concatenate([q_latent_g[l].reshape(2, 128), kv_latent_g[l].reshape(2, 128)], 0).T) for l in ls])
    gqr = np.stack([np.ascontiguousarray(np.broadcast_to(q_head_g[l][None, :], (128, DK))) for l in ls])
    gkr = np.stack([np.ascontiguousarray(np.broadcast_to(k_head_g[l][None, :], (128, DK))) for l in ls])
    bmt = np.stack([np.ascontiguousarray(b_merge[l].reshape(2, 8, 128).transpose(2, 0, 1).reshape(128, 16)) for l in ls])
    return ng.astype(np.float32), gl.astype(np.float32), gqr.astype(np.float32), gkr.astype(np.float32), bmt.astype(np.float32)


FUSED_LAYERS = 4


def kernel(x, norm_g, w_in, q_latent_g, kv_latent_g, w_uq, w_ukv, q_head_g, k_head_g,
           w_attn_proj, w_fourier_proj, b_merge, w_out):
    f = lambda a: np.ascontiguousarray(np.asarray(a, dtype=np.float32))
    x, norm_g, w_in, q_latent_g, kv_latent_g = f(x), f(norm_g), f(w_in), f(q_latent_g), f(kv_latent_g)
    w_uq, w_ukv, q_head_g, k_head_g = f(w_uq), f(w_ukv), f(q_head_g), f(k_head_g)
    w_attn_proj, w_fourier_proj, b_merge, w_out = f(w_attn_proj), f(w_fourier_proj), f(b_merge), f(w_out)
    depth = w_in.shape[0]
    ccsc, ib, i_f = _consts()
    cur = [np.ascontiguousarray(x[c // 2, (c % 2) * NTOK:(c % 2 + 1) * NTOK, :]) for c in range(8)]
    step = FUSED_LAYERS
    for l0 in range(0, depth, step):
        ls = list(range(l0, min(l0 + step, depth)))
        nc = _get_nc(len(ls))
        ng, gl, gqr, gkr, bmt = _layer_params(norm_g, q_latent_g, kv_latent_g, q_head_g, k_head_g, b_merge, ls)
        sl = slice(ls[0], ls[-1] + 1)
        in_maps = []
        for c in range(8):
            rope, dft = _tables(c % 2)
            in_maps.append({
                "x": cur[c], "w_in": w_in[sl], "w_uq": w_uq[sl], "w_ukv": w_ukv[sl],
                "w_attn": w_attn_proj[sl], "w_four": w_fourier_proj[sl], "w_out": w_out[sl],
                "norm_gT": ng, "glatT": gl, "gq_rep": gqr, "gk_rep": gkr, "bmT": bmt,
                "rope_cs": rope, "dft": dft, "ccsc": ccsc, "ident_bf": ib, "ident_f": i_f,
            })
        res = run_bass_kernel_spmd(nc, in_maps, core_ids=list(range(8)))
        cur = [np.asarray(res.results[c]["y"], dtype=np.float32) for c in range(8)]
        if DEBUG:
            _CACHE["dbg"] = [{k: np.asarray(res.results[c][k]) for k in ("dbg_og", "dbg_fg", "dbg_m")} for c in range(8)]
    out = np.empty((4, S, D), dtype=np.float32)
    for c in range(8):
        out[c // 2, (c % 2) * NTOK:(c % 2 + 1) * NTOK, :] = cur[c]
    return out
```

```python
import numpy as np
import ml_dtypes
import concourse.bass as bass
import concourse.mybir as mybir
from concourse.bass_utils import run_bass_kernel_spmd

F32 = mybir.dt.float32
BF16 = mybir.dt.bfloat16
AF = mybir.ActivationFunctionType
ALU = mybir.AluOpType
AX = mybir.AxisListType

D = 1024
S = 4096
NTOK = 2048
NT = NTOK // 128
H = 8
DK = 96
DV = 64
INW = 4128
EPS = 1e-6
C_CQ, C_KPE, C_ZA, C_UF, C_ZF, C_GA, C_GF = 0, 512, 544, 1056, 1568, 2080, 3104
ENGS = ("pe", "act", "dve", "pool", "sp")
NDMA = 8


class Sched:
    def __init__(self, nc):
        self.nc = nc
        self.streams = {e: [] for e in ENGS}
        self.sems = {}
        self.cur = {}
        self.cnt = {}
        self.waited = {e: {} for e in ENGS}
        self.lastw = {}
        self.readers = {}
        self.dma_gen = {q: [0] * NDMA for q in ("sp", "pool")}
        self.dma_rr = {q: 0 for q in ("sp", "pool")}
        self.epoch = -1
        self.ncc = 0
        for q in ("sp", "pool"):
            for i in range(NDMA):
                self._sem(f"dma_{q}_{i}")
        self.new_epoch()

    def _sem(self, name):
        self.sems[name] = self.nc.alloc_semaphore(name)
        return name

    def new_epoch(self):
        self.epoch += 1
        for e in ("pe", "act", "dve", "pool"):
            self.cur[e] = self._sem(f"c_{e}_{self.epoch}")
            self.cnt[e] = 0

    def _deps(self, reads, writes):
        deps = {}

        def add(tok):
            if tok is not None:
                n, v = tok
                if v > deps.get(n, 0):
                    deps[n] = v
        for k in reads:
            add(self.lastw.get(k))
        for k in writes:
            add(self.lastw.get(k))
            for n, v in self.readers.get(k, {}).items():
                add((n, v))
        return deps

    def _emit(self, eng, deps, fn, semname, inc):
        waits = []
        w = self.waited[eng]
        for n, v in deps.items():
            if v > w.get(n, 0):
                waits.append((n, v))
                w[n] = v
        self.streams[eng].append((waits, fn, semname, inc))

    def _mark(self, tok, reads, writes):
        n, v = tok
        for k in reads:
            self.readers.setdefault(k, {})[n] = v
        for k in writes:
            self.lastw[k] = tok
            self.readers[k] = {}

    def op(self, eng, fn, reads=(), writes=()):
        deps = self._deps(reads, writes)
        self.cnt[eng] += 1
        tok = (self.cur[eng], self.cnt[eng])
        self._emit(eng, deps, fn, tok[0], 1)
        self._mark(tok, reads, writes)

    def dma(self, q, fn, reads=(), writes=()):
        deps = self._deps(reads, writes)
        i = self.dma_rr[q]
        self.dma_rr[q] = (i + 1) % NDMA
        name = f"dma_{q}_{i}"
        gen = self.dma_gen[q][i]
        if gen > 0:
            deps[name] = max(deps.get(name, 0), 16 * gen)
        self.dma_gen[q][i] = gen + 1
        tok = (name, 16 * (gen + 1))
        self._emit(q, deps, fn, name, 16)
        self._mark(tok, reads, writes)

    def collective(self, slot, fn, reads=(), writes=()):
        deps = self._deps(reads, writes)
        name = f"cc_{slot}"
        if name not in self.sems:
            self._sem(name)
            self.cc_gen = getattr(self, "cc_gen", {})
            self.cc_gen[name] = 0
        self.cc_gen[name] += 1
        tok = (name, self.cc_gen[name])
        self._emit("pool", deps, fn, name, 1)
        self._mark(tok, reads, writes)

    def final_wait(self, eng, keys):
        deps = self._deps(keys, ())
        self._emit(eng, deps, None, None, 0)

    def replay(self, eng, e):
        for waits, fn, semname, inc in self.streams[eng]:
            for n, v in waits:
                e.wait_ge(self.sems[n], v)
            if fn is not None:
                ins = fn(e)
                ins.then_inc(self.sems[semname], inc)


class _Stop(Exception):
    pass


def build_program(L, stop=None):
    nc = bass.Bass("TRN2", target_bir_lowering=False)
    sc = Sched(nc)

    def din(name, shape, dt=F32):
        return nc.dram_tensor(name, shape, dt, kind="ExternalInput").ap()

    x_in = din("x", [NTOK, D])
    w_in = din("w_in", [L, D, INW])
    w_uq = din("w_uq", [L, 256, H * DK])
    w_ukv = din("w_ukv", [L, 256, H * 128])
    w_attn = din("w_attn", [L, 512, D])
    w_four = din("w_four", [L, 512, D])
    w_out = din("w_out", [L, D, D])
    normg_d = din("norm_gT", [L, 128, 8])
    glat_d = din("glatT", [L, 128, 4])
    gq_d = din("gq_rep", [L, 128, DK])
    gk_d = din("gk_rep", [L, 128, DK])
    bm_d = din("bmT", [L, 128, 16])
    rope_d = din("rope_cs", [128, NT, 32])
    dft_d = din("dft", [4, 8, 128, 4096], BF16)
    ccsc_d = din("ccsc", [128, 256], BF16)
    identb_d = din("ident_bf", [128, 128], BF16)
    identf_d = din("ident_f", [128, 128])
    y_out = nc.dram_tensor("y", [NTOK, D], F32, kind="ExternalOutput").ap()
    if DEBUG:
        dbg_og = nc.dram_tensor("dbg_og", [128, 4, NTOK], BF16, kind="ExternalOutput").ap()
        dbg_fg = nc.dram_tensor("dbg_fg", [128, 4, NTOK], BF16, kind="ExternalOutput").ap()
        dbg_m = nc.dram_tensor("dbg_m", [128, 8, NTOK], BF16, kind="ExternalOutput").ap()

    kloc = [[nc.dram_tensor(f"kloc{l}_{j}", [H * DK, 512], BF16) for j in range(4)] for l in range(L)]
    kful = [[nc.dram_tensor(f"kful{l}_{j}", [2 * H * DK, 512], BF16) for j in range(4)] for l in range(L)]
    vloc = [[nc.dram_tensor(f"vloc{l}_{j}", [512, H * 65], BF16) for j in range(4)] for l in range(L)]
    vful = [[nc.dram_tensor(f"vful{l}_{j}", [2 * 512, H * 65], BF16) for j in range(4)] for l in range(L)]
    aloc = [[nc.dram_tensor(f"aloc{l}_{j}", [512, 1024], BF16) for j in range(4)] for l in range(L)]
    aful = [[nc.dram_tensor(f"aful{l}_{j}", [2 * 512, 1024], BF16) for j in range(4)] for l in range(L)]
    xbuf = [nc.dram_tensor(f"xbuf{l}", [NTOK, D], F32) for l in range(max(L - 1, 1))]

    def sb(name, shape, dt):
        return nc.alloc_sbuf_tensor(name, shape, dt).ap()

    hT = sb("hT", [128, 8, NTOK], BF16)
    qT = sb("qT", [128, 8, NTOK], BF16)
    ogT = sb("ogT", [128, 4, NTOK], BF16)
    fgT = sb("fgT", [128, 4, NTOK], BF16)
    arena = sb("arena", [128, 16384], BF16)
    wst = sb("wst", [128, 2, 8, 512], BF16)
    stage = sb("stage", [128, 2, 1024], F32)
    xt = sb("xt", [128, 2, D], F32)
    tmp = sb("tmp", [128, 18432], BF16)
    identb = sb("identb", [128, 128], BF16)
    identf = sb("identf", [128, 128], F32)
    rope = sb("rope", [128, NT, 32], F32)
    ccsc = sb("ccsc_sb", [128, 256], BF16)
    normg = sb("normg", [128, 8], F32)
    glat = sb("glat", [128, 4], F32)
    gq = sb("gq", [128, DK], F32)
    gk = sb("gk", [128, DK], F32)
    bm = sb("bm", [128, 16], F32)
    stats = sb("stats", [128, 64], F32)
    eps_t = sb("eps_t", [128, 1], F32)

    psQ = nc.alloc_psum_tensor("psQ", [128, 1024], F32).ap()
    psK = nc.alloc_psum_tensor("psK", [128, 1024], F32).ap()
    psS = [nc.alloc_psum_tensor(f"psS{i}", [128, 1024], F32).ap() for i in range(2)]

    def bank(i):
        if i < 2:
            return psQ[:, i * 512:(i + 1) * 512]
        if i < 4:
            return psK[:, (i - 2) * 512:(i - 1) * 512]
        return psS[(i - 4) // 2][:, ((i - 4) % 2) * 512:((i - 4) % 2 + 1) * 512]

    def bk(i):
        return ("ps", i)

    class Tmp:
        def __init__(self, base_ap, prefix, nbytes):
            self.base, self.prefix, self.nbytes = base_ap, prefix, nbytes
            self.off = 0

        def reset(self):
            self.off = 0

        def get(self, shape, dt):
            es = 4 if dt == F32 else 2
            n = int(np.prod(shape[1:]))
            nb = (n * es + 63) // 64 * 64
            a, b = self.off, self.off + nb
            assert b <= self.nbytes, (self.prefix, b)
            self.off = b
            ap = self.base[:, a // 2:(a + n * es) // 2]
            if dt == F32:
                ap = ap.bitcast(F32)
            if len(shape) == 3:
                ap = ap.rearrange("p (a b) -> p a b", b=shape[2])
            ap = ap[0:shape[0]]
            keys = tuple((self.prefix, i) for i in range(a // 1024, (b - 1) // 1024 + 1))
            return ap, keys

    T = Tmp(tmp, "tmp", 36864)
    AR = Tmp(arena, "arena", 32768)

    stage_rr = [0]

    def load_w(dst, src, n, dst_keys, inner=None, eng="pool"):
        s = stage_rr[0]
        stage_rr[0] ^= 1
        st_ap = stage[:, s, 0:n]
        if inner is not None:
            st_ap = st_ap.rearrange("p (a b) -> p a b", b=inner)
        sc.dma("sp", lambda e, o=st_ap, i=src: e.dma_start(out=o, in_=i), writes=[("stage", s)])
        sc.op(eng, lambda e, o=dst, i=st_ap: e.tensor_copy(out=o, in_=i),
              reads=[("stage", s)], writes=dst_keys)

    def load_small(dst, src, key):
        sc.dma("sp", lambda e, o=dst, i=src: e.dma_start(out=o, in_=i), writes=[key])

    sc.op("dve", lambda e: e.memset(eps_t, EPS), writes=["eps_t"])
    load_small(identb, identb_d, "identb")
    load_small(identf, identf_d, "identf")
    load_small(rope, rope_d, "rope")
    load_small(ccsc, ccsc_d, "ccsc")

    def w_in_v(l):
        return w_in[l].rearrange("(k p) n -> p k n", p=128)

    def chk(name):
        if stop == name:
            sc.dma("sp", lambda e: e.dma_start(out=y_out, in_=x_in), writes=[("xdst", L - 1, t) for t in range(NT)])
            raise _Stop()

    try:
      chk("init")
      for l in range(L):
        if l > 0:
            sc.new_epoch()
        x_src = x_in if l == 0 else xbuf[l - 1].ap()
        x_dst = y_out if l == L - 1 else xbuf[l].ap()
        xs_v = x_src.rearrange("(t p) d -> p t d", p=128)
        xd_v = x_dst.rearrange("(t p) d -> p t d", p=128)

        load_small(normg, normg_d[l], "normg")
        load_small(glat, glat_d[l], "glat")
        load_small(gq, gq_d[l], "gq")
        load_small(gk, gk_d[l], "gk")
        load_small(bm, bm_d[l], "bm")
        wv = w_in_v(l)
        AR.reset()
        wA, k_wA = AR.get([128, 8, 544], BF16)
        wuq_sb, k_wuq = AR.get([128, 2, H * DK], BF16)
        wukv_sb, k_wukv = AR.get([128, 2, H * 128], BF16)
        uT, k_uT = AR.get([128, 4, 512], BF16)
        atile, k_atile = AR.get([128, 2, 1024], BF16)
        for k in range(8):
            load_w(wA[:, k, :], wv[:, k, 0:544], 544, [("wA", k)] + list(k_wA), eng="dve")
        wuq_v = w_uq[l].rearrange("(k p) n -> p k n", p=128)
        wukv_v = w_ukv[l].rearrange("(k p) n -> p k n", p=128)
        for k in range(2):
            load_w(wuq_sb[:, k, :], wuq_v[:, k, :], H * DK, [("wuq", k)] + list(k_wuq), eng="dve")
            load_w(wukv_sb[:, k, :], wukv_v[:, k, :], H * 128, [("wukv", k)] + list(k_wukv), eng="dve")
        wA_keys = [("wA", k) for k in range(8)] + list(k_wA)

        groups = [[0, 1], [2, 3], [4, 5], [6, 7]]

        def exchange1(nm, loc_l, ful_l, slot0, j):
            sc.collective(slot0 + j, lambda e, i=loc_l[j], o=ful_l[j]: e.collective_compute(
                "AllGather", ALU.bypass, replica_groups=groups, ins=[i.ap().opt()], outs=[o.ap().opt()]),
                reads=[(nm + "loc", l, j * 4 + i) for i in range(4)], writes=[(nm + "ful", l, j)])

        T.reset()
        xsb, k_xsb = T.get([128, D], BF16)
        junk, k_junk = T.get([128, D], BF16)
        junk2, k_junk2 = T.get([128, 32], BF16)
        cn, k_cn = T.get([128, 512], BF16)
        cnT, k_cnT = T.get([128, 4, 128], BF16)
        qsq, k_qsq = T.get([128, 768], F32)
        qn, k_qn = T.get([128, 768], F32)
        qg, k_qg = T.get([128, 768], F32)
        qb, k_qb = T.get([128, 768], BF16)
        ksq, k_ksq = T.get([128, 512], F32)
        knn, k_knn = T.get([128, 512], F32)
        kb, k_kb = T.get([128, 768], BF16)
        kpeg, k_kpeg = T.get([128, 32], F32)
        kpeh, k_kpeh = T.get([128, 256], F32)
        rtq, k_rtq = T.get([128, 4, 128], F32)
        rtk, k_rtk = T.get([128, 4, 128], F32)
        vb, k_vb = T.get([128, 2, H * 65], BF16)
        ktile, k_ktile = T.get([128, 2, 1024], BF16)

        vb4 = vb.rearrange("p s (h c) -> p s h c", c=65)
        sc.op("dve", lambda e, o=vb: e.memset(o, 1.0), writes=k_vb)

        def load_x(t):
            rd = [("xdst", l - 1, t)] if l > 0 else []
            sc.dma("sp", lambda e, o=xt[:, t % 2, :], i=xs_v[:, t, :]: e.dma_start(out=o, in_=i), reads=rd, writes=[("xt", t % 2)])

        qps = psQ[:, 0:768]
        q3 = qps.rearrange("p (h d) -> p h d", d=DK)
        kv3 = psK.rearrange("p (h d) -> p h d", d=128)
        qn3 = qn.rearrange("p (h d) -> p h d", d=DK)
        qg3 = qg.rearrange("p (h d) -> p h d", d=DK)
        qb3 = qb.rearrange("p (h d) -> p h d", d=DK)
        ksq3 = ksq.rearrange("p (h d) -> p h d", d=64)
        knn3 = knn.rearrange("p (h d) -> p h d", d=64)
        kb3 = kb.rearrange("p (h d) -> p h d", d=DK)
        kpeh3 = kpeh.rearrange("p (h d) -> p h d", d=32)
        ST = lambda a, b_=None: stats[:, a:(a + 1 if b_ is None else b_)]

        def tr_heads(e, dst, src):
            ins = None
            for h in range(H):
                ins = e.transpose(out=dst[0:DK, h, :], in_=src[:, h, :], identity=identb)
            return ins

        def chain_S1(t):
            ops = []
            xs_ = xt[:, t % 2, :]
            kx = ("xt", t % 2)
            ops.append(lambda: sc.op("act", lambda e: e.activation(out=junk, in_=xs_, func=AF.Square, accum_out=ST(0)),
                                     reads=[kx], writes=list(k_junk) + [("st", 0)]))
            ops.append(lambda: sc.op("act", lambda e: e.activation(out=ST(2), in_=ST(0), func=AF.Sqrt, scale=1.0 / D, bias=eps_t),
                                     reads=[("st", 0), "eps_t"], writes=[("st", 2)]))
            ops.append(lambda: sc.op("dve", lambda e: e.reciprocal(out=ST(3), in_=ST(2)), reads=[("st", 2)], writes=[("st", 3)]))
            ops.append(lambda: sc.op("act", lambda e: e.activation(out=xsb, in_=xs_, func=AF.Copy, scale=ST(3)),
                                     reads=[kx, ("st", 3)], writes=k_xsb))
            pt = bank(6).bitcast(BF16).rearrange("p (k j) -> p k j", j=128)

            def f_tr(e):
                ins = None
                for k in range(8):
                    ins = e.transpose(out=pt[:, k, :], in_=xsb[:, k * 128:(k + 1) * 128], identity=identb)
                return ins
            ops.append(lambda: sc.op("pe", f_tr, reads=list(k_xsb) + ["identb"], writes=[bk(6)]))
            hT_t = hT[:, :, t * 128:(t + 1) * 128]
            ops.append(lambda: sc.op("dve", lambda e: e.tensor_tensor(out=hT_t, in0=pt, in1=normg.unsqueeze(2).to_broadcast([128, 8, 128]), op=ALU.mult),
                                     reads=[bk(6), "normg"], writes=[("hT", t)]))

            def f_cq(e):
                ins = None
                for k in range(8):
                    ins = e.matmul(bank(4), lhsT=hT[:, k, t * 128:(t + 1) * 128], rhs=wA[:, k, 0:512], start=(k == 0), stop=(k == 7))
                for k in range(8):
                    ins = e.matmul(bank(5)[:, 0:32], lhsT=hT[:, k, t * 128:(t + 1) * 128], rhs=wA[:, k, 512:544], start=(k == 0), stop=(k == 7))
                return ins
            ops.append(lambda: sc.op("pe", f_cq, reads=[("hT", t)] + wA_keys, writes=[bk(4), bk(5)]))
            for j in range(2):
                ops.append(lambda j=j: sc.op("act", lambda e: e.activation(out=junk[:, 0:256], in_=bank(4)[:, j * 256:(j + 1) * 256], func=AF.Square, accum_out=ST(4 + j)),
                                             reads=[bk(4)], writes=list(k_junk) + [("st", 4 + j)]))
            ops.append(lambda: sc.op("act", lambda e: e.activation(out=ST(8, 10), in_=ST(4, 6), func=AF.Sqrt, scale=1.0 / 256, bias=eps_t),
                                     reads=[("st", 4), ("st", 5), "eps_t"], writes=[("st", 8)]))
            ops.append(lambda: sc.op("dve", lambda e: e.reciprocal(out=ST(10, 12), in_=ST(8, 10)), reads=[("st", 8)], writes=[("st", 10)]))
            for j in range(2):
                ops.append(lambda j=j: sc.op("dve", lambda e: e.tensor_scalar(out=cn[:, j * 256:(j + 1) * 256], in0=bank(4)[:, j * 256:(j + 1) * 256], scalar1=ST(10 + j), scalar2=None, op0=ALU.mult),
                                             reads=[bk(4), ("st", 10)], writes=k_cn))
            pc = bank(7).bitcast(BF16)[:, 0:512].rearrange("p (k j) -> p k j", j=128)

            def f_trc(e):
                ins = None
                for k in range(4):
                    ins = e.transpose(out=pc[:, k, :], in_=cn[:, k * 128:(k + 1) * 128], identity=identb)
                return ins
            ops.append(lambda: sc.op("pe", f_trc, reads=list(k_cn) + ["identb"], writes=[bk(7)]))
            ops.append(lambda: sc.op("dve", lambda e: e.tensor_tensor(out=cnT, in0=pc, in1=glat.unsqueeze(2).to_broadcast([128, 4, 128]), op=ALU.mult),
                                     reads=[bk(7), "glat"], writes=k_cnT))

            def f_q(e):
                ins = None
                for k in range(2):
                    ins = e.matmul(bank(0), lhsT=cnT[:, k, :], rhs=wuq_sb[:, k, 0:512], start=(k == 0), stop=(k == 1))
                for k in range(2):
                    ins = e.matmul(bank(1)[:, 0:256], lhsT=cnT[:, k, :], rhs=wuq_sb[:, k, 512:768], start=(k == 0), stop=(k == 1))
                for hf in range(2):
                    for k in range(2):
                        ins = e.matmul(bank(2 + hf), lhsT=cnT[:, 2 + k, :], rhs=wukv_sb[:, k, hf * 512:(hf + 1) * 512], start=(k == 0), stop=(k == 1))
                return ins
            last = lambda: sc.op("pe", f_q, reads=list(k_cnT) + [("wuq", 0), ("wuq", 1), ("wukv", 0), ("wukv", 1)] + list(k_wuq) + list(k_wukv),
                                 writes=[bk(0), bk(1), bk(2), bk(3)])
            return ops, last

        def rope_chain(ops, eng, x1, x2, o1, o2, rt, k_src, k_rt, k_dst, cos_b, sin_b):
            rt4 = rt.rearrange("p a (h d) -> p a h d", d=16)
            ops.append(lambda: sc.op(eng, lambda e: e.tensor_tensor(out=rt4[:, 0], in0=x1, in1=cos_b, op=ALU.mult), reads=list(k_src) + ["rope"], writes=k_rt))
            ops.append(lambda: sc.op(eng, lambda e: e.tensor_tensor(out=rt4[:, 1], in0=x2, in1=sin_b, op=ALU.mult), reads=list(k_src) + ["rope"], writes=k_rt))
            ops.append(lambda: sc.op(eng, lambda e: e.tensor_tensor(out=rt4[:, 2], in0=x2, in1=cos_b, op=ALU.mult), reads=list(k_src) + ["rope"], writes=k_rt))
            ops.append(lambda: sc.op(eng, lambda e: e.tensor_tensor(out=rt4[:, 3], in0=x1, in1=sin_b, op=ALU.mult), reads=list(k_src) + ["rope"], writes=k_rt))
            ops.append(lambda: sc.op(eng, lambda e: e.tensor_tensor(out=o1, in0=rt4[:, 0], in1=rt4[:, 1], op=ALU.subtract), reads=k_rt, writes=k_dst))
            ops.append(lambda: sc.op(eng, lambda e: e.tensor_tensor(out=o2, in0=rt4[:, 2], in1=rt4[:, 3], op=ALU.add), reads=k_rt, writes=k_dst))

        def chain_Sq(t):
            ops = []
            cos_b = rope[:, t, 0:16].unsqueeze(1).to_broadcast([128, H, 16])
            sin_b = rope[:, t, 16:32].unsqueeze(1).to_broadcast([128, H, 16])
            ops.append(lambda: sc.op("act", lambda e: e.activation(out=qsq, in_=qps, func=AF.Square), reads=[bk(0), bk(1)], writes=k_qsq))
            ops.append(lambda: sc.op("dve", lambda e: e.tensor_reduce(out=ST(16, 24), in_=qsq.rearrange("p (h d) -> p h d", d=DK), axis=AX.X, op=ALU.add),
                                     reads=k_qsq, writes=[("st", 16)]))
            ops.append(lambda: sc.op("act", lambda e: e.activation(out=ST(32, 40), in_=ST(16, 24), func=AF.Sqrt, scale=1.0 / DK, bias=eps_t),
                                     reads=[("st", 16), "eps_t"], writes=[("st", 32)]))
            ops.append(lambda: sc.op("dve", lambda e: e.reciprocal(out=ST(40, 48), in_=ST(32, 40)), reads=[("st", 32)], writes=[("st", 40)]))
            def q_scale():
                for h in range(H):
                    sc.op("dve", lambda e, h=h: e.scalar_tensor_tensor(out=qg3[:, h, :], in0=q3[:, h, :], scalar=ST(40 + h), in1=gq, op0=ALU.mult, op1=ALU.mult),
                          reads=[bk(0), bk(1), ("st", 40), "gq"], writes=k_qg)
            ops.append(q_scale)
            ops.append(lambda: sc.op("act", lambda e: e.activation(out=qb3[:, :, 0:64], in_=qg3[:, :, 0:64], func=AF.Copy), reads=k_qg, writes=k_qb))
            rope_chain(ops, "dve", qg3[:, :, 64:80], qg3[:, :, 80:96], qb3[:, :, 64:80], qb3[:, :, 80:96], rtq, k_qg, k_rtq, k_qb, cos_b, sin_b)
            pq = bank(6).bitcast(BF16).rearrange("p (h j) -> p h j", j=128)
            ops.append(lambda: sc.op("pe", lambda e: tr_heads(e, pq, qb3), reads=list(k_qb) + ["identb"], writes=[bk(6)]))
            ops.append(lambda: sc.op("act", lambda e: e.activation(out=qT[0:DK, :, t * 128:(t + 1) * 128], in_=pq[0:DK], func=AF.Copy),
                                     reads=[bk(6)], writes=[("qT", t)]))
            return ops

        def chain_Sk(t):
            ops = []
            cos_b = rope[:, t, 0:16].unsqueeze(1).to_broadcast([128, H, 16])
            sin_b = rope[:, t, 16:32].unsqueeze(1).to_broadcast([128, H, 16])
            vs = t % 2
            kvs = ("vb", vs)
            rk = ST(56, 64)
            ops.append(lambda: sc.op("act", lambda e: e.activation(out=ksq3, in_=kv3[:, :, 0:64], func=AF.Square), reads=[bk(2), bk(3)], writes=k_ksq))
            ops.append(lambda: sc.op("act", lambda e: e.activation(out=junk2, in_=bank(5)[:, 0:32], func=AF.Square, accum_out=ST(12)),
                                     reads=[bk(5)], writes=list(k_junk2) + [("st", 12)]))
            ops.append(lambda: sc.op("dve", lambda e: e.tensor_tensor(out=kpeg, in0=bank(5)[:, 0:32], in1=gk[:, 64:96], op=ALU.mult),
                                     reads=[bk(5), "gk"], writes=k_kpeg))
            ops.append(lambda: sc.op("act", lambda e: e.activation(out=vb4[:, vs, :, 0:64], in_=kv3[:, :, 64:128], func=AF.Copy),
                                     reads=[bk(2), bk(3)] + list(k_vb), writes=[kvs]))
            ops.append(lambda: sc.op("dve", lambda e: e.tensor_reduce(out=ST(48, 56), in_=ksq3, axis=AX.X, op=ALU.add), reads=k_ksq, writes=[("st", 48)]))
            ops.append(lambda: sc.op("dve", lambda e: e.tensor_scalar(out=ST(48, 56), in0=ST(48, 56), scalar1=ST(12), scalar2=None, op0=ALU.add),
                                     reads=[("st", 48), ("st", 12)], writes=[("st", 48)]))
            ops.append(lambda: sc.op("act", lambda e: e.activation(out=ST(24, 32), in_=ST(48, 56), func=AF.Sqrt, scale=1.0 / DK, bias=eps_t),
                                     reads=[("st", 48), "eps_t"], writes=[("st", 24)]))
            ops.append(lambda: sc.op("dve", lambda e: e.reciprocal(out=ST(56, 64), in_=ST(24, 32)), reads=[("st", 24)], writes=[("st", 56)]))
            def k_scale():
                for h in range(H):
                    sc.op("dve", lambda e, h=h: e.scalar_tensor_tensor(out=kb3[:, h, 0:64], in0=kv3[:, h, 0:64], scalar=ST(56 + h), in1=gk[:, 0:64], op0=ALU.mult, op1=ALU.mult),
                          reads=[bk(2), bk(3), ("st", 56), "gk"], writes=k_kb)
            ops.append(k_scale)
            vdst = vloc[l][t // 4].ap()[(t % 4) * 128:(t % 4 + 1) * 128, :]
            ops.append(lambda: sc.dma("sp", lambda e: e.dma_start(out=vdst, in_=vb[:, vs, :]),
                                      reads=[kvs] + list(k_vb), writes=[("vloc", l, t)]))
            ops.append(lambda: sc.op("dve", lambda e: e.tensor_tensor(out=kpeh3, in0=kpeg.unsqueeze(1).to_broadcast([128, H, 32]), in1=rk.unsqueeze(2).to_broadcast([128, H, 32]), op=ALU.mult),
                                     reads=list(k_kpeg) + [("st", 56)], writes=k_kpeh))
            rope_chain(ops, "dve", kpeh3[:, :, 0:16], kpeh3[:, :, 16:32], kb3[:, :, 64:80], kb3[:, :, 80:96], rtk, k_kpeh, k_rtk, k_kb, cos_b, sin_b)
            pk = bank(7).bitcast(BF16).rearrange("p (h j) -> p h j", j=128)
            ops.append(lambda: sc.op("pe", lambda e: tr_heads(e, pk, kb3), reads=list(k_kb) + ["identb"], writes=[bk(7)]))
            kt_ap = ktile[:, vs, :].rearrange("p (h j) -> p h j", j=128)
            kks = ("ktile", vs)
            ops.append(lambda: sc.op("act", lambda e: e.activation(out=kt_ap[0:DK], in_=pk[0:DK], func=AF.Copy),
                                     reads=[bk(7)] + list(k_ktile), writes=[kks]))
            kdst = kloc[l][t // 4].ap().rearrange("(h d) n -> d h n", d=DK)[:, :, (t % 4) * 128:(t % 4 + 1) * 128]
            ops.append(lambda: sc.dma("sp", lambda e: e.dma_start(out=kdst, in_=kt_ap[0:DK]),
                                      reads=[kks] + list(k_ktile), writes=[("kloc", l, t)]))
            return ops

        def interleave(chains):
            n = max(len(c) for c in chains)
            for r in range(n):
                for c in chains:
                    if r < len(c):
                        c[r]()

        for k in range(8):
            load_w(wst[:, 0, k, :], wv[:, k, C_UF:C_UF + 512], 512, [("wst", 0, k)])
        wst0_keys = [("wst", 0, k) for k in range(8)]

        def a3_u_chain(tt):
            ops = []
            for g in range(4):
                def f_u(e, g=g):
                    ins = None
                    for k in range(8):
                        ins = e.matmul(bank(7), lhsT=wst[:, 0, k, g * 128:(g + 1) * 128], rhs=hT[:, k, tt * 512:(tt + 1) * 512], start=(k == 0), stop=(k == 7))
                    return ins
                ops.append(lambda f_u=f_u: sc.op("pe", f_u, reads=[("hT", tt * 4 + j) for j in range(4)] + wst0_keys, writes=[bk(7)]))
                ops.append(lambda g=g: sc.op("act", lambda e: e.activation(out=uT[:, g, :], in_=bank(7), func=AF.Copy),
                                             reads=[bk(7)], writes=[("uT", g)] + list(k_uT)))
            noop_ = lambda: None

            def fa_pair(j):
                t = tt * 4 + j
                ps2 = psS[1]
                pkeys = [bk(6), bk(7)]
                a_s = t % 2
                adst = aloc[l][t // 4].ap()[(t % 4) * 128:(t % 4 + 1) * 128, :]

                def f_a(e):
                    ins = None
                    for g in range(4):
                        ins = e.matmul(ps2[:, g * 256:(g + 1) * 256], lhsT=uT[:, g, j * 128:(j + 1) * 128], rhs=ccsc, start=True, stop=True)
                    return ins

                def mm():
                    sc.op("pe", f_a, reads=[("uT", g) for g in range(4)] + list(k_uT) + ["ccsc"], writes=pkeys)

                def ev():
                    sc.op("act", lambda e: e.activation(out=atile[:, a_s, :], in_=ps2, func=AF.Copy),
                          reads=pkeys + list(k_atile), writes=[("atile", a_s)])
                    sc.dma("sp", lambda e: e.dma_start(out=adst, in_=atile[:, a_s, :]),
                           reads=[("atile", a_s)] + list(k_atile), writes=[("aloc", l, t)])
                return [mm, ev]

            ops += fa_pair(0) + fa_pair(1) + [noop_, noop_, noop_] + fa_pair(2) + [noop_, noop_, noop_] + fa_pair(3)
            return ops

        def a3_group(tt):
            for j in range(4):
                t = tt * 4 + j
                ps2 = psS[1]
                pkeys = [bk(6), bk(7)]

                def f_a(e, j=j, ps2=ps2):
                    ins = None
                    for g in range(4):
                        ins = e.matmul(ps2[:, g * 256:(g + 1) * 256], lhsT=uT[:, g, j * 128:(j + 1) * 128], rhs=ccsc, start=True, stop=True)
                    return ins
                sc.op("pe", f_a, reads=[("uT", g) for g in range(4)] + list(k_uT) + ["ccsc"], writes=pkeys)
                a_s = t % 2
                sc.op("act", lambda e, o=atile[:, a_s, :], i=ps2: e.activation(out=o, in_=i, func=AF.Copy),
                      reads=pkeys + list(k_atile), writes=[("atile", a_s)])
                adst = aloc[l][t // 4].ap()[(t % 4) * 128:(t % 4 + 1) * 128, :]
                sc.dma("sp", lambda e, o=adst, i=atile[:, a_s, :]: e.dma_start(out=o, in_=i),
                       reads=[("atile", a_s)] + list(k_atile), writes=[("aloc", l, t)])

        load_x(0)
        load_x(1)
        ops1, last1 = chain_S1(0)
        interleave([ops1])
        last1()
        for t in range(NT):
            chains = [chain_Sq(t), chain_Sk(t)]
            nxt = None
            if t + 1 < NT:
                if t + 2 < NT:
                    load_x(t + 2)
                ops1, nxt = chain_S1(t + 1)
                chains = [ops1] + chains
            if t % 4 == 3:
                chains.append(a3_u_chain(t // 4))
            interleave(chains)
            if nxt is not None:
                nxt()
            if (t >= 5 and t % 4 == 1) or t == NT - 1:
                for gidx in ([t // 4 - 1] if t < NT - 1 else [t // 4 - 1, t // 4] if t % 4 == 1 else [t // 4]):
                    exchange1("k", kloc[l], kful[l], 0, gidx)
                    exchange1("v", vloc[l], vful[l], 4, gidx)
                    exchange1("a", aloc[l], aful[l], 8, gidx)

        chk("A")

        chk("X")
        for k in range(8):
            load_w(wst[:, 1, k, :], wv[:, k, C_ZA:C_ZA + 512], 512, [("wst", 1, k)])
        wst1_keys = [("wst", 1, k) for k in range(8)]

        T.reset()
        AR.reset()
        NPT = 4
        pT_slots = [T.get([128, 1024], BF16) for _ in range(NPT)]
        oacc2 = [T.get([128, 512], F32) for _ in range(2)]
        za2, og2 = [], []
        for s_ in range(2):
            za2.append(T.get([128, NT, 128], BF16))
            og2.append(T.get([128, NT, 128], BF16))
        rinv, k_rinv = T.get([128, 8], F32)
        kv_bufs = []
        for s_ in range(2):
            kT_b, k_kT = AR.get([128, S], BF16)
            v_b, k_v = AR.get([128, 32, 65], BF16)
            kv_bufs.append((kT_b, k_kT, v_b, k_v))
        scale = float(DK) ** -0.5
        kfv = [kful[l][j].ap().rearrange("(r h d) n -> r h d n", r=2, h=H) for j in range(4)]
        vfv = [vful[l][j].ap().rearrange("(r i p) (h c) -> p r i h c", p=128, i=4, c=65) for j in range(4)]
        its = [(h, qt, kp) for h in range(H) for qt in range(4) for kp in range(16)]
        LOOK = 2
        SD = [(psK, 2), (psS[0], 4), (psS[1], 6)]

        def kvk(h):
            kT_b, k_kT, v_b, k_v = kv_bufs[h % 2]
            return [("kT", h % 2, i) for i in range(8)] + [("v", h % 2, j) for j in range(8)] + list(k_kT) + list(k_v)

        def kv_load(h):
            kT_b, k_kT, v_b, k_v = kv_bufs[h % 2]
            for r in range(2):
                for j in range(4):
                    c0 = r * NTOK + j * 512
                    first = (r == 0 and j == 0)
                    sc.dma("sp", lambda e, o=kT_b[0:DK, c0:c0 + 512], i=kfv[j][r, h]: e.dma_start(out=o, in_=i),
                           reads=[("kful", l, j)], writes=[("kT", h % 2, r * 4 + j)] + (list(k_kT) + list(k_v) if first else []))
            v5 = v_b.rearrange("p (r j i) c -> p r j i c", r=2, j=4)
            for j in range(4):
                for r in range(2):
                    sc.dma("sp", lambda e, o=v5[:, r, j], i=vfv[j][:, r, :, h, :]: e.dma_start(out=o, in_=i),
                           reads=[("vful", l, j)], writes=[("v", h % 2, j * 2 + r)])

        zt, k_zt = T.get([128, 512], F32)

        def za_part(hp, tq):
            za_sb, k_za = za2[hp % 2]

            def f_za(e):
                ins = None
                for j in range(4):
                    t = tq * 4 + j
                    for k in range(8):
                        ins = e.matmul(bank(1)[:, j * 128:(j + 1) * 128], lhsT=hT[:, k, t * 128:(t + 1) * 128], rhs=wst[:, 1, k, hp * 128:(hp + 1) * 128], start=(k == 0), stop=(k == 7))
                return ins
            sc.op("pe", f_za, reads=[("hT", tq * 4 + j) for j in range(4)] + wst1_keys, writes=[bk(1)])
            sc.op("act", lambda e: e.activation(out=zt, in_=bank(1), func=AF.Exp, scale=-1.0), reads=[bk(1)], writes=k_zt)
            sc.op("dve", lambda e: e.tensor_scalar(out=zt, in0=zt, scalar1=1.0, scalar2=None, op0=ALU.add), reads=k_zt, writes=k_zt)
            sc.op("dve", lambda e: e.reciprocal(out=zt, in_=zt), reads=k_zt, writes=k_zt)
            sc.op("dve", lambda e: e.tensor_tensor(out=za_sb[:, tq * 4:(tq + 1) * 4, :], in0=bank(1).rearrange("p (j c) -> p j c", c=128),
                                                   in1=zt.rearrange("p (j c) -> p j c", c=128), op=ALU.mult),
                  reads=[bk(1)] + list(k_zt), writes=[("za", hp % 2, tq)] + list(k_za))

        def trg_part(hp, tq):
            og, k_og = og2[hp % 2]
            pg = bank(1).bitcast(BF16)[:, 0:512].rearrange("p (j c) -> p j c", c=128)

            def f_trg(e):
                ins = None
                for j in range(4):
                    ins = e.transpose(out=pg[:, j, :], in_=og[:, tq * 4 + j, :], identity=identb)
                return ins
            sc.op("pe", f_trg, reads=[("og", hp % 2, tq * 4 + j) for j in range(4)] + list(k_og) + ["identb"], writes=[bk(1)])
            sc.op("dve", lambda e: e.tensor_copy(out=ogT[:, hp, tq * 512:(tq + 1) * 512], in_=bank(1).bitcast(BF16)[:, 0:512]),
                  reads=[bk(1)], writes=[("ogT", hp, tq)])

        def emit_qk(i):
            h, qt, kp = its[i]
            if h == 0 and qt == 0 and kp in (1, 3, 5, 7):
                za_part(0, (kp - 1) // 2)
            if h % 2 == 1 and kp == 4 and h + 1 < H:
                za_part((h + 1) // 2, qt)
            if h % 2 == 0 and h >= 2 and kp == 10:
                trg_part(h // 2 - 1, qt)
            kT_b = kv_bufs[h % 2][0]
            sd, b0 = SD[i % 3]
            pslot = i % NPT

            def f_qk(e, sd=sd, kT_b=kT_b, h=h, qt=qt, kp=kp):
                ins = None
                for u in range(2):
                    kt = 2 * kp + u
                    ins = e.matmul(sd[:, u * 512:(u + 1) * 512], lhsT=kT_b[0:DK, kt * 128:(kt + 1) * 128], rhs=qT[0:DK, h, qt * 512:(qt + 1) * 512], start=True, stop=True)
                return ins
            sc.op("pe", f_qk, reads=kvk(h) + [("qT", qt * 4 + j) for j in range(4)], writes=[bk(b0), bk(b0 + 1)])
            pT_s, k_pT_s = pT_slots[pslot]
            sc.op("act", lambda e, o=pT_s, i=sd: e.activation(out=o, in_=i, func=AF.Exp, scale=scale),
                  reads=[bk(b0), bk(b0 + 1)], writes=[("pT", pslot)] + list(k_pT_s))

        def emit_pv(i):
            h, qt, kp = its[i]
            hp, hh = h // 2, h % 2
            v_b = kv_bufs[h % 2][2]
            pslot = i % NPT
            ob = 0
            osl = qt % 2

            pT_s, k_pT_s = pT_slots[pslot]

            def f_pv(e, ob=ob, v_b=v_b, pT_s=pT_s, kp=kp):
                ins = None
                for u in range(2):
                    kt = 2 * kp + u
                    ins = e.matmul(bank(ob)[0:65, :], lhsT=v_b[:, kt, :], rhs=pT_s[:, u * 512:(u + 1) * 512], start=(kt == 0), stop=(kt == 31))
                return ins
            sc.op("pe", f_pv, reads=kvk(h) + [("pT", pslot)] + list(k_pT_s), writes=[bk(ob)])
            if kp != 15:
                return
            if qt == 3 and h + 2 < H:
                kv_load(h + 2)
            za_sb, k_za = za2[hp % 2]
            og, k_og = og2[hp % 2]
            oacc, k_oacc = oacc2[osl]
            sc.op("dve", lambda e, o=oacc[0:65, :], i=bank(ob)[0:65, :]: e.tensor_copy(out=o, in_=i),
                  reads=[bk(ob)], writes=[("oacc", osl)] + list(k_oacc))
            po = bank(1)[:, 0:260].rearrange("p (j c) -> p j c", c=65)

            def f_tro(e, oacc=oacc, po=po):
                ins = None
                for j in range(4):
                    ins = e.transpose(out=po[:, j, :], in_=oacc[0:65, j * 128:(j + 1) * 128], identity=identf[0:65, 0:65])
                return ins
            sc.op("pe", f_tro, reads=[("oacc", osl), "identf"] + list(k_oacc), writes=[bk(1)])
            sc.op("dve", lambda e, o=rinv[:, 0:4], i=po[:, :, 64]: e.reciprocal(out=o, in_=i),
                  reads=[bk(1)], writes=k_rinv)
            for j in range(4):
                t = qt * 4 + j
                sc.op("dve", lambda e, o=og[:, t, hh * 64:(hh + 1) * 64], i=po[:, j, 0:64], s_=rinv[:, j:j + 1], z=za_sb[:, t, hh * 64:(hh + 1) * 64]:
                      e.scalar_tensor_tensor(out=o, in0=i, scalar=s_, in1=z, op0=ALU.mult, op1=ALU.mult),
                      reads=[bk(1), ("za", hp % 2, qt)] + list(k_rinv) + list(k_za), writes=[("og", hp % 2, t)] + list(k_og))

        kv_load(0)
        kv_load(1)
        for i in range(LOOK):
            emit_qk(i)
        for i in range(len(its)):
            if i + LOOK < len(its):
                emit_qk(i + LOOK)
            emit_pv(i)
        for tq in range(4):
            trg_part(H // 2 - 1, tq)

        chk("B")
        for k in range(8):
            load_w(wst[:, 0, k, :], wv[:, k, C_ZF:C_ZF + 512], 512, [("wst", 0, k)])
        T.reset()
        AR.reset()
        szf, k_szf = T.get([128, 2, 512], BF16)
        NBUF = 3
        dbufs = [AR.get([128, 4096], BF16) for _ in range(NBUF)]
        abufs = [T.get([128, 4, 1024], BF16) for _ in range(NBUF)]
        fscale = float(S * 128) ** -0.5
        di = 0
        for kt in range(4):
            for scg in range(8):
                b = di % NBUF
                di += 1
                dbuf, k_dbuf = dbufs[b]
                abuf, k_abuf = abufs[b]
                sc.dma("sp", lambda e, o=dbuf, i=dft_d[kt, scg]: e.dma_start(out=o, in_=i),
                       writes=[("dbuf", b)] + list(k_dbuf))
                a_src = aful[l][scg % 4].ap()[(scg // 4) * 512:(scg // 4 + 1) * 512, :].rearrange("(sci p) n -> p sci n", p=128)
                sc.dma("sp", lambda e, o=abuf, i=a_src: e.dma_start(out=o, in_=i),
                       reads=[("aful", l, scg % 4)], writes=[("abuf", b)] + list(k_abuf))
                d4 = dbuf.rearrange("p (sci cs k) -> p sci cs k", cs=2, k=512)
                a5 = abuf.rearrange("p sci (g cs m) -> p sci g cs m", cs=2, m=128)

                def f_f(e, scg=scg, d4=d4, a5=a5):
                    ins = None
                    for sci in range(4):
                        for g in range(4):
                            for cs in range(2):
                                first = (scg == 0 and sci == 0 and cs == 0)
                                last = (scg == 7 and sci == 3 and cs == 1)
                                ins = e.matmul(bank(g), lhsT=a5[:, sci, g, cs, :], rhs=d4[:, sci, cs, :], start=first, stop=last)
                    return ins
                sc.op("pe", f_f, reads=[("dbuf", b), ("abuf", b)] + list(k_dbuf) + list(k_abuf), writes=[bk(0), bk(1), bk(2), bk(3)])
            for g in range(4):
                pb = 4 + (g % 2)
                zs = g % 2

                def f_zf(e, g=g, kt=kt, pb=pb):
                    ins = None
                    for k in range(8):
                        ins = e.matmul(bank(pb), lhsT=wst[:, 0, k, g * 128:(g + 1) * 128], rhs=hT[:, k, kt * 512:(kt + 1) * 512], start=(k == 0), stop=(k == 7))
                    return ins
                sc.op("pe", f_zf, reads=[("hT", kt * 4 + j) for j in range(4)] + wst0_keys, writes=[bk(pb)])
                sc.op("act", lambda e, o=szf[:, zs, :], i=bank(pb): e.activation(out=o, in_=i, func=AF.Silu),
                      reads=[bk(pb)], writes=[("szf", zs)] + list(k_szf))
                sc.op("dve", lambda e, o=fgT[:, g, kt * 512:(kt + 1) * 512], i=bank(g), z=szf[:, zs, :]:
                      e.scalar_tensor_tensor(out=o, in0=i, scalar=fscale, in1=z, op0=ALU.mult, op1=ALU.mult),
                      reads=[bk(g), ("szf", zs)] + list(k_szf), writes=[("fgT", g, kt)])

        chk("C")
        T.reset()
        AR.reset()
        wat_sb, k_wat = AR.get([128, 4, D], BF16)
        wfo_sb, k_wfo = AR.get([128, 4, D], BF16)
        wo_sb, k_wo = AR.get([128, 8, D], BF16)
        wat_v = w_attn[l].rearrange("(k p) n -> p k n", p=128)
        wfo_v = w_four[l].rearrange("(k p) n -> p k n", p=128)
        wo_v = w_out[l].rearrange("(k p) n -> p k n", p=128)
        for k in range(4):
            load_w(wat_sb[:, k, :], wat_v[:, k, :], D, [("wat", k)] + list(k_wat), eng="dve")
            load_w(wfo_sb[:, k, :], wfo_v[:, k, :], D, [("wfo", k)] + list(k_wfo), eng="dve")
        sg_slots = [T.get([128, 512], F32) for _ in range(4)]
        t1_slots = [T.get([128, 512], F32) for _ in range(2)]
        mT = qT
        for j in range(8):
            ws = j % 2
            wk_all = [("wst", ws, k) for k in range(8)]
            load_w(wst[:, ws, :, 0:128], wv[:, :, C_GA + j * 128:C_GA + (j + 1) * 128], 1024, wk_all, inner=128)
            load_w(wst[:, ws, :, 128:256], wv[:, :, C_GF + j * 128:C_GF + (j + 1) * 128], 1024, wk_all, inner=128)
            wkeys = [("wst", ws, k) for k in range(8)]
            load_w(wo_sb[:, j, :], wo_v[:, j, :], D, [("wo", j)] + list(k_wo), eng="dve")
            for tt in range(4):
                st_ = (j * 4 + tt) % 2
                b0 = 4 * st_

                def f_g(e, ws=ws, tt=tt, b0=b0, j=j):
                    ins = None
                    for a in range(2):
                        for k in range(8):
                            ins = e.matmul(bank(b0 + a), lhsT=wst[:, ws, k, a * 128:(a + 1) * 128], rhs=hT[:, k, tt * 512:(tt + 1) * 512], start=(k == 0), stop=(k == 7))
                    for k in range(4):
                        ins = e.matmul(bank(b0 + 2), lhsT=wat_sb[:, k, j * 128:(j + 1) * 128], rhs=ogT[:, k, tt * 512:(tt + 1) * 512], start=(k == 0), stop=(k == 3))
                    for k in range(4):
                        ins = e.matmul(bank(b0 + 3), lhsT=wfo_sb[:, k, j * 128:(j + 1) * 128], rhs=fgT[:, k, tt * 512:(tt + 1) * 512], start=(k == 0), stop=(k == 3))
                    return ins
                sc.op("pe", f_g,
                      reads=wkeys + [("hT", tt * 4 + i) for i in range(4)] + [("wat", k) for k in range(4)] + [("wfo", k) for k in range(4)]
                      + list(k_wat) + list(k_wfo) + [("ogT", k, tt) for k in range(4)] + [("fgT", k, tt) for k in range(4)],
                      writes=[bk(b0 + i) for i in range(4)] + [("qT", tt * 4 + i) for i in range(0)])
                sga, k_sga = sg_slots[st_ * 2]
                sgf, k_sgf = sg_slots[st_ * 2 + 1]
                t1s, k_t1s = t1_slots[st_]
                for a, (sgx, k_sgx) in enumerate(((sga, k_sga), (sgf, k_sgf))):
                    sc.op("act", lambda e, o=sgx, i=bank(b0 + a), b_=bm[:, a * 8 + j:a * 8 + j + 1]: e.activation(out=o, in_=i, func=AF.Sigmoid, bias=b_),
                          reads=[bk(b0 + a), "bm"], writes=list(k_sgx))
                sc.op("dve", lambda e, o=t1s, i=bank(b0 + 2), z=sga: e.tensor_tensor(out=o, in0=i, in1=z, op=ALU.mult),
                      reads=[bk(b0 + 2)] + list(k_sga), writes=list(k_t1s))
                sc.op("dve", lambda e, o=sgf, i=bank(b0 + 3), z=sgf: e.tensor_tensor(out=o, in0=i, in1=z, op=ALU.mult),
                      reads=[bk(b0 + 3)] + list(k_sgf), writes=list(k_sgf))
                sc.op("dve", lambda e, o=mT[:, j, tt * 512:(tt + 1) * 512], i=t1s, z=sgf: e.tensor_tensor(out=o, in0=i, in1=z, op=ALU.add),
                      reads=list(k_sgf) + list(k_t1s) + [("qT", tt * 4 + i) for i in range(4)],
                      writes=[("mT", j, tt)] + [("qT", tt * 4 + i) for i in range(4)])
        xo, k_xo = T.get([128, 2, D], F32)
        load_x(0)
        for t in range(NT):
            xs_ = xt[:, t % 2, :]
            kx = ("xt", t % 2)
            if t + 1 < NT:
                load_x(t + 1)
            ps2 = psQ if t % 2 == 0 else psK
            pkeys = [bk(0), bk(1)] if t % 2 == 0 else [bk(2), bk(3)]

            def f_o(e, t=t, ps2=ps2):
                ins = None
                for hf in range(2):
                    for k in range(8):
                        ins = e.matmul(ps2[:, hf * 512:(hf + 1) * 512], lhsT=mT[:, k, t * 128:(t + 1) * 128], rhs=wo_sb[:, k, hf * 512:(hf + 1) * 512], start=(k == 0), stop=(k == 7))
                return ins
            sc.op("pe", f_o, reads=[("mT", k, t // 4) for k in range(8)] + [("qT", t)] + [("wo", k) for k in range(8)] + list(k_wo), writes=pkeys)
            xs2 = t % 2
            sc.op("dve", lambda e, o=xo[:, xs2, :], i=ps2, z=xs_: e.tensor_tensor(out=o, in0=i, in1=z, op=ALU.add),
                  reads=pkeys + [kx] + list(k_xo), writes=[("xo", xs2)])
            sc.dma("sp", lambda e, o=xd_v[:, t, :], i=xo[:, xs2, :]: e.dma_start(out=o, in_=i),
                   reads=[("xo", xs2)] + list(k_xo), writes=[("xdst", l, t)])
        if DEBUG and l == 0:
            sc.dma("sp", lambda e: e.dma_start(out=dbg_og, in_=ogT), reads=[("ogT", a, b) for a in range(4) for b in range(4)], writes=["dbg_og"])
            sc.dma("sp", lambda e: e.dma_start(out=dbg_fg, in_=fgT), reads=[("fgT", a, b) for a in range(4) for b in range(4)], writes=["dbg_fg"])
            sc.dma("sp", lambda e: e.dma_start(out=dbg_m, in_=qT), reads=[("mT", a, b) for a in range(8) for b in range(4)] + [("qT", t) for t in range(NT)], writes=["dbg_m"])
    except _Stop:
        pass
    if DEBUG:
        sc.final_wait("sp", ["dbg_og", "dbg_fg", "dbg_m"])
    sc.final_wait("sp", [("xdst", L - 1, t) for t in range(NT)])

    with nc.Block() as block:
        @block.tensor
        def _(e):
            sc.replay("pe", e)

        @block.scalar
        def _(e):
            sc.replay("act", e)

        @block.vector
        def _(e):
            sc.replay("dve", e)

        @block.gpsimd
        def _(e):
            sc.replay("pool", e)

        @block.sync
        def _(e):
            sc.replay("sp", e)
    return nc


_CACHE = {}


def _tables(half):
    key = ("tab", half)
    if key in _CACHE:
        return _CACHE[key]
    hd = 16
    inv_freq = (10000.0 ** (-np.arange(hd, dtype=np.float32) / hd)).astype(np.float32)
    pos = (half * NTOK + np.arange(NTOK, dtype=np.float32)).astype(np.float32)
    ang = (pos[:, None] * inv_freq[None, :]).astype(np.float32)
    cs = np.concatenate([np.cos(ang), np.sin(ang)], axis=1).astype(np.float32)
    rope = np.ascontiguousarray(cs.reshape(NT, 128, 32).transpose(1, 0, 2))
    s_idx = np.arange(S, dtype=np.int64).reshape(8, 4, 128)
    k_idx = (half * NTOK + np.arange(NTOK, dtype=np.int64)).reshape(4, 512)
    prod = (s_idx[None, :, :, :, None] * k_idx[:, None, None, None, :]) % S
    th = (2.0 * np.pi / S) * prod.astype(np.float64)
    c = np.cos(th)
    sn = -np.sin(th)
    tab = np.stack([c, sn], axis=4)
    tab = tab.transpose(0, 1, 3, 2, 4, 5)
    dft = np.ascontiguousarray(tab.reshape(4, 8, 128, 4096)).astype(ml_dtypes.bfloat16)
    _CACHE[key] = (rope, dft)
    return rope, dft


def _consts():
    if "c" in _CACHE:
        return _CACHE["c"]
    c_i = np.arange(128, dtype=np.int64)
    th = (2.0 * np.pi / 128) * ((c_i[:, None] * c_i[None, :]) % 128).astype(np.float64)
    ccsc = np.concatenate([np.cos(th), np.sin(th)], axis=1).astype(ml_dtypes.bfloat16)
    ib = np.eye(128, dtype=np.float32).astype(ml_dtypes.bfloat16)
    i_f = np.eye(128, dtype=np.float32)
    _CACHE["c"] = (ccsc, ib, i_f)
    return _CACHE["c"]


STOP = None
DEBUG = False


def _get_nc(L):
    key = ("nc", L, STOP)
    if key not in _CACHE:
        _CACHE[key] = build_program(L, STOP)
    return _CACHE[key]


def _layer_params(norm_g, q_latent_g, kv_latent_g, q_head_g, k_head_g, b_merge, ls):
    L = len(ls)
    ng = np.stack([np.ascontiguousarray(norm_g[l].reshape(8, 128).T) for l in ls])
    gl = np.stack([np.ascontiguousarray(np.concatenate([q_latent_g[l].reshape(2, 128), kv_latent_g[l].reshape(2, 128)], 0).T) for l in ls])
    gqr = np.stack([np.ascontiguousarray(np.broadcast_to(q_head_g[l][None, :], (128, DK))) for l in ls])
    gkr = np.stack([np.ascontiguousarray(np.broadcast_to(k_head_g[l][None, :], (128, DK))) for l in ls])
    bmt = np.stack([np.ascontiguousarray(b_merge[l].reshape(2, 8, 128).transpose(2, 0, 1).reshape(128, 16)) for l in ls])
    return ng.astype(np.float32), gl.astype(np.float32), gqr.astype(np.float32), gkr.astype(np.float32), bmt.astype(np.float32)


FUSED_LAYERS = 4


def kernel(x, norm_g, w_in, q_latent_g, kv_latent_g, w_uq, w_ukv, q_head_g, k_head_g,
           w_attn_proj, w_fourier_proj, b_merge, w_out):
    f = lambda a: np.ascontiguousarray(np.asarray(a, dtype=np.float32))
    x, norm_g, w_in, q_latent_g, kv_latent_g = f(x), f(norm_g), f(w_in), f(q_latent_g), f(kv_latent_g)
    w_uq, w_ukv, q_head_g, k_head_g = f(w_uq), f(w_ukv), f(q_head_g), f(k_head_g)
    w_attn_proj, w_fourier_proj, b_merge, w_out = f(w_attn_proj), f(w_fourier_proj), f(b_merge), f(w_out)
    depth = w_in.shape[0]
    ccsc, ib, i_f = _consts()
    cur = [np.ascontiguousarray(x[c // 2, (c % 2) * NTOK:(c % 2 + 1) * NTOK, :]) for c in range(8)]
    step = FUSED_LAYERS
    for l0 in range(0, depth, step):
        ls = list(range(l0, min(l0 + step, depth)))
        nc = _get_nc(len(ls))
        ng, gl, gqr, gkr, bmt = _layer_params(norm_g, q_latent_g, kv_latent_g, q_head_g, k_head_g, b_merge, ls)
        sl = slice(ls[0], ls[-1] + 1)
        in_maps = []
        for c in range(8):
            rope, dft = _tables(c % 2)
            in_maps.append({
                "x": cur[c], "w_in": w_in[sl], "w_uq": w_uq[sl], "w_ukv": w_ukv[sl],
                "w_attn": w_attn_proj[sl], "w_four": w_fourier_proj[sl], "w_out": w_out[sl],
                "norm_gT": ng, "glatT": gl, "gq_rep": gqr, "gk_rep": gkr, "bmT": bmt,
                "rope_cs": rope, "dft": dft, "ccsc": ccsc, "ident_bf": ib, "ident_f": i_f,
            })
        res = run_bass_kernel_spmd(nc, in_maps, core_ids=list(range(8)))
        cur = [np.asarray(res.results[c]["y"], dtype=np.float32) for c in range(8)]
        if DEBUG:
            _CACHE["dbg"] = [{k: np.asarray(res.results[c][k]) for k in ("dbg_og", "dbg_fg", "dbg_m")} for c in range(8)]
    out = np.empty((4, S, D), dtype=np.float32)
    for c in range(8):
        out[c // 2, (c % 2) * NTOK:(c % 2 + 1) * NTOK, :] = cur[c]
    return out
```

```python
import numpy as np
import ml_dtypes
import concourse.bass as bass
import concourse.mybir as mybir
from concourse.bass_utils import run_bass_kernel_spmd

F32 = mybir.dt.float32
BF16 = mybir.dt.bfloat16
AF = mybir.ActivationFunctionType
ALU = mybir.AluOpType
AX = mybir.AxisListType

D = 1024
S = 4096
NTOK = 2048
NT = NTOK // 128
H = 8
DK = 96
DV = 64
INW = 4128
EPS = 1e-6
C_CQ, C_KPE, C_ZA, C_UF, C_ZF, C_GA, C_GF = 0, 512, 544, 1056, 1568, 2080, 3104
ENGS = ("pe", "act", "dve", "pool", "sp")
NDMA = 8


class Sched:
    def __init__(self, nc):
        self.nc = nc
        self.streams = {e: [] for e in ENGS}
        self.sems = {}
        self.cur = {}
        self.cnt = {}
        self.waited = {e: {} for e in ENGS}
        self.lastw = {}
        self.readers = {}
        self.dma_gen = {q: [0] * NDMA for q in ("sp", "pool")}
        self.dma_rr = {q: 0 for q in ("sp", "pool")}
        self.epoch = -1
        self.ncc = 0
        for q in ("sp", "pool"):
            for i in range(NDMA):
                self._sem(f"dma_{q}_{i}")
        self.new_epoch()

    def _sem(self, name):
        self.sems[name] = self.nc.alloc_semaphore(name)
        return name

    def new_epoch(self):
        self.epoch += 1
        for e in ("pe", "act", "dve", "pool"):
            self.cur[e] = self._sem(f"c_{e}_{self.epoch}")
            self.cnt[e] = 0

    def _deps(self, reads, writes):
        deps = {}

        def add(tok):
            if tok is not None:
                n, v = tok
                if v > deps.get(n, 0):
                    deps[n] = v
        for k in reads:
            add(self.lastw.get(k))
        for k in writes:
            add(self.lastw.get(k))
            for n, v in self.readers.get(k, {}).items():
                add((n, v))
        return deps

    def _emit(self, eng, deps, fn, semname, inc):
        waits = []
        w = self.waited[eng]
        for n, v in deps.items():
            if v > w.get(n, 0):
                waits.append((n, v))
                w[n] = v
        self.streams[eng].append((waits, fn, semname, inc))

    def _mark(self, tok, reads, writes):
        n, v = tok
        for k in reads:
            self.readers.setdefault(k, {})[n] = v
        for k in writes:
            self.lastw[k] = tok
            self.readers[k] = {}

    def op(self, eng, fn, reads=(), writes=()):
        deps = self._deps(reads, writes)
        self.cnt[eng] += 1
        tok = (self.cur[eng], self.cnt[eng])
        self._emit(eng, deps, fn, tok[0], 1)
        self._mark(tok, reads, writes)

    def dma(self, q, fn, reads=(), writes=()):
        deps = self._deps(reads, writes)
        i = self.dma_rr[q]
        self.dma_rr[q] = (i + 1) % NDMA
        name = f"dma_{q}_{i}"
        gen = self.dma_gen[q][i]
        if gen > 0:
            deps[name] = max(deps.get(name, 0), 16 * gen)
        self.dma_gen[q][i] = gen + 1
        tok = (name, 16 * (gen + 1))
        self._emit(q, deps, fn, name, 16)
        self._mark(tok, reads, writes)

    def collective(self, slot, fn, reads=(), writes=()):
        deps = self._deps(reads, writes)
        name = f"cc_{slot}"
        if name not in self.sems:
            self._sem(name)
            self.cc_gen = getattr(self, "cc_gen", {})
            self.cc_gen[name] = 0
        self.cc_gen[name] += 1
        tok = (name, self.cc_gen[name])
        self._emit("pool", deps, fn, name, 1)
        self._mark(tok, reads, writes)

    def final_wait(self, eng, keys):
        deps = self._deps(keys, ())
        self._emit(eng, deps, None, None, 0)

    def replay(self, eng, e):
        for waits, fn, semname, inc in self.streams[eng]:
            for n, v in waits:
                e.wait_ge(self.sems[n], v)
            if fn is not None:
                ins = fn(e)
                ins.then_inc(self.sems[semname], inc)


class _Stop(Exception):
    pass


def build_program(L, stop=None):
    nc = bass.Bass("TRN2", target_bir_lowering=False)
    sc = Sched(nc)

    def din(name, shape, dt=F32):
        return nc.dram_tensor(name, shape, dt, kind="ExternalInput").ap()

    x_in = din("x", [NTOK, D])
    w_in = din("w_in", [L, D, INW])
    w_uq = din("w_uq", [L, 256, H * DK])
    w_ukv = din("w_ukv", [L, 256, H * 128])
    w_attn = din("w_attn", [L, 512, D])
    w_four = din("w_four", [L, 512, D])
    w_out = din("w_out", [L, D, D])
    normg_d = din("norm_gT", [L, 128, 8])
    glat_d = din("glatT", [L, 128, 4])
    gq_d = din("gq_rep", [L, 128, DK])
    gk_d = din("gk_rep", [L, 128, DK])
    bm_d = din("bmT", [L, 128, 16])
    rope_d = din("rope_cs", [128, NT, 32])
    dft_d = din("dft", [4, 8, 128, 4096], BF16)
    ccsc_d = din("ccsc", [128, 256], BF16)
    identb_d = din("ident_bf", [128, 128], BF16)
    identf_d = din("ident_f", [128, 128])
    y_out = nc.dram_tensor("y", [NTOK, D], F32, kind="ExternalOutput").ap()
    if DEBUG:
        dbg_og = nc.dram_tensor("dbg_og", [128, 4, NTOK], BF16, kind="ExternalOutput").ap()
        dbg_fg = nc.dram_tensor("dbg_fg", [128, 4, NTOK], BF16, kind="ExternalOutput").ap()
        dbg_m = nc.dram_tensor("dbg_m", [128, 8, NTOK], BF16, kind="ExternalOutput").ap()

    kloc = [[nc.dram_tensor(f"kloc{l}_{j}", [H * DK, 512], BF16) for j in range(4)] for l in range(L)]
    kful = [[nc.dram_tensor(f"kful{l}_{j}", [2 * H * DK, 512], BF16) for j in range(4)] for l in range(L)]
    vloc = [[nc.dram_tensor(f"vloc{l}_{j}", [512, H * 65], BF16) for j in range(4)] for l in range(L)]
    vful = [[nc.dram_tensor(f"vful{l}_{j}", [2 * 512, H * 65], BF16) for j in range(4)] for l in range(L)]
    aloc = [[nc.dram_tensor(f"aloc{l}_{j}", [512, 1024], BF16) for j in range(4)] for l in range(L)]
    aful = [[nc.dram_tensor(f"aful{l}_{j}", [2 * 512, 1024], BF16) for j in range(4)] for l in range(L)]
    xbuf = [nc.dram_tensor(f"xbuf{l}", [NTOK, D], F32) for l in range(max(L - 1, 1))]

    def sb(name, shape, dt):
        return nc.alloc_sbuf_tensor(name, shape, dt).ap()

    hT = sb("hT", [128, 8, NTOK], BF16)
    qT = sb("qT", [128, 8, NTOK], BF16)
    ogT = sb("ogT", [128, 4, NTOK], BF16)
    fgT = sb("fgT", [128, 4, NTOK], BF16)
    arena = sb("arena", [128, 16384], BF16)
    wst = sb("wst", [128, 2, 8, 512], BF16)
    stage = sb("stage", [128, 2, 1024], F32)
    xt = sb("xt", [128, 2, D], F32)
    tmp = sb("tmp", [128, 18432], BF16)
    identb = sb("identb", [128, 128], BF16)
    identf = sb("identf", [128, 128], F32)
    rope = sb("rope", [128, NT, 32], F32)
    ccsc = sb("ccsc_sb", [128, 256], BF16)
    normg = sb("normg", [128, 8], F32)
    glat = sb("glat", [128, 4], F32)
    gq = sb("gq", [128, DK], F32)
    gk = sb("gk", [128, DK], F32)
    bm = sb("bm", [128, 16], F32)
    stats = sb("stats", [128, 64], F32)
    eps_t = sb("eps_t", [128, 1], F32)

    psQ = nc.alloc_psum_tensor("psQ", [128, 1024], F32).ap()
    psK = nc.alloc_psum_tensor("psK", [128, 1024], F32).ap()
    psS = [nc.alloc_psum_tensor(f"psS{i}", [128, 1024], F32).ap() for i in range(2)]

    def bank(i):
        if i < 2:
            return psQ[:, i * 512:(i + 1) * 512]
        if i < 4:
            return psK[:, (i - 2) * 512:(i - 1) * 512]
        return psS[(i - 4) // 2][:, ((i - 4) % 2) * 512:((i - 4) % 2 + 1) * 512]

    def bk(i):
        return ("ps", i)

    class Tmp:
        def __init__(self, base_ap, prefix, nbytes):
            self.base, self.prefix, self.nbytes = base_ap, prefix, nbytes
            self.off = 0

        def reset(self):
            self.off = 0

        def get(self, shape, dt):
            es = 4 if dt == F32 else 2
            n = int(np.prod(shape[1:]))
            nb = (n * es + 63) // 64 * 64
            a, b = self.off, self.off + nb
            assert b <= self.nbytes, (self.prefix, b)
            self.off = b
            ap = self.base[:, a // 2:(a + n * es) // 2]
            if dt == F32:
                ap = ap.bitcast(F32)
            if len(shape) == 3:
                ap = ap.rearrange("p (a b) -> p a b", b=shape[2])
            ap = ap[0:shape[0]]
            keys = tuple((self.prefix, i) for i in range(a // 1024, (b - 1) // 1024 + 1))
            return ap, keys

    T = Tmp(tmp, "tmp", 36864)
    AR = Tmp(arena, "arena", 32768)

    stage_rr = [0]

    def load_w(dst, src, n, dst_keys, inner=None, eng="pool"):
        s = stage_rr[0]
        stage_rr[0] ^= 1
        st_ap = stage[:, s, 0:n]
        if inner is not None:
            st_ap = st_ap.rearrange("p (a b) -> p a b", b=inner)
        sc.dma("sp", lambda e, o=st_ap, i=src: e.dma_start(out=o, in_=i), writes=[("stage", s)])
        sc.op(eng, lambda e, o=dst, i=st_ap: e.tensor_copy(out=o, in_=i),
              reads=[("stage", s)], writes=dst_keys)

    def load_small(dst, src, key):
        sc.dma("sp", lambda e, o=dst, i=src: e.dma_start(out=o, in_=i), writes=[key])

    sc.op("dve", lambda e: e.memset(eps_t, EPS), writes=["eps_t"])
    load_small(identb, identb_d, "identb")
    load_small(identf, identf_d, "identf")
    load_small(rope, rope_d, "rope")
    load_small(ccsc, ccsc_d, "ccsc")

    def w_in_v(l):
        return w_in[l].rearrange("(k p) n -> p k n", p=128)

    def chk(name):
        if stop == name:
            sc.dma("sp", lambda e: e.dma_start(out=y_out, in_=x_in), writes=[("xdst", L - 1, t) for t in range(NT)])
            raise _Stop()

    try:
      chk("init")
      for l in range(L):
        if l > 0:
            sc.new_epoch()
        x_src = x_in if l == 0 else xbuf[l - 1].ap()
        x_dst = y_out if l == L - 1 else xbuf[l].ap()
        xs_v = x_src.rearrange("(t p) d -> p t d", p=128)
        xd_v = x_dst.rearrange("(t p) d -> p t d", p=128)

        load_small(normg, normg_d[l], "normg")
        load_small(glat, glat_d[l], "glat")
        load_small(gq, gq_d[l], "gq")
        load_small(gk, gk_d[l], "gk")
        load_small(bm, bm_d[l], "bm")
        wv = w_in_v(l)
        AR.reset()
        wA, k_wA = AR.get([128, 8, 544], BF16)
        wuq_sb, k_wuq = AR.get([128, 2, H * DK], BF16)
        wukv_sb, k_wukv = AR.get([128, 2, H * 128], BF16)
        uT, k_uT = AR.get([128, 4, 512], BF16)
        atile, k_atile = AR.get([128, 2, 1024], BF16)
        for k in range(8):
            load_w(wA[:, k, :], wv[:, k, 0:544], 544, [("wA", k)] + list(k_wA), eng="dve")
        wuq_v = w_uq[l].rearrange("(k p) n -> p k n", p=128)
        wukv_v = w_ukv[l].rearrange("(k p) n -> p k n", p=128)
        for k in range(2):
            load_w(wuq_sb[:, k, :], wuq_v[:, k, :], H * DK, [("wuq", k)] + list(k_wuq), eng="dve")
            load_w(wukv_sb[:, k, :], wukv_v[:, k, :], H * 128, [("wukv", k)] + list(k_wukv), eng="dve")
        wA_keys = [("wA", k) for k in range(8)] + list(k_wA)

        groups = [[0, 1], [2, 3], [4, 5], [6, 7]]

        def exchange1(nm, loc_l, ful_l, slot0, j):
            sc.collective(slot0 + j, lambda e, i=loc_l[j], o=ful_l[j]: e.collective_compute(
                "AllGather", ALU.bypass, replica_groups=groups, ins=[i.ap().opt()], outs=[o.ap().opt()]),
                reads=[(nm + "loc", l, j * 4 + i) for i in range(4)], writes=[(nm + "ful", l, j)])

        T.reset()
        xsb, k_xsb = T.get([128, D], BF16)
        junk, k_junk = T.get([128, D], BF16)
        junk2, k_junk2 = T.get([128, 32], BF16)
        cn, k_cn = T.get([128, 512], BF16)
        cnT, k_cnT = T.get([128, 4, 128], BF16)
        qsq, k_qsq = T.get([128, 768], F32)
        qn, k_qn = T.get([128, 768], F32)
        qg, k_qg = T.get([128, 768], F32)
        qb, k_qb = T.get([128, 768], BF16)
        ksq, k_ksq = T.get([128, 512], F32)
        knn, k_knn = T.get([128, 512], F32)
        kb, k_kb = T.get([128, 768], BF16)
        kpeg, k_kpeg = T.get([128, 32], F32)
        kpeh, k_kpeh = T.get([128, 256], F32)
        rtq, k_rtq = T.get([128, 4, 128], F32)
        rtk, k_rtk = T.get([128, 4, 128], F32)
        vb, k_vb = T.get([128, 2, H * 65], BF16)
        ktile, k_ktile = T.get([128, 2, 1024], BF16)

        vb4 = vb.rearrange("p s (h c) -> p s h c", c=65)
        sc.op("dve", lambda e, o=vb: e.memset(o, 1.0), writes=k_vb)

        def load_x(t):
            rd = [("xdst", l - 1, t)] if l > 0 else []
            sc.dma("sp", lambda e, o=xt[:, t % 2, :], i=xs_v[:, t, :]: e.dma_start(out=o, in_=i), reads=rd, writes=[("xt", t % 2)])

        qps = psQ[:, 0:768]
        q3 = qps.rearrange("p (h d) -> p h d", d=DK)
        kv3 = psK.rearrange("p (h d) -> p h d", d=128)
        qn3 = qn.rearrange("p (h d) -> p h d", d=DK)
        qg3 = qg.rearrange("p (h d) -> p h d", d=DK)
        qb3 = qb.rearrange("p (h d) -> p h d", d=DK)
        ksq3 = ksq.rearrange("p (h d) -> p h d", d=64)
        knn3 = knn.rearrange("p (h d) -> p h d", d=64)
        kb3 = kb.rearrange("p (h d) -> p h d", d=DK)
        kpeh3 = kpeh.rearrange("p (h d) -> p h d", d=32)
        ST = lambda a, b_=None: stats[:, a:(a + 1 if b_ is None else b_)]

        def tr_heads(e, dst, src):
            ins = None
            for h in range(H):
                ins = e.transpose(out=dst[0:DK, h, :], in_=src[:, h, :], identity=identb)
            return ins

        def chain_S1(t):
            ops = []
            xs_ = xt[:, t % 2, :]
            kx = ("xt", t % 2)
            ops.append(lambda: sc.op("act", lambda e: e.activation(out=junk, in_=xs_, func=AF.Square, accum_out=ST(0)),
                                     reads=[kx], writes=list(k_junk) + [("st", 0)]))
            ops.append(lambda: sc.op("act", lambda e: e.activation(out=ST(2), in_=ST(0), func=AF.Sqrt, scale=1.0 / D, bias=eps_t),
                                     reads=[("st", 0), "eps_t"], writes=[("st", 2)]))
            ops.append(lambda: sc.op("dve", lambda e: e.reciprocal(out=ST(3), in_=ST(2)), reads=[("st", 2)], writes=[("st", 3)]))
            ops.append(lambda: sc.op("act", lambda e: e.activation(out=xsb, in_=xs_, func=AF.Copy, scale=ST(3)),
                                     reads=[kx, ("st", 3)], writes=k_xsb))
            pt = bank(6).bitcast(BF16).rearrange("p (k j) -> p k j", j=128)

            def f_tr(e):
                ins = None
                for k in range(8):
                    ins = e.transpose(out=pt[:, k, :], in_=xsb[:, k * 128:(k + 1) * 128], identity=identb)
                return ins
            ops.append(lambda: sc.op("pe", f_tr, reads=list(k_xsb) + ["identb"], writes=[bk(6)]))
            hT_t = hT[:, :, t * 128:(t + 1) * 128]
            ops.append(lambda: sc.op("dve", lambda e: e.tensor_tensor(out=hT_t, in0=pt, in1=normg.unsqueeze(2).to_broadcast([128, 8, 128]), op=ALU.mult),
                                     reads=[bk(6), "normg"], writes=[("hT", t)]))

            def f_cq(e):
                ins = None
                for k in range(8):
                    ins = e.matmul(bank(4), lhsT=hT[:, k, t * 128:(t + 1) * 128], rhs=wA[:, k, 0:512], start=(k == 0), stop=(k == 7))
                for k in range(8):
                    ins = e.matmul(bank(5)[:, 0:32], lhsT=hT[:, k, t * 128:(t + 1) * 128], rhs=wA[:, k, 512:544], start=(k == 0), stop=(k == 7))
                return ins
            ops.append(lambda: sc.op("pe", f_cq, reads=[("hT", t)] + wA_keys, writes=[bk(4), bk(5)]))
            for j in range(2):
                ops.append(lambda j=j: sc.op("act", lambda e: e.activation(out=junk[:, 0:256], in_=bank(4)[:, j * 256:(j + 1) * 256], func=AF.Square, accum_out=ST(4 + j)),
                                             reads=[bk(4)], writes=list(k_junk) + [("st", 4 + j)]))
            ops.append(lambda: sc.op("act", lambda e: e.activation(out=ST(8, 10), in_=ST(4, 6), func=AF.Sqrt, scale=1.0 / 256, bias=eps_t),
                                     reads=[("st", 4), ("st", 5), "eps_t"], writes=[("st", 8)]))
            ops.append(lambda: sc.op("dve", lambda e: e.reciprocal(out=ST(10, 12), in_=ST(8, 10)), reads=[("st", 8)], writes=[("st", 10)]))
            for j in range(2):
                ops.append(lambda j=j: sc.op("dve", lambda e: e.tensor_scalar(out=cn[:, j * 256:(j + 1) * 256], in0=bank(4)[:, j * 256:(j + 1) * 256], scalar1=ST(10 + j), scalar2=None, op0=ALU.mult),
                                             reads=[bk(4), ("st", 10)], writes=k_cn))
            pc = bank(7).bitcast(BF16)[:, 0:512].rearrange("p (k j) -> p k j", j=128)

            def f_trc(e):
                ins = None
                for k in range(4):
                    ins = e.transpose(out=pc[:, k, :], in_=cn[:, k * 128:(k + 1) * 128], identity=identb)
                return ins
            ops.append(lambda: sc.op("pe", f_trc, reads=list(k_cn) + ["identb"], writes=[bk(7)]))
            ops.append(lambda: sc.op("dve", lambda e: e.tensor_tensor(out=cnT, in0=pc, in1=glat.unsqueeze(2).to_broadcast([128, 4, 128]), op=ALU.mult),
                                     reads=[bk(7), "glat"], writes=k_cnT))

            def f_q(e):
                ins = None
                for k in range(2):
                    ins = e.matmul(bank(0), lhsT=cnT[:, k, :], rhs=wuq_sb[:, k, 0:512], start=(k == 0), stop=(k == 1))
                for k in range(2):
                    ins = e.matmul(bank(1)[:, 0:256], lhsT=cnT[:, k, :], rhs=wuq_sb[:, k, 512:768], start=(k == 0), stop=(k == 1))
                for hf in range(2):
                    for k in range(2):
                        ins = e.matmul(bank(2 + hf), lhsT=cnT[:, 2 + k, :], rhs=wukv_sb[:, k, hf * 512:(hf + 1) * 512], start=(k == 0), stop=(k == 1))
                return ins
            last = lambda: sc.op("pe", f_q, reads=list(k_cnT) + [("wuq", 0), ("wuq", 1), ("wukv", 0), ("wukv", 1)] + list(k_wuq) + list(k_wukv),
                                 writes=[bk(0), bk(1), bk(2), bk(3)])
            return ops, last

        def rope_chain(ops, eng, x1, x2, o1, o2, rt, k_src, k_rt, k_dst, cos_b, sin_b):
            rt4 = rt.rearrange("p a (h d) -> p a h d", d=16)
            ops.append(lambda: sc.op(eng, lambda e: e.tensor_tensor(out=rt4[:, 0], in0=x1, in1=cos_b, op=ALU.mult), reads=list(k_src) + ["rope"], writes=k_rt))
            ops.append(lambda: sc.op(eng, lambda e: e.tensor_tensor(out=rt4[:, 1], in0=x2, in1=sin_b, op=ALU.mult), reads=list(k_src) + ["rope"], writes=k_rt))
            ops.append(lambda: sc.op(eng, lambda e: e.tensor_tensor(out=rt4[:, 2], in0=x2, in1=cos_b, op=ALU.mult), reads=list(k_src) + ["rope"], writes=k_rt))
            ops.append(lambda: sc.op(eng, lambda e: e.tensor_tensor(out=rt4[:, 3], in0=x1, in1=sin_b, op=ALU.mult), reads=list(k_src) + ["rope"], writes=k_rt))
            ops.append(lambda: sc.op(eng, lambda e: e.tensor_tensor(out=o1, in0=rt4[:, 0], in1=rt4[:, 1], op=ALU.subtract), reads=k_rt, writes=k_dst))
            ops.append(lambda: sc.op(eng, lambda e: e.tensor_tensor(out=o2, in0=rt4[:, 2], in1=rt4[:, 3], op=ALU.add), reads=k_rt, writes=k_dst))

        def chain_Sq(t):
            ops = []
            cos_b = rope[:, t, 0:16].unsqueeze(1).to_broadcast([128, H, 16])
            sin_b = rope[:, t, 16:32].unsqueeze(1).to_broadcast([128, H, 16])
            ops.append(lambda: sc.op("act", lambda e: e.activation(out=qsq, in_=qps, func=AF.Square), reads=[bk(0), bk(1)], writes=k_qsq))
            ops.append(lambda: sc.op("dve", lambda e: e.tensor_reduce(out=ST(16, 24), in_=qsq.rearrange("p (h d) -> p h d", d=DK), axis=AX.X, op=ALU.add),
                                     reads=k_qsq, writes=[("st", 16)]))
            ops.append(lambda: sc.op("act", lambda e: e.activation(out=ST(32, 40), in_=ST(16, 24), func=AF.Sqrt, scale=1.0 / DK, bias=eps_t),
                                     reads=[("st", 16), "eps_t"], writes=[("st", 32)]))
            ops.append(lambda: sc.op("dve", lambda e: e.reciprocal(out=ST(40, 48), in_=ST(32, 40)), reads=[("st", 32)], writes=[("st", 40)]))
            ops.append(lambda: sc.op("dve", lambda e: e.tensor_tensor(out=qn3, in0=q3, in1=ST(40, 48).unsqueeze(2).to_broadcast([128, H, DK]), op=ALU.mult),
                                     reads=[bk(0), bk(1), ("st", 40)], writes=k_qn))
            ops.append(lambda: sc.op("pool", lambda e: e.tensor_tensor(out=qg3, in0=qn3, in1=gq.unsqueeze(1).to_broadcast([128, H, DK]), op=ALU.mult),
                                     reads=list(k_qn) + ["gq"], writes=k_qg))
            ops.append(lambda: sc.op("pool", lambda e: e.tensor_copy(out=qb3[:, :, 0:64], in_=qg3[:, :, 0:64]), reads=k_qg, writes=k_qb))
            rope_chain(ops, "pool", qg3[:, :, 64:80], qg3[:, :, 80:96], qb3[:, :, 64:80], qb3[:, :, 80:96], rtq, k_qg, k_rtq, k_qb, cos_b, sin_b)
            pq = bank(6).bitcast(BF16).rearrange("p (h j) -> p h j", j=128)
            ops.append(lambda: sc.op("pe", lambda e: tr_heads(e, pq, qb3), reads=list(k_qb) + ["identb"], writes=[bk(6)]))
            ops.append(lambda: sc.op("act", lambda e: e.activation(out=qT[0:DK, :, t * 128:(t + 1) * 128], in_=pq[0:DK], func=AF.Copy),
                                     reads=[bk(6)], writes=[("qT", t)]))
            return ops

        def chain_Sk(t):
            ops = []
            cos_b = rope[:, t, 0:16].unsqueeze(1).to_broadcast([128, H, 16])
            sin_b = rope[:, t, 16:32].unsqueeze(1).to_broadcast([128, H, 16])
            vs = t % 2
            kvs = ("vb", vs)
            rk = ST(56, 64)
            ops.append(lambda: sc.op("act", lambda e: e.activation(out=ksq3, in_=kv3[:, :, 0:64], func=AF.Square), reads=[bk(2), bk(3)], writes=k_ksq))
            ops.append(lambda: sc.op("act", lambda e: e.activation(out=junk2, in_=bank(5)[:, 0:32], func=AF.Square, accum_out=ST(12)),
                                     reads=[bk(5)], writes=list(k_junk2) + [("st", 12)]))
            ops.append(lambda: sc.op("dve", lambda e: e.tensor_tensor(out=kpeg, in0=bank(5)[:, 0:32], in1=gk[:, 64:96], op=ALU.mult),
                                     reads=[bk(5), "gk"], writes=k_kpeg))
            ops.append(lambda: sc.op("act", lambda e: e.activation(out=vb4[:, vs, :, 0:64], in_=kv3[:, :, 64:128], func=AF.Copy),
                                     reads=[bk(2), bk(3)] + list(k_vb), writes=[kvs]))
            ops.append(lambda: sc.op("dve", lambda e: e.tensor_reduce(out=ST(48, 56), in_=ksq3, axis=AX.X, op=ALU.add), reads=k_ksq, writes=[("st", 48)]))
            ops.append(lambda: sc.op("dve", lambda e: e.tensor_scalar(out=ST(48, 56), in0=ST(48, 56), scalar1=ST(12), scalar2=None, op0=ALU.add),
                                     reads=[("st", 48), ("st", 12)], writes=[("st", 48)]))
            ops.append(lambda: sc.op("act", lambda e: e.activation(out=ST(24, 32), in_=ST(48, 56), func=AF.Sqrt, scale=1.0 / DK, bias=eps_t),
                                     reads=[("st", 48), "eps_t"], writes=[("st", 24)]))
            ops.append(lambda: sc.op("dve", lambda e: e.reciprocal(out=ST(56, 64), in_=ST(24, 32)), reads=[("st", 24)], writes=[("st", 56)]))
            ops.append(lambda: sc.op("dve", lambda e: e.tensor_tensor(out=knn3, in0=kv3[:, :, 0:64], in1=rk.unsqueeze(2).to_broadcast([128, H, 64]), op=ALU.mult),
                                     reads=[bk(2), bk(3), ("st", 56)], writes=k_knn))
            vdst = vloc[l][t // 4].ap()[(t % 4) * 128:(t % 4 + 1) * 128, :]
            ops.append(lambda: sc.dma("sp", lambda e: e.dma_start(out=vdst, in_=vb[:, vs, :]),
                                      reads=[kvs] + list(k_vb), writes=[("vloc", l, t)]))
            ops.append(lambda: sc.op("pool", lambda e: e.tensor_tensor(out=kb3[:, :, 0:64], in0=knn3, in1=gk[:, 0:64].unsqueeze(1).to_broadcast([128, H, 64]), op=ALU.mult),
                                     reads=list(k_knn) + ["gk"], writes=k_kb))
            ops.append(lambda: sc.op("dve", lambda e: e.tensor_tensor(out=kpeh3, in0=kpeg.unsqueeze(1).to_broadcast([128, H, 32]), in1=rk.unsqueeze(2).to_broadcast([128, H, 32]), op=ALU.mult),
                                     reads=list(k_kpeg) + [("st", 56)], writes=k_kpeh))
            rope_chain(ops, "dve", kpeh3[:, :, 0:16], kpeh3[:, :, 16:32], kb3[:, :, 64:80], kb3[:, :, 80:96], rtk, k_kpeh, k_rtk, k_kb, cos_b, sin_b)
            pk = bank(7).bitcast(BF16).rearrange("p (h j) -> p h j", j=128)
            ops.append(lambda: sc.op("pe", lambda e: tr_heads(e, pk, kb3), reads=list(k_kb) + ["identb"], writes=[bk(7)]))
            kt_ap = ktile[:, vs, :].rearrange("p (h j) -> p h j", j=128)
            kks = ("ktile", vs)
            ops.append(lambda: sc.op("act", lambda e: e.activation(out=kt_ap[0:DK], in_=pk[0:DK], func=AF.Copy),
                                     reads=[bk(7)] + list(k_ktile), writes=[kks]))
            kdst = kloc[l][t // 4].ap().rearrange("(h d) n -> d h n", d=DK)[:, :, (t % 4) * 128:(t % 4 + 1) * 128]
            ops.append(lambda: sc.dma("sp", lambda e: e.dma_start(out=kdst, in_=kt_ap[0:DK]),
                                      reads=[kks] + list(k_ktile), writes=[("kloc", l, t)]))
            return ops

        def interleave(chains):
            n = max(len(c) for c in chains)
            for r in range(n):
                for c in chains:
                    if r < len(c):
                        c[r]()

        for k in range(8):
            load_w(wst[:, 0, k, :], wv[:, k, C_UF:C_UF + 512], 512, [("wst", 0, k)])
        wst0_keys = [("wst", 0, k) for k in range(8)]

        def a3_u_chain(tt):
            ops = []
            for g in range(4):
                def f_u(e, g=g):
                    ins = None
                    for k in range(8):
                        ins = e.matmul(bank(7), lhsT=wst[:, 0, k, g * 128:(g + 1) * 128], rhs=hT[:, k, tt * 512:(tt + 1) * 512], start=(k == 0), stop=(k == 7))
                    return ins
                ops.append(lambda f_u=f_u: sc.op("pe", f_u, reads=[("hT", tt * 4 + j) for j in range(4)] + wst0_keys, writes=[bk(7)]))
                ops.append(lambda g=g: sc.op("act", lambda e: e.activation(out=uT[:, g, :], in_=bank(7), func=AF.Copy),
                                             reads=[bk(7)], writes=[("uT", g)] + list(k_uT)))
            noop_ = lambda: None

            def fa_pair(j):
                t = tt * 4 + j
                ps2 = psS[1]
                pkeys = [bk(6), bk(7)]
                a_s = t % 2
                adst = aloc[l][t // 4].ap()[(t % 4) * 128:(t % 4 + 1) * 128, :]

                def f_a(e):
                    ins = None
                    for g in range(4):
                        ins = e.matmul(ps2[:, g * 256:(g + 1) * 256], lhsT=uT[:, g, j * 128:(j + 1) * 128], rhs=ccsc, start=True, stop=True)
                    return ins

                def mm():
                    sc.op("pe", f_a, reads=[("uT", g) for g in range(4)] + list(k_uT) + ["ccsc"], writes=pkeys)

                def ev():
                    sc.op("act", lambda e: e.activation(out=atile[:, a_s, :], in_=ps2, func=AF.Copy),
                          reads=pkeys + list(k_atile), writes=[("atile", a_s)])
                    sc.dma("sp", lambda e: e.dma_start(out=adst, in_=atile[:, a_s, :]),
                           reads=[("atile", a_s)] + list(k_atile), writes=[("aloc", l, t)])
                return [mm, ev]

            ops += fa_pair(0) + fa_pair(1) + [noop_, noop_, noop_] + fa_pair(2) + [noop_, noop_, noop_] + fa_pair(3)
            return ops

        def a3_group(tt):
            for j in range(4):
                t = tt * 4 + j
                ps2 = psS[1]
                pkeys = [bk(6), bk(7)]

                def f_a(e, j=j, ps2=ps2):
                    ins = None
                    for g in range(4):
                        ins = e.matmul(ps2[:, g * 256:(g + 1) * 256], lhsT=uT[:, g, j * 128:(j + 1) * 128], rhs=ccsc, start=True, stop=True)
                    return ins
                sc.op("pe", f_a, reads=[("uT", g) for g in range(4)] + list(k_uT) + ["ccsc"], writes=pkeys)
                a_s = t % 2
                sc.op("act", lambda e, o=atile[:, a_s, :], i=ps2: e.activation(out=o, in_=i, func=AF.Copy),
                      reads=pkeys + list(k_atile), writes=[("atile", a_s)])
                adst = aloc[l][t // 4].ap()[(t % 4) * 128:(t % 4 + 1) * 128, :]
                sc.dma("sp", lambda e, o=adst, i=atile[:, a_s, :]: e.dma_start(out=o, in_=i),
                       reads=[("atile", a_s)] + list(k_atile), writes=[("aloc", l, t)])

        load_x(0)
        load_x(1)
        ops1, last1 = chain_S1(0)
        interleave([ops1])
        last1()
        for t in range(NT):
            chains = [chain_Sq(t), chain_Sk(t)]
            nxt = None
            if t + 1 < NT:
                if t + 2 < NT:
                    load_x(t + 2)
                ops1, nxt = chain_S1(t + 1)
                chains = [ops1] + chains
            if t % 4 == 3:
                chains.append(a3_u_chain(t // 4))
            interleave(chains)
            if nxt is not None:
                nxt()
            if (t >= 5 and t % 4 == 1) or t == NT - 1:
                for gidx in ([t // 4 - 1] if t < NT - 1 else [t // 4 - 1, t // 4] if t % 4 == 1 else [t // 4]):
                    exchange1("k", kloc[l], kful[l], 0, gidx)
                    exchange1("v", vloc[l], vful[l], 4, gidx)
                    exchange1("a", aloc[l], aful[l], 8, gidx)

        chk("A")

        chk("X")
        for k in range(8):
            load_w(wst[:, 1, k, :], wv[:, k, C_ZA:C_ZA + 512], 512, [("wst", 1, k)])
        wst1_keys = [("wst", 1, k) for k in range(8)]

        T.reset()
        AR.reset()
        NPT = 4
        pT_slots = [T.get([128, 1024], BF16) for _ in range(NPT)]
        oacc2 = [T.get([128, 512], F32) for _ in range(2)]
        za2, og2 = [], []
        for s_ in range(2):
            za2.append(T.get([128, NT, 128], BF16))
            og2.append(T.get([128, NT, 128], BF16))
        rinv, k_rinv = T.get([128, 8], F32)
        kv_bufs = []
        for s_ in range(2):
            kT_b, k_kT = AR.get([128, S], BF16)
            v_b, k_v = AR.get([128, 32, 65], BF16)
            kv_bufs.append((kT_b, k_kT, v_b, k_v))
        scale = float(DK) ** -0.5
        kfv = [kful[l][j].ap().rearrange("(r h d) n -> r h d n", r=2, h=H) for j in range(4)]
        vfv = [vful[l][j].ap().rearrange("(r i p) (h c) -> p r i h c", p=128, i=4, c=65) for j in range(4)]
        its = [(h, qt, kp) for h in range(H) for qt in range(4) for kp in range(16)]
        LOOK = 2
        SD = [(psK, 2), (psS[0], 4), (psS[1], 6)]

        def kvk(h):
            kT_b, k_kT, v_b, k_v = kv_bufs[h % 2]
            return [("kT", h % 2, i) for i in range(8)] + [("v", h % 2, j) for j in range(8)] + list(k_kT) + list(k_v)

        def kv_load(h):
            kT_b, k_kT, v_b, k_v = kv_bufs[h % 2]
            for r in range(2):
                for j in range(4):
                    c0 = r * NTOK + j * 512
                    first = (r == 0 and j == 0)
                    sc.dma("sp", lambda e, o=kT_b[0:DK, c0:c0 + 512], i=kfv[j][r, h]: e.dma_start(out=o, in_=i),
                           reads=[("kful", l, j)], writes=[("kT", h % 2, r * 4 + j)] + (list(k_kT) + list(k_v) if first else []))
            v5 = v_b.rearrange("p (r j i) c -> p r j i c", r=2, j=4)
            for j in range(4):
                for r in range(2):
                    sc.dma("sp", lambda e, o=v5[:, r, j], i=vfv[j][:, r, :, h, :]: e.dma_start(out=o, in_=i),
                           reads=[("vful", l, j)], writes=[("v", h % 2, j * 2 + r)])

        zt, k_zt = T.get([128, 512], F32)

        def za_part(hp, tq):
            za_sb, k_za = za2[hp % 2]

            def f_za(e):
                ins = None
                for j in range(4):
                    t = tq * 4 + j
                    for k in range(8):
                        ins = e.matmul(bank(1)[:, j * 128:(j + 1) * 128], lhsT=hT[:, k, t * 128:(t + 1) * 128], rhs=wst[:, 1, k, hp * 128:(hp + 1) * 128], start=(k == 0), stop=(k == 7))
                return ins
            sc.op("pe", f_za, reads=[("hT", tq * 4 + j) for j in range(4)] + wst1_keys, writes=[bk(1)])
            sc.op("act", lambda e: e.activation(out=zt, in_=bank(1), func=AF.Exp, scale=-1.0), reads=[bk(1)], writes=k_zt)
            sc.op("dve", lambda e: e.tensor_scalar(out=zt, in0=zt, scalar1=1.0, scalar2=None, op0=ALU.add), reads=k_zt, writes=k_zt)
            sc.op("dve", lambda e: e.reciprocal(out=zt, in_=zt), reads=k_zt, writes=k_zt)
            sc.op("dve", lambda e: e.tensor_tensor(out=za_sb[:, tq * 4:(tq + 1) * 4, :], in0=bank(1).rearrange("p (j c) -> p j c", c=128),
                                                   in1=zt.rearrange("p (j c) -> p j c", c=128), op=ALU.mult),
                  reads=[bk(1)] + list(k_zt), writes=[("za", hp % 2, tq)] + list(k_za))

        def trg_part(hp, tq):
            og, k_og = og2[hp % 2]
            pg = bank(1).bitcast(BF16)[:, 0:512].rearrange("p (j c) -> p j c", c=128)

            def f_trg(e):
                ins = None
                for j in range(4):
                    ins = e.transpose(out=pg[:, j, :], in_=og[:, tq * 4 + j, :], identity=identb)
                return ins
            sc.op("pe", f_trg, reads=[("og", hp % 2, tq * 4 + j) for j in range(4)] + list(k_og) + ["identb"], writes=[bk(1)])
            sc.op("dve", lambda e: e.tensor_copy(out=ogT[:, hp, tq * 512:(tq + 1) * 512], in_=bank(1).bitcast(BF16)[:, 0:512]),
                  reads=[bk(1)], writes=[("ogT", hp, tq)])

        def emit_qk(i):
            h, qt, kp = its[i]
            if h == 0 and qt == 0 and kp in (1, 3, 5, 7):
                za_part(0, (kp - 1) // 2)
            if h % 2 == 1 and kp == 4 and h + 1 < H:
                za_part((h + 1) // 2, qt)
            if h % 2 == 0 and h >= 2 and kp == 10:
                trg_part(h // 2 - 1, qt)
            kT_b = kv_bufs[h % 2][0]
            sd, b0 = SD[i % 3]
            pslot = i % NPT

            def f_qk(e, sd=sd, kT_b=kT_b, h=h, qt=qt, kp=kp):
                ins = None
                for u in range(2):
                    kt = 2 * kp + u
                    ins = e.matmul(sd[:, u * 512:(u + 1) * 512], lhsT=kT_b[0:DK, kt * 128:(kt + 1) * 128], rhs=qT[0:DK, h, qt * 512:(qt + 1) * 512], start=True, stop=True)
                return ins
            sc.op("pe", f_qk, reads=kvk(h) + [("qT", qt * 4 + j) for j in range(4)], writes=[bk(b0), bk(b0 + 1)])
            pT_s, k_pT_s = pT_slots[pslot]
            sc.op("act", lambda e, o=pT_s, i=sd: e.activation(out=o, in_=i, func=AF.Exp, scale=scale),
                  reads=[bk(b0), bk(b0 + 1)], writes=[("pT", pslot)] + list(k_pT_s))

        def emit_pv(i):
            h, qt, kp = its[i]
            hp, hh = h // 2, h % 2
            v_b = kv_bufs[h % 2][2]
            pslot = i % NPT
            ob = 0
            osl = qt % 2

            pT_s, k_pT_s = pT_slots[pslot]

            def f_pv(e, ob=ob, v_b=v_b, pT_s=pT_s, kp=kp):
                ins = None
                for u in range(2):
                    kt = 2 * kp + u
                    ins = e.matmul(bank(ob)[0:65, :], lhsT=v_b[:, kt, :], rhs=pT_s[:, u * 512:(u + 1) * 512], start=(kt == 0), stop=(kt == 31))
                return ins
            sc.op("pe", f_pv, reads=kvk(h) + [("pT", pslot)] + list(k_pT_s), writes=[bk(ob)])
            if kp != 15:
                return
            if qt == 3 and h + 2 < H:
                kv_load(h + 2)
            za_sb, k_za = za2[hp % 2]
            og, k_og = og2[hp % 2]
            oacc, k_oacc = oacc2[osl]
            sc.op("dve", lambda e, o=oacc[0:65, :], i=bank(ob)[0:65, :]: e.tensor_copy(out=o, in_=i),
                  reads=[bk(ob)], writes=[("oacc", osl)] + list(k_oacc))
            po = bank(1)[:, 0:260].rearrange("p (j c) -> p j c", c=65)

            def f_tro(e, oacc=oacc, po=po):
                ins = None
                for j in range(4):
                    ins = e.transpose(out=po[:, j, :], in_=oacc[0:65, j * 128:(j + 1) * 128], identity=identf[0:65, 0:65])
                return ins
            sc.op("pe", f_tro, reads=[("oacc", osl), "identf"] + list(k_oacc), writes=[bk(1)])
            sc.op("dve", lambda e, o=rinv[:, 0:4], i=po[:, :, 64]: e.reciprocal(out=o, in_=i),
                  reads=[bk(1)], writes=k_rinv)
            for j in range(4):
                t = qt * 4 + j
                sc.op("dve", lambda e, o=og[:, t, hh * 64:(hh + 1) * 64], i=po[:, j, 0:64], s_=rinv[:, j:j + 1], z=za_sb[:, t, hh * 64:(hh + 1) * 64]:
                      e.scalar_tensor_tensor(out=o, in0=i, scalar=s_, in1=z, op0=ALU.mult, op1=ALU.mult),
                      reads=[bk(1), ("za", hp % 2, qt)] + list(k_rinv) + list(k_za), writes=[("og", hp % 2, t)] + list(k_og))

        kv_load(0)
        kv_load(1)
        for i in range(LOOK):
            emit_qk(i)
        for i in range(len(its)):
            if i + LOOK < len(its):
                emit_qk(i + LOOK)
            emit_pv(i)
        for tq in range(4):
            trg_part(H // 2 - 1, tq)

        chk("B")
        for k in range(8):
            load_w(wst[:, 0, k, :], wv[:, k, C_ZF:C_ZF + 512], 512, [("wst", 0, k)])
        T.reset()
        AR.reset()
        szf, k_szf = T.get([128, 2, 512], BF16)
        NBUF = 3
        dbufs = [AR.get([128, 4096], BF16) for _ in range(NBUF)]
        abufs = [T.get([128, 4, 1024], BF16) for _ in range(NBUF)]
        fscale = float(S * 128) ** -0.5
        di = 0
        for kt in range(4):
            for scg in range(8):
                b = di % NBUF
                di += 1
                dbuf, k_dbuf = dbufs[b]
                abuf, k_abuf = abufs[b]
                sc.dma("sp", lambda e, o=dbuf, i=dft_d[kt, scg]: e.dma_start(out=o, in_=i),
                       writes=[("dbuf", b)] + list(k_dbuf))
                a_src = aful[l][scg % 4].ap()[(scg // 4) * 512:(scg // 4 + 1) * 512, :].rearrange("(sci p) n -> p sci n", p=128)
                sc.dma("sp", lambda e, o=abuf, i=a_src: e.dma_start(out=o, in_=i),
                       reads=[("aful", l, scg % 4)], writes=[("abuf", b)] + list(k_abuf))
                d4 = dbuf.rearrange("p (sci cs k) -> p sci cs k", cs=2, k=512)
                a5 = abuf.rearrange("p sci (g cs m) -> p sci g cs m", cs=2, m=128)

                def f_f(e, scg=scg, d4=d4, a5=a5):
                    ins = None
                    for sci in range(4):
                        for g in range(4):
                            for cs in range(2):
                                first = (scg == 0 and sci == 0 and cs == 0)
                                last = (scg == 7 and sci == 3 and cs == 1)
                                ins = e.matmul(bank(g), lhsT=a5[:, sci, g, cs, :], rhs=d4[:, sci, cs, :], start=first, stop=last)
                    return ins
                sc.op("pe", f_f, reads=[("dbuf", b), ("abuf", b)] + list(k_dbuf) + list(k_abuf), writes=[bk(0), bk(1), bk(2), bk(3)])
            for g in range(4):
                pb = 4 + (g % 2)
                zs = g % 2

                def f_zf(e, g=g, kt=kt, pb=pb):
                    ins = None
                    for k in range(8):
                        ins = e.matmul(bank(pb), lhsT=wst[:, 0, k, g * 128:(g + 1) * 128], rhs=hT[:, k, kt * 512:(kt + 1) * 512], start=(k == 0), stop=(k == 7))
                    return ins
                sc.op("pe", f_zf, reads=[("hT", kt * 4 + j) for j in range(4)] + wst0_keys, writes=[bk(pb)])
                sc.op("act", lambda e, o=szf[:, zs, :], i=bank(pb): e.activation(out=o, in_=i, func=AF.Silu),
                      reads=[bk(pb)], writes=[("szf", zs)] + list(k_szf))
                sc.op("dve", lambda e, o=fgT[:, g, kt * 512:(kt + 1) * 512], i=bank(g), z=szf[:, zs, :]:
                      e.scalar_tensor_tensor(out=o, in0=i, scalar=fscale, in1=z, op0=ALU.mult, op1=ALU.mult),
                      reads=[bk(g), ("szf", zs)] + list(k_szf), writes=[("fgT", g, kt)])

        chk("C")
        T.reset()
        AR.reset()
        wat_sb, k_wat = AR.get([128, 4, D], BF16)
        wfo_sb, k_wfo = AR.get([128, 4, D], BF16)
        wo_sb, k_wo = AR.get([128, 8, D], BF16)
        wat_v = w_attn[l].rearrange("(k p) n -> p k n", p=128)
        wfo_v = w_four[l].rearrange("(k p) n -> p k n", p=128)
        wo_v = w_out[l].rearrange("(k p) n -> p k n", p=128)
        for k in range(4):
            load_w(wat_sb[:, k, :], wat_v[:, k, :], D, [("wat", k)] + list(k_wat), eng="dve")
            load_w(wfo_sb[:, k, :], wfo_v[:, k, :], D, [("wfo", k)] + list(k_wfo), eng="dve")
        sg_slots = [T.get([128, 512], F32) for _ in range(4)]
        t1_slots = [T.get([128, 512], F32) for _ in range(2)]
        mT = qT
        for j in range(8):
            ws = j % 2
            wk_all = [("wst", ws, k) for k in range(8)]
            load_w(wst[:, ws, :, 0:128], wv[:, :, C_GA + j * 128:C_GA + (j + 1) * 128], 1024, wk_all, inner=128)
            load_w(wst[:, ws, :, 128:256], wv[:, :, C_GF + j * 128:C_GF + (j + 1) * 128], 1024, wk_all, inner=128)
            wkeys = [("wst", ws, k) for k in range(8)]
            load_w(wo_sb[:, j, :], wo_v[:, j, :], D, [("wo", j)] + list(k_wo), eng="dve")
            for tt in range(4):
                st_ = (j * 4 + tt) % 2
                b0 = 4 * st_

                def f_g(e, ws=ws, tt=tt, b0=b0, j=j):
                    ins = None
                    for a in range(2):
                        for k in range(8):
                            ins = e.matmul(bank(b0 + a), lhsT=wst[:, ws, k, a * 128:(a + 1) * 128], rhs=hT[:, k, tt * 512:(tt + 1) * 512], start=(k == 0), stop=(k == 7))
                    for k in range(4):
                        ins = e.matmul(bank(b0 + 2), lhsT=wat_sb[:, k, j * 128:(j + 1) * 128], rhs=ogT[:, k, tt * 512:(tt + 1) * 512], start=(k == 0), stop=(k == 3))
                    for k in range(4):
                        ins = e.matmul(bank(b0 + 3), lhsT=wfo_sb[:, k, j * 128:(j + 1) * 128], rhs=fgT[:, k, tt * 512:(tt + 1) * 512], start=(k == 0), stop=(k == 3))
                    return ins
                sc.op("pe", f_g,
                      reads=wkeys + [("hT", tt * 4 + i) for i in range(4)] + [("wat", k) for k in range(4)] + [("wfo", k) for k in range(4)]
                      + list(k_wat) + list(k_wfo) + [("ogT", k, tt) for k in range(4)] + [("fgT", k, tt) for k in range(4)],
                      writes=[bk(b0 + i) for i in range(4)] + [("qT", tt * 4 + i) for i in range(0)])
                sga, k_sga = sg_slots[st_ * 2]
                sgf, k_sgf = sg_slots[st_ * 2 + 1]
                t1s, k_t1s = t1_slots[st_]
                for a, (sgx, k_sgx) in enumerate(((sga, k_sga), (sgf, k_sgf))):
                    sc.op("act", lambda e, o=sgx, i=bank(b0 + a), b_=bm[:, a * 8 + j:a * 8 + j + 1]: e.activation(out=o, in_=i, func=AF.Sigmoid, bias=b_),
                          reads=[bk(b0 + a), "bm"], writes=list(k_sgx))
                sc.op("dve", lambda e, o=t1s, i=bank(b0 + 2), z=sga: e.tensor_tensor(out=o, in0=i, in1=z, op=ALU.mult),
                      reads=[bk(b0 + 2)] + list(k_sga), writes=list(k_t1s))
                sc.op("dve", lambda e, o=sgf, i=bank(b0 + 3), z=sgf: e.tensor_tensor(out=o, in0=i, in1=z, op=ALU.mult),
                      reads=[bk(b0 + 3)] + list(k_sgf), writes=list(k_sgf))
                sc.op("dve", lambda e, o=mT[:, j, tt * 512:(tt + 1) * 512], i=t1s, z=sgf: e.tensor_tensor(out=o, in0=i, in1=z, op=ALU.add),
                      reads=list(k_sgf) + list(k_t1s) + [("qT", tt * 4 + i) for i in range(4)],
                      writes=[("mT", j, tt)] + [("qT", tt * 4 + i) for i in range(4)])
        xo, k_xo = T.get([128, 2, D], F32)

        def load_x_q(t):
            rd = [("xdst", l - 1, t)] if l > 0 else []
            sc.dma("pool", lambda e, o=xt[:, t % 2, :], i=xs_v[:, t, :]: e.dma_start(out=o, in_=i), reads=rd, writes=[("xt", t % 2)])

        load_x_q(0)
        for t in range(NT):
            xs_ = xt[:, t % 2, :]
            kx = ("xt", t % 2)
            if t + 1 < NT:
                load_x_q(t + 1)
            ps2 = psQ if t % 2 == 0 else psK
            pkeys = [bk(0), bk(1)] if t % 2 == 0 else [bk(2), bk(3)]

            def f_o(e, t=t, ps2=ps2):
                ins = None
                for hf in range(2):
                    for k in range(8):
                        ins = e.matmul(ps2[:, hf * 512:(hf + 1) * 512], lhsT=mT[:, k, t * 128:(t + 1) * 128], rhs=wo_sb[:, k, hf * 512:(hf + 1) * 512], start=(k == 0), stop=(k == 7))
                return ins
            sc.op("pe", f_o, reads=[("mT", k, t // 4) for k in range(8)] + [("qT", t)] + [("wo", k) for k in range(8)] + list(k_wo), writes=pkeys)
            xs2 = t % 2
            sc.op("dve", lambda e, o=xo[:, xs2, :], i=ps2, z=xs_: e.tensor_tensor(out=o, in0=i, in1=z, op=ALU.add),
                  reads=pkeys + [kx] + list(k_xo), writes=[("xo", xs2)])
            sc.dma("sp", lambda e, o=xd_v[:, t, :], i=xo[:, xs2, :]: e.dma_start(out=o, in_=i),
                   reads=[("xo", xs2)] + list(k_xo), writes=[("xdst", l, t)])
        if DEBUG and l == 0:
            sc.dma("sp", lambda e: e.dma_start(out=dbg_og, in_=ogT), reads=[("ogT", a, b) for a in range(4) for b in range(4)], writes=["dbg_og"])
            sc.dma("sp", lambda e: e.dma_start(out=dbg_fg, in_=fgT), reads=[("fgT", a, b) for a in range(4) for b in range(4)], writes=["dbg_fg"])
            sc.dma("sp", lambda e: e.dma_start(out=dbg_m, in_=qT), reads=[("mT", a, b) for a in range(8) for b in range(4)] + [("qT", t) for t in range(NT)], writes=["dbg_m"])
    except _Stop:
        pass
    if DEBUG:
        sc.final_wait("sp", ["dbg_og", "dbg_fg", "dbg_m"])
    sc.final_wait("sp", [("xdst", L - 1, t) for t in range(NT)])

    with nc.Block() as block:
        @block.tensor
        def _(e):
            sc.replay("pe", e)

        @block.scalar
        def _(e):
            sc.replay("act", e)

        @block.vector
        def _(e):
            sc.replay("dve", e)

        @block.gpsimd
        def _(e):
            sc.replay("pool", e)

        @block.sync
        def _(e):
            sc.replay("sp", e)
    return nc


_CACHE = {}


def _tables(half):
    key = ("tab", half)
    if key in _CACHE:
        return _CACHE[key]
    hd = 16
    inv_freq = (10000.0 ** (-np.arange(hd, dtype=np.float32) / hd)).astype(np.float32)
    pos = (half * NTOK + np.arange(NTOK, dtype=np.float32)).astype(np.float32)
    ang = (pos[:, None] * inv_freq[None, :]).astype(np.float32)
    cs = np.concatenate([np.cos(ang), np.sin(ang)], axis=1).astype(np.float32)
    rope = np.ascontiguousarray(cs.reshape(NT, 128, 32).transpose(1, 0, 2))
    s_idx = np.arange(S, dtype=np.int64).reshape(8, 4, 128)
    k_idx = (half * NTOK + np.arange(NTOK, dtype=np.int64)).reshape(4, 512)
    prod = (s_idx[None, :, :, :, None] * k_idx[:, None, None, None, :]) % S
    th = (2.0 * np.pi / S) * prod.astype(np.float64)
    c = np.cos(th)
    sn = -np.sin(th)
    tab = np.stack([c, sn], axis=4)
    tab = tab.transpose(0, 1, 3, 2, 4, 5)
    dft = np.ascontiguousarray(tab.reshape(4, 8, 128, 4096)).astype(ml_dtypes.bfloat16)
    _CACHE[key] = (rope, dft)
    return rope, dft


def _consts():
    if "c" in _CACHE:
        return _CACHE["c"]
    c_i = np.arange(128, dtype=np.int64)
    th = (2.0 * np.pi / 128) * ((c_i[:, None] * c_i[None, :]) % 128).astype(np.float64)
    ccsc = np.concatenate([np.cos(th), np.sin(th)], axis=1).astype(ml_dtypes.bfloat16)
    ib = np.eye(128, dtype=np.float32).astype(ml_dtypes.bfloat16)
    i_f = np.eye(128, dtype=np.float32)
    _CACHE["c"] = (ccsc, ib, i_f)
    return _CACHE["c"]


STOP = None
DEBUG = False


def _get_nc(L):
    key = ("nc", L, STOP)
    if key not in _CACHE:
        _CACHE[key] = build_program(L, STOP)
    return _CACHE[key]


def _layer_params(norm_g, q_latent_g, kv_latent_g, q_head_g, k_head_g, b_merge, ls):
    L = len(ls)
    ng = np.stack([np.ascontiguousarray(norm_g[l].reshape(8, 128).T) for l in ls])
    gl = np.stack([np.ascontiguousarray(np.concatenate([q_latent_g[l].reshape(2, 128), kv_latent_g[l].reshape(2, 128)], 0).T) for l in ls])
    gqr = np.stack([np.ascontiguousarray(np.broadcast_to(q_head_g[l][None, :], (128, DK))) for l in ls])
    gkr = np.stack([np.ascontiguousarray(np.broadcast_to(k_head_g[l][None, :], (128, DK))) for l in ls])
    bmt = np.stack([np.ascontiguousarray(b_merge[l].reshape(2, 8, 128).transpose(2, 0, 1).reshape(128, 16)) for l in ls])
    return ng.astype(np.float32), gl.astype(np.float32), gqr.astype(np.float32), gkr.astype(np.float32), bmt.astype(np.float32)


FUSED_LAYERS = 4


def kernel(x, norm_g, w_in, q_latent_g, kv_latent_g, w_uq, w_ukv, q_head_g, k_head_g,
           w_attn_proj, w_fourier_proj, b_merge, w_out):
    f = lambda a: np.ascontiguousarray(np.asarray(a, dtype=np.float32))
    x, norm_g, w_in, q_latent_g, kv_latent_g = f(x), f(norm_g), f(w_in), f(q_latent_g), f(kv_latent_g)
    w_uq, w_ukv, q_head_g, k_head_g = f(w_uq), f(w_ukv), f(q_head_g), f(k_head_g)
    w_attn_proj, w_fourier_proj, b_merge, w_out = f(w_attn_proj), f(w_fourier_proj), f(b_merge), f(w_out)
    depth = w_in.shape[0]
    ccsc, ib, i_f = _consts()
    cur = [np.ascontiguousarray(x[c // 2, (c % 2) * NTOK:(c % 2 + 1) * NTOK, :]) for c in range(8)]
    step = FUSED_LAYERS
    for l0 in range(0, depth, step):
        ls = list(range(l0, min(l0 + step, depth)))
        nc = _get_nc(len(ls))
        ng, gl, gqr, gkr, bmt = _layer_params(norm_g, q_latent_g, kv_latent_g, q_head_g, k_head_g, b_merge, ls)
        sl = slice(ls[0], ls[-1] + 1)
        in_maps = []
        for c in range(8):
            rope, dft = _tables(c % 2)
            in_maps.append({
                "x": cur[c], "w_in": w_in[sl], "w_uq": w_uq[sl], "w_ukv": w_ukv[sl],
                "w_attn": w_attn_proj[sl], "w_four": w_fourier_proj[sl], "w_out": w_out[sl],
                "norm_gT": ng, "glatT": gl, "gq_rep": gqr, "gk_rep": gkr, "bmT": bmt,
                "rope_cs": rope, "dft": dft, "ccsc": ccsc, "ident_bf": ib, "ident_f": i_f,
            })
        res = run_bass_kernel_spmd(nc, in_maps, core_ids=list(range(8)))
        cur = [np.asarray(res.results[c]["y"], dtype=np.float32) for c in range(8)]
        if DEBUG:
            _CACHE["dbg"] = [{k: np.asarray(res.results[c][k]) for k in ("dbg_og", "dbg_fg", "dbg_m")} for c in range(8)]
    out = np.empty((4, S, D), dtype=np.float32)
    for c in range(8):
        out[c // 2, (c % 2) * NTOK:(c % 2 + 1) * NTOK, :] = cur[c]
    return out
```
